# Optimizing a Trainium2 kernel written in Bass

```python
import math
import jax, jax.numpy as jnp
from jax import lax
import numpy as np

D_MODEL = 2048
BATCH = 16
SEQ = 256
DEPTH = 4
DEC_BATCH = 2
DEC_SEQ = 2048
PAST_LEN = 512

GRID_W = 64
N_MIXERS = 3
N_CONV_LAYERS = (DEPTH + 2) // 3
N_SSD_LAYERS = (DEPTH + 1) // 3
N_ATTN_LAYERS = DEPTH // 3
EPS = 1e-6

CONV_W = 31

SSM_EXPAND = 2
D_INNER = SSM_EXPAND * D_MODEL
SSM_HEAD_DIM = 64
SSM_HEADS = D_INNER // SSM_HEAD_DIM
SSM_GROUPS = 8
SSM_STATE = 128
SSM_CONV_W = 5
SSM_CHUNK = 128
SSM_CONV_DIM = D_INNER + 2 * SSM_GROUPS * SSM_STATE
SSM_IN_W = D_INNER + SSM_CONV_DIM + 2 * SSM_HEADS

HEAD_DIM = 128
N_HEADS = D_MODEL // HEAD_DIM
N_KV_HEADS = 4
KV_GROUP = N_HEADS // N_KV_HEADS
ROPE_PAIRS_PER_AXIS = HEAD_DIM // 4
ROPE_THETA = 10000.0
Q_BLOCK = 128
QKV_W = (N_HEADS + 2 * N_KV_HEADS) * HEAD_DIM

D_FF = 5632
FFN_CONV_W = 3

kernel_name = 'hybrid_diffusion_prefix_trunk_step'


def rms_norm(x, g):
    xf = x.astype(jnp.float32)
    y = xf * lax.rsqrt(jnp.mean(xf * xf, axis=-1, keepdims=True) + EPS)
    return (y * g.astype(jnp.float32)).astype(x.dtype)


def layer_norm(x, g, b):
    xf = x.astype(jnp.float32)
    xc = xf - jnp.mean(xf, axis=-1, keepdims=True)
    y = xc * lax.rsqrt(jnp.mean(xc * xc, axis=-1, keepdims=True) + EPS)
    return (y * g.astype(jnp.float32) + b.astype(jnp.float32)).astype(x.dtype)


def dwconv(x, w, b):
    width = w.shape[0]
    left = (width - 1) // 2
    y = lax.conv_general_dilated(
        x, w[:, None, :].astype(x.dtype), window_strides=(1,),
        padding=[(left, width - 1 - left)],
        dimension_numbers=('NWC', 'WIO', 'NWC'),
        feature_group_count=x.shape[-1])
    return y + b.astype(x.dtype)


def modulation(cond, w_mod, b_mod):
    m = jax.nn.silu(cond) @ w_mod + b_mod
    return jnp.split(m[..., None, :], 6, axis=-1)


def conformer_conv(h, w_pw1, b_pw1, w_dw, b_dw, ln_g, ln_b, w_pw2, b_pw2):
    u = h @ w_pw1 + b_pw1
    a, g = jnp.split(u, 2, axis=-1)
    u = a * jax.nn.sigmoid(g)
    u = dwconv(u, w_dw, b_dw)
    u = jax.nn.silu(layer_norm(u, ln_g, ln_b))
    return u @ w_pw2 + b_pw2


def ssd_scan(x, dt, a, bm, cm, h0):
    bsz, length, nh, p = x.shape
    g, n = bm.shape[-2:]
    r = nh // g
    nc = length // SSM_CHUNK
    L = SSM_CHUNK
    xc = (x * dt[..., None]).reshape(bsz, nc, L, g, r, p)
    a_cs = jnp.cumsum((dt * a).reshape(bsz, nc, L, g, r), axis=2)
    bc = bm.reshape(bsz, nc, L, g, n)
    cc = cm.reshape(bsz, nc, L, g, n)
    seg = a_cs[:, :, :, None] - a_cs[:, :, None, :]
    mask = jnp.tril(jnp.ones((L, L), dtype=bool))[None, None, :, :, None, None]
    decay = jnp.exp(jnp.where(mask, seg, -jnp.inf))
    scores = jnp.einsum('bclgn,bcsgn->bclsg', cc, bc)
    y_diag = jnp.einsum('bclsgr,bcsgrp->bclgrp', scores[..., None] * decay, xc)
    decay_to_end = jnp.exp(a_cs[:, :, -1:] - a_cs)
    chunk_states = jnp.einsum('bclgn,bclgr,bclgrp->bcgrpn', bc, decay_to_end, xc)
    chunk_decay = jnp.exp(a_cs[:, :, -1])

    def step(state, inp):
        cs, cd = inp
        return state * cd[..., None, None] + cs, state

    final, prev = lax.scan(step, h0.reshape(bsz, g, r, p, n),
                           (jnp.moveaxis(chunk_states, 1, 0), jnp.moveaxis(chunk_decay, 1, 0)))
    prev = jnp.moveaxis(prev, 0, 1)
    y_off = jnp.einsum('bclgn,bcgrpn,bclgr->bclgrp', cc, prev, jnp.exp(a_cs))
    y = (y_diag + y_off).reshape(bsz, length, nh, p)
    return y, final.reshape(bsz, nh, p, n)


def ssd_mixer(h, h0, w_in, w_conv, b_conv, dt_bias, a_log, d_skip, norm_g, w_out):
    bsz, length, _ = h.shape
    z, xbc, dt = jnp.split(h @ w_in, [D_INNER, D_INNER + SSM_CONV_DIM], axis=-1)
    xbc = jax.nn.silu(dwconv(xbc, w_conv, b_conv)).astype(jnp.float32)
    xs, bm, cm = jnp.split(xbc, [D_INNER, D_INNER + SSM_GROUPS * SSM_STATE], axis=-1)
    xs = xs.reshape(bsz, length, SSM_HEADS, SSM_HEAD_DIM)
    bm = bm.reshape(bsz, length, SSM_GROUPS, SSM_STATE)
    cm = cm.reshape(bsz, length, SSM_GROUPS, SSM_STATE)
    dt = jax.nn.softplus(dt.astype(jnp.float32).reshape(bsz, length, 2, SSM_HEADS)
                         + dt_bias.astype(jnp.float32))
    a = -jnp.exp(a_log.astype(jnp.float32))
    h0 = h0.astype(jnp.float32)
    y_f, s_f = ssd_scan(xs, dt[:, :, 0], a[0], bm, cm, h0[:, 0])
    y_b, s_b = ssd_scan(jnp.flip(xs, 1), jnp.flip(dt[:, :, 1], 1), a[1],
                        jnp.flip(bm, 1), jnp.flip(cm, 1), h0[:, 1])
    y = y_f + jnp.flip(y_b, 1) + d_skip.astype(jnp.float32)[:, None] * xs
    y = y.reshape(bsz, length, D_INNER) * jax.nn.silu(z.astype(jnp.float32))
    y = rms_norm(y, norm_g).astype(h.dtype)
    return y @ w_out, jnp.stack([s_f, s_b], axis=1)


def qkv_heads(h, w_qkv, q_norm, k_norm):
    bsz, length, _ = h.shape
    q, k, v = jnp.split(h @ w_qkv, [N_HEADS * HEAD_DIM, (N_HEADS + N_KV_HEADS) * HEAD_DIM], axis=-1)
    q = rms_norm(q.reshape(bsz, length, N_HEADS, HEAD_DIM), q_norm)
    k = rms_norm(k.reshape(bsz, length, N_KV_HEADS, HEAD_DIM), k_norm)
    v = v.reshape(bsz, length, N_KV_HEADS, HEAD_DIM)
    return q, k, v


def axial_rope_tables(rows):
    pos_row = jnp.repeat(jnp.arange(rows, dtype=jnp.float32), GRID_W)
    pos_col = jnp.tile(jnp.arange(GRID_W, dtype=jnp.float32), rows)
    inv = ROPE_THETA ** (-jnp.arange(ROPE_PAIRS_PER_AXIS, dtype=jnp.float32) / ROPE_PAIRS_PER_AXIS)
    ang = jnp.concatenate([pos_row[:, None] * inv, pos_col[:, None] * inv], axis=-1)
    return jnp.cos(ang)[:, None, :], jnp.sin(ang)[:, None, :]


def apply_rope(x, cos, sin):
    xf = x.astype(jnp.float32).reshape(*x.shape[:-1], HEAD_DIM // 2, 2)
    x1, x2 = xf[..., 0], xf[..., 1]
    out = jnp.stack([x1 * cos - x2 * sin, x1 * sin + x2 * cos], axis=-1)
    return out.reshape(x.shape).astype(x.dtype)


def block_attention(q, k, v):
    bsz, lq = q.shape[:2]
    nblk = lq // Q_BLOCK
    qb = jnp.moveaxis(q.reshape(bsz, nblk, Q_BLOCK, N_KV_HEADS, KV_GROUP, HEAD_DIM), 1, 0)
    scale = HEAD_DIM ** -0.5

    def one_block(qblk):
        s = jnp.einsum('bqkgd,bskd->bkgqs', qblk, k, preferred_element_type=jnp.float32) * scale
        p = jax.nn.softmax(s, axis=-1).astype(v.dtype)
        return jnp.einsum('bkgqs,bskd->bqkgd', p, v)

    o = lax.map(one_block, qb)
    return jnp.moveaxis(o, 0, 1).reshape(bsz, lq, N_HEADS * HEAD_DIM)


def conv_ffn(h, w_up, w_dw, b_dw, w_down):
    u = dwconv(h @ w_up, w_dw, b_dw)
    a, g = jnp.split(u, 2, axis=-1)
    return (jax.nn.silu(g) * a) @ w_down


def setup_inputs(seed: int = 0) -> dict:
    key = jax.random.key(seed)
    keys = iter(jax.random.split(key, 64))

    def nrm(shape, scale=1.0):
        return jax.random.normal(next(keys), shape, jnp.float32) * scale

    def gain(shape):
        return 1.0 + nrm(shape, 0.02)

    D = D_MODEL
    inv = D ** -0.5
    dt0 = jnp.exp(jax.random.uniform(next(keys), (N_SSD_LAYERS, 2, SSM_HEADS), jnp.float32,
                                     math.log(1e-3), math.log(1e-1)))
    dt_bias = dt0 + jnp.log(-jnp.expm1(-dt0))
    a_log = jnp.log(jax.random.uniform(next(keys), (N_SSD_LAYERS, 2, SSM_HEADS), jnp.float32, 1.0, 16.0))
    return {
        'x_prompt': nrm((BATCH, SEQ, D)),
        'x_sample': nrm((DEC_BATCH, DEC_SEQ, D)),
        'c': nrm((DEC_BATCH, D)),
        'state_ssd': nrm((DEC_BATCH, N_SSD_LAYERS, 2, SSM_HEADS, SSM_HEAD_DIM, SSM_STATE), 0.5),
        'cache_k': nrm((DEC_BATCH, N_ATTN_LAYERS, PAST_LEN, N_KV_HEADS, HEAD_DIM)),
        'cache_v': nrm((DEC_BATCH, N_ATTN_LAYERS, PAST_LEN, N_KV_HEADS, HEAD_DIM)),
        'c_ctx': nrm((D,)),
        'w_mod': nrm((DEPTH, D, 6 * D), 0.5 * inv),
        'b_mod': nrm((DEPTH, 6 * D), 0.02),
        'norm_pre': gain((DEPTH, 2, D)),
        'norm_post': gain((DEPTH, 2, D)),
        'cv_w_pw1': nrm((N_CONV_LAYERS, D, 2 * D), inv),
        'cv_b_pw1': nrm((N_CONV_LAYERS, 2 * D), 0.02),
        'cv_w_dw': nrm((N_CONV_LAYERS, CONV_W, D), CONV_W ** -0.5),
        'cv_b_dw': nrm((N_CONV_LAYERS, D), 0.02),
        'cv_ln_g': gain((N_CONV_LAYERS, D)),
        'cv_ln_b': nrm((N_CONV_LAYERS, D), 0.02),
        'cv_w_pw2': nrm((N_CONV_LAYERS, D, D), inv),
        'cv_b_pw2': nrm((N_CONV_LAYERS, D), 0.02),
        'ssd_w_in': nrm((N_SSD_LAYERS, D, SSM_IN_W), inv),
        'ssd_w_conv': nrm((N_SSD_LAYERS, SSM_CONV_W, SSM_CONV_DIM), SSM_CONV_W ** -0.5),
        'ssd_b_conv': nrm((N_SSD_LAYERS, SSM_CONV_DIM), 0.02),
        'ssd_dt_bias': dt_bias,
        'ssd_a_log': a_log,
        'ssd_d': gain((N_SSD_LAYERS, SSM_HEADS)),
        'ssd_norm_g': gain((N_SSD_LAYERS, D_INNER)),
        'ssd_w_out': nrm((N_SSD_LAYERS, D_INNER, D), D_INNER ** -0.5),
        'attn_w_qkv': nrm((N_ATTN_LAYERS, D, QKV_W), inv),
        'attn_q_norm': gain((N_ATTN_LAYERS, HEAD_DIM)),
        'attn_k_norm': gain((N_ATTN_LAYERS, HEAD_DIM)),
        'attn_w_o': nrm((N_ATTN_LAYERS, N_HEADS * HEAD_DIM, D), (N_HEADS * HEAD_DIM) ** -0.5),
        'ffn_w_up': nrm((DEPTH, D, 2 * D_FF), inv),
        'ffn_w_dw': nrm((DEPTH, FFN_CONV_W, 2 * D_FF), FFN_CONV_W ** -0.5),
        'ffn_b_dw': nrm((DEPTH, 2 * D_FF), 0.02),
        'ffn_w_down': nrm((DEPTH, D_FF, D), D_FF ** -0.5),
    }


def reference(x_prompt, x_sample, c, state_ssd, cache_k, cache_v, c_ctx, w_mod, b_mod,
              norm_pre, norm_post, cv_w_pw1, cv_b_pw1, cv_w_dw, cv_b_dw, cv_ln_g, cv_ln_b,
              cv_w_pw2, cv_b_pw2, ssd_w_in, ssd_w_conv, ssd_b_conv, ssd_dt_bias, ssd_a_log,
              ssd_d, ssd_norm_g, ssd_w_out, attn_w_qkv, attn_q_norm, attn_k_norm, attn_w_o,
              ffn_w_up, ffn_w_dw, ffn_b_dw, ffn_w_down):
    rows = x_sample.shape[1] // GRID_W
    cos, sin = axial_rope_tables(rows)
    yp, ys = x_prompt, x_sample
    new_ssd, new_k, new_v = [], [], []
    for i in range(DEPTH):
        kind, j = i % N_MIXERS, i // N_MIXERS
        sh_mp, sc_mp, ga_mp, sh_fp, sc_fp, ga_fp = modulation(c_ctx, w_mod[i], b_mod[i])
        sh_ms, sc_ms, ga_ms, sh_fs, sc_fs, ga_fs = modulation(c, w_mod[i], b_mod[i])
        hp = rms_norm(yp, norm_pre[i, 0]) * (1 + sc_mp) + sh_mp
        hs = rms_norm(ys, norm_pre[i, 0]) * (1 + sc_ms) + sh_ms
        if kind == 0:
            cargs = (cv_w_pw1[j], cv_b_pw1[j], cv_w_dw[j], cv_b_dw[j], cv_ln_g[j], cv_ln_b[j],
                     cv_w_pw2[j], cv_b_pw2[j])
            op = conformer_conv(hp, *cargs)
            os_ = conformer_conv(hs, *cargs)
        elif kind == 1:
            sargs = (ssd_w_in[j], ssd_w_conv[j], ssd_b_conv[j], ssd_dt_bias[j], ssd_a_log[j],
                     ssd_d[j], ssd_norm_g[j], ssd_w_out[j])
            zero_state = jnp.zeros((hp.shape[0], 2, SSM_HEADS, SSM_HEAD_DIM, SSM_STATE), jnp.float32)
            op, st = ssd_mixer(hp, zero_state, *sargs)
            new_ssd.append(st)
            os_, _ = ssd_mixer(hs, state_ssd[:, j], *sargs)
        else:
            qp, kp, vp = qkv_heads(hp, attn_w_qkv[j], attn_q_norm[j], attn_k_norm[j])
            op = block_attention(qp, kp, vp) @ attn_w_o[j]
            new_k.append(kp)
            new_v.append(vp)
            qs, ks, vs = qkv_heads(hs, attn_w_qkv[j], attn_q_norm[j], attn_k_norm[j])
            qs = apply_rope(qs, cos, sin)
            ks = apply_rope(ks, cos, sin)
            k_all = jnp.concatenate([ks, cache_k[:, j].astype(ks.dtype)], axis=1)
            v_all = jnp.concatenate([vs, cache_v[:, j].astype(vs.dtype)], axis=1)
            os_ = block_attention(qs, k_all, v_all) @ attn_w_o[j]
        yp = yp + ga_mp * rms_norm(op, norm_post[i, 0])
        ys = ys + ga_ms * rms_norm(os_, norm_post[i, 0])
        fargs = (ffn_w_up[i], ffn_w_dw[i], ffn_b_dw[i], ffn_w_down[i])
        hp = rms_norm(yp, norm_pre[i, 1]) * (1 + sc_fp) + sh_fp
        hs = rms_norm(ys, norm_pre[i, 1]) * (1 + sc_fs) + sh_fs
        yp = yp + ga_fp * rms_norm(conv_ffn(hp, *fargs), norm_post[i, 1])
        ys = ys + ga_fs * rms_norm(conv_ffn(hs, *fargs), norm_post[i, 1])
    new_state_ssd = jnp.stack(new_ssd, axis=1)
    new_cache_k = jnp.stack(new_k, axis=1)
    new_cache_v = jnp.stack(new_v, axis=1)
    return (yp, ys, new_state_ssd, new_cache_k, new_cache_v)
```

```python
import numpy as np
import ml_dtypes
from contextlib import ExitStack
import concourse.bass as bass
import concourse.mybir as mybir
from concourse.bass_utils import run_bass_kernel_spmd

F32 = mybir.dt.float32
BF16 = mybir.dt.bfloat16
AF = mybir.ActivationFunctionType
ALU = mybir.AluOpType
EPS = 1e-6
NEG = -30000.0


class Cfg:
    def __init__(self, **kw):
        self.D = 2048
        self.T = 2048
        self.SEG = 256
        self.DFF = 5632
        self.CW = 31
        self.DEPTH = 4
        self.DI = 4096
        self.SH = 64
        self.SG = 8
        self.SCW = 5
        self.NH = 16
        self.NKV = 4
        self.PAST = 512
        self.GRID_W = 64
        self.NCORES = 8
        for k, v in kw.items():
            setattr(self, k, v)
        self.KC = self.D // 128
        self.FC = self.DFF // 128
        self.NSEG = self.T // self.SEG
        self.TB = max(1, self.T // 512)
        self.TBW = min(512, self.T)
        self.NCONV = (self.DEPTH + 2) // 3
        self.NSSD = (self.DEPTH + 1) // 3
        self.NATT = self.DEPTH // 3


class Cell:
    __slots__ = ("w", "r", "excl")

    def __init__(self, excl=False):
        self.w = None
        self.r = {}
        self.excl = excl


class Agent:
    __slots__ = ("sem", "step", "count")

    def __init__(self, sem, step):
        self.sem = sem
        self.step = step
        self.count = 0


class Prog:
    def __init__(self, nc, es, n_lanes=32):
        self.nc = nc
        self.es = es
        self.eng = {"pe": nc.tensor, "act": nc.scalar, "dve": nc.vector, "pool": nc.gpsimd, "sp": nc.sync}
        self.agents = {}
        for n in self.eng:
            self.agents[n] = Agent(es.enter_context(nc.semaphore("sem_" + n)), 1)
        self.n_lanes = n_lanes
        for i in range(n_lanes):
            self.agents["L%d" % i] = Agent(es.enter_context(nc.semaphore("sem_L%d" % i)), 16)
        self.waited = {n: {} for n in self.eng}
        self.lane_rr = 0
        self.lane_rr_pool = 0
        self.ninstr = 0

    def cell(self):
        return Cell()

    def _wait(self, e, agent, idx):
        if idx <= 0 or self.waited[e].get(agent, 0) >= idx:
            return
        a = self.agents[agent]
        self.eng[e].wait_ge(a.sem, idx * a.step)
        self.waited[e][agent] = idx

    @staticmethod
    def _flat(cells):
        out = []
        for c in cells:
            if isinstance(c, (list, tuple)):
                out.extend(Prog._flat(c))
            else:
                out.append(c)
        return out

    def handoff(self, from_cells, to_cells):
        r = {}
        for c in self._flat(from_cells):
            if c.w is not None and r.get(c.w[0], 0) < c.w[1]:
                r[c.w[0]] = c.w[1]
            for a, i in c.r.items():
                if r.get(a, 0) < i:
                    r[a] = i
        for t in self._flat(to_cells):
            t.w = None
            t.r = dict(r)

    def _deps(self, e, reads, writes, skip_self):
        need = {}
        for c in reads:
            if c.w is not None:
                if need.get(c.w[0], 0) < c.w[1]:
                    need[c.w[0]] = c.w[1]
            if c.excl:
                for a, i in c.r.items():
                    if a != e and need.get(a, 0) < i:
                        need[a] = i
        for c in writes:
            if c.w is not None:
                if need.get(c.w[0], 0) < c.w[1]:
                    need[c.w[0]] = c.w[1]
            for a, i in c.r.items():
                if need.get(a, 0) < i:
                    need[a] = i
        for a, i in need.items():
            if a == e and skip_self:
                continue
            self._wait(e, a, i)

    def op(self, e, fn, reads=(), writes=(), inc=True):
        reads = self._flat(reads)
        writes = self._flat(writes)
        self._deps(e, reads, writes, skip_self=(e == "pe"))
        ins = fn(self.eng[e])
        A = self.agents[e]
        if inc:
            A.count += 1
            ins.then_inc(A.sem, 1)
            idx = A.count
        else:
            idx = A.count + 1
        for c in reads:
            if c.r.get(e, 0) < idx:
                c.r[e] = idx
        for c in writes:
            c.w = (e, idx)
            c.r = {}
        self.ninstr += 1
        return ins

    def dma(self, q, out, in_, reads=(), writes=()):
        if q == "pool":
            lane = "L%d" % (self.n_lanes - 8 + self.lane_rr_pool)
            self.lane_rr_pool = (self.lane_rr_pool + 1) % 8
        else:
            lane = "L%d" % self.lane_rr
            self.lane_rr = (self.lane_rr + 1) % (self.n_lanes - 8)
        L = self.agents[lane]
        reads = self._flat(reads)
        writes = self._flat(writes)
        self._deps(q, reads, writes, skip_self=False)
        self._wait(q, lane, L.count)
        ins = self.eng[q].dma_start(out=out, in_=in_)
        L.count += 1
        ins.then_inc(L.sem, 16)
        for c in reads:
            c.r[lane] = L.count
        for c in writes:
            c.w = (lane, L.count)
            c.r = {}
        self.ninstr += 1
        return ins

    def finish(self):
        for i in range(self.n_lanes):
            L = self.agents["L%d" % i]
            self._wait("sp", "L%d" % i, L.count)
        for n in ("pe", "act", "dve", "pool"):
            self._wait("sp", n, self.agents[n].count)


def pp(v):
    v = np.asarray(v, np.float32)
    return np.ascontiguousarray(v.reshape(-1, 128).T)


def fm(x):
    T, C = x.shape
    return np.ascontiguousarray(x.T.reshape(C // 128, 128, T))


def unfm(y):
    c, p, T = y.shape
    return np.ascontiguousarray(y.reshape(c * p, T).T)


class SmallPack:
    def __init__(self):
        self.off = {}
        self.arrs = []
        self.n = 0

    def add(self, name, arr):
        arr = np.asarray(arr, np.float32)
        assert arr.shape[0] == 128 and arr.ndim == 2, (name, arr.shape)
        self.off[name] = (self.n, arr.shape[1])
        self.arrs.append(arr)
        self.n += arr.shape[1]

    def build(self):
        return np.ascontiguousarray(np.concatenate(self.arrs, axis=1))


def small_layout(cfg, inputs=None, cond=None, flags=None):
    c = cfg
    sp = SmallPack()
    sp.add("cond", pp(cond) if cond is not None else np.zeros((128, c.KC), np.float32))
    fl = np.zeros((128, 8), np.float32)
    if flags is not None:
        fl[:, :len(flags)] = np.asarray(flags, np.float32)[None, :]
    sp.add("flags", fl)

    def g(name, shape):
        if inputs is None:
            return np.zeros(shape, np.float32)
        return np.asarray(inputs[name], np.float32)

    for i in range(c.DEPTH):
        sp.add("b_mod%d" % i, pp(g("b_mod", (c.DEPTH, 6 * c.D))[i]))
        npre = g("norm_pre", (c.DEPTH, 2, c.D))[i]
        npo = g("norm_post", (c.DEPTH, 2, c.D))[i]
        sp.add("npre%d_0" % i, pp(npre[0]))
        sp.add("npre%d_1" % i, pp(npre[1]))
        sp.add("npost%d_0" % i, pp(npo[0]))
        sp.add("npost%d_1" % i, pp(npo[1]))
        sp.add("ffn_b%d" % i, pp(g("ffn_b_dw", (c.DEPTH, 2 * c.DFF))[i]))
        wdw = g("ffn_w_dw", (c.DEPTH, 3, 2 * c.DFF))[i]
        for k in range(3):
            sp.add("ffn_w%d_%d" % (i, k), pp(wdw[k]))
    for j in range(c.NCONV):
        sp.add("cv_b1_%d" % j, pp(g("cv_b_pw1", (c.NCONV, 2 * c.D))[j]))
        wdw = g("cv_w_dw", (c.NCONV, c.CW, c.D))[j]
        for k in range(c.CW):
            sp.add("cv_w%d_%d" % (j, k), pp(wdw[k]))
        sp.add("cv_bdw_%d" % j, pp(g("cv_b_dw", (c.NCONV, c.D))[j]))
        sp.add("cv_lng_%d" % j, pp(g("cv_ln_g", (c.NCONV, c.D))[j]))
        sp.add("cv_lnb_%d" % j, pp(g("cv_ln_b", (c.NCONV, c.D))[j]))
        sp.add("cv_b2_%d" % j, pp(g("cv_b_pw2", (c.NCONV, c.D))[j]))
    for j in range(c.NATT):
        sp.add("att_qn%d" % j, g("attn_q_norm", (c.NATT, 128))[j].reshape(128, 1))
        sp.add("att_kn%d" % j, g("attn_k_norm", (c.NATT, 128))[j].reshape(128, 1))
    ncd = c.DI + 2 * c.SG * 128
    for j in range(c.NSSD):
        wc = g("ssd_w_conv", (c.NSSD, c.SCW, ncd))[j]
        for k in range(c.SCW):
            sp.add("ssd_cw%d_%d" % (j, k), pp(wc[k]))
        sp.add("ssd_cb%d" % j, pp(g("ssd_b_conv", (c.NSSD, ncd))[j]))
        sp.add("ssd_ng%d" % j, pp(g("ssd_norm_g", (c.NSSD, c.DI))[j]))
        sp.add("ssd_dch%d" % j, pp(np.repeat(g("ssd_d", (c.NSSD, c.SH))[j], 64)))
        sp.add("ssd_dtb%d" % j, np.broadcast_to(g("ssd_dt_bias", (c.NSSD, 2, c.SH))[j].reshape(1, -1), (128, 2 * c.SH)))
        sp.add("ssd_alog%d" % j, np.broadcast_to(g("ssd_a_log", (c.NSSD, 2, c.SH))[j].reshape(1, -1), (128, 2 * c.SH)))
    return sp


class Builder:
    def __init__(self, cfg, debug=False, n_layers=None):
        self.cfg = cfg
        self.debug = debug
        self.n_layers = cfg.DEPTH if n_layers is None else n_layers
        self.sp_layout = small_layout(cfg)

    def sv(self, name, col=None, n=1):
        o, w = self.sp_layout.off[name]
        if col is None:
            return self.SV[:, o:o + w]
        return self.SV[:, o + col:o + col + n]

    def work(self):
        i = self.wk_rr
        self.wk_rr = (self.wk_rr + 1) % len(self.WK)
        return self.WK[i], self.WKc[i]

    def wslot(self):
        i = self.wb_rr
        self.wb_rr = (self.wb_rr + 1) % len(self.WB)
        return self.WB[i], self.WBc[i]

    def psum(self):
        i = self.ps_rr
        self.ps_rr = (self.ps_rr + 1) % 2
        return self.PS[i], self.PSc[i]

    def psum_bank(self):
        c = self.cfg
        k = self.psb_rr
        self.psb_rr = (self.psb_rr + 1) % 8
        i, b = k // 4, k % 4
        return self.PS[i][:, b * 512:(b + 1) * 512], self.PSc[i][b]

    def load_w(self, w_ap, r0, nrows, c0, ncols):
        P = self.P
        slot, cell = self.wslot()
        kcn = nrows // 128
        assert kcn * ncols <= self.WSLOT
        view = slot[:, 0:kcn * ncols].rearrange("p (k m) -> p k m", k=kcn)
        src = w_ap[r0:r0 + nrows, c0:c0 + ncols].rearrange("(k p) m -> p k m", p=128)
        P.dma("pool", view, src, writes=[cell])
        return view, cell

    def matmul_acc(self, ps, pscell, wview, wcell, mcol, kcn, rhs_fn, first=True, last=True, kc0=0, ktot=None):
        P = self.P
        c = self.cfg
        ktot = kcn if ktot is None else ktot
        for kc in range(kcn):
            for tb in range(c.TB):
                rhs, rcells = rhs_fn(kc, tb)
                st = first and (kc0 + kc == 0)
                sp_ = last and (kc0 + kc == ktot - 1)
                fin = (kc == kcn - 1) and (tb == c.TB - 1)
                P.op("pe",
                     lambda e, o=ps[:, tb * c.TBW:(tb + 1) * c.TBW], l=wview[:, kc, mcol:mcol + 128], r=rhs, st=st, sp_=sp_:
                     e.matmul(o, lhsT=l, rhs=r, start=st, stop=sp_),
                     reads=[wcell] + list(rcells), writes=[pscell], inc=fin)

    def hb_rhs(self, kc, tb):
        c = self.cfg
        return self.HB[:, kc, tb * c.TBW:(tb + 1) * c.TBW], [self.HBc]

    def colsum_bcast(self, acc, acc_cell):
        P = self.P
        c = self.cfg
        ps, pc = self.psum()
        for tb in range(c.TB):
            P.op("pe", lambda e, o=ps[:, tb * c.TBW:(tb + 1) * c.TBW], r=acc[:, tb * c.TBW:(tb + 1) * c.TBW]:
                 e.matmul(o, lhsT=self.ONES_F[:], rhs=r, start=True, stop=True),
                 reads=[acc_cell, self.constc], writes=[pc], inc=(tb == c.TB - 1))
        return ps, pc

    def rstd_from_sq(self, acc, acc_cell, out, out_cell, n):
        P = self.P
        c = self.cfg
        ps, pc = self.colsum_bcast(acc, acc_cell)
        P.op("dve", lambda e: e.tensor_scalar(out=out[:, 0:c.T], in0=ps[:, 0:c.T], scalar1=1.0 / n, scalar2=EPS,
                                              op0=ALU.mult, op1=ALU.add), reads=[pc], writes=[out_cell])
        P.op("act", lambda e: e.activation(out=out[:, 0:c.T], in_=out[:, 0:c.T], func=AF.Sqrt),
             reads=[out_cell], writes=[out_cell])
        P.op("dve", lambda e: e.reciprocal(out=out[:, 0:c.T], in_=out[:, 0:c.T]), reads=[out_cell], writes=[out_cell])

    def sq_accum(self, t, tcell, acc, acc_cell, first):
        P = self.P
        c = self.cfg
        if first:
            P.op("act", lambda e: e.activation(out=acc[:, 0:c.T], in_=t, func=AF.Square), reads=[tcell], writes=[acc_cell])
        else:
            sq, sqc = self.work()
            P.op("act", lambda e: e.activation(out=sq[:, 0:c.T], in_=t, func=AF.Square), reads=[tcell], writes=[sqc])
            P.op("dve", lambda e: e.tensor_tensor(out=acc[:, 0:c.T], in0=acc[:, 0:c.T], in1=sq[:, 0:c.T], op=ALU.add),
                 reads=[sqc, acc_cell], writes=[acc_cell])

    def modulation(self, i):
        P = self.P
        c = self.cfg
        w = self.W["w_mod"][i]
        ps, pc = self.psum()
        noc = 6 * c.KC
        cw = min(256, 6 * c.D)
        for cb in range(6 * c.D // cw):
            view, wc = self.load_w(w, 0, c.D, cb * cw, cw)
            for ml in range(cw // 128):
                oc = cb * (cw // 128) + ml
                for kc in range(c.KC):
                    P.op("pe", lambda e, o=ps[:, oc:oc + 1], l=view[:, kc, ml * 128:(ml + 1) * 128], r=self.SC[:, kc:kc + 1],
                         st=(kc == 0), sp_=(kc == c.KC - 1): e.matmul(o, lhsT=l, rhs=r, start=st, stop=sp_),
                         reads=[wc, self.constc], writes=[pc], inc=(kc == c.KC - 1))
        mod = self.MOD[:, i, :]
        P.op("dve", lambda e: e.tensor_tensor(out=mod, in0=ps[:, 0:noc], in1=self.sv("b_mod%d" % i), op=ALU.add),
             reads=[pc, self.constc], writes=[self.MODc])
        for s in range(2):
            sc = self.MOD[:, i, (3 * s + 1) * c.KC:(3 * s + 2) * c.KC]
            ga = self.MOD[:, i, (3 * s + 2) * c.KC:(3 * s + 3) * c.KC]
            P.op("dve", lambda e, sc=sc, s=s: e.scalar_tensor_tensor(out=sc, in0=sc, scalar=1.0, in1=self.sv("npre%d_%d" % (i, s)),
                                                                   op0=ALU.add, op1=ALU.mult),
                 reads=[self.MODc, self.constc], writes=[self.MODc])
            P.op("dve", lambda e, ga=ga, s=s: e.tensor_tensor(out=ga, in0=ga, in1=self.sv("npost%d_%d" % (i, s)), op=ALU.mult),
                 reads=[self.MODc, self.constc], writes=[self.MODc])

    def modv(self, i, which, col):
        c = self.cfg
        idx = {"sh_m": 0, "sc_m": 1, "ga_m": 2, "sh_f": 3, "sc_f": 4, "ga_f": 5}[which]
        return self.MOD[:, i, idx * c.KC + col: idx * c.KC + col + 1]

    def prenorm(self, i, s):
        P = self.P
        c = self.cfg
        sc = "sc_m" if s == 0 else "sc_f"
        sh = "sh_m" if s == 0 else "sh_f"
        for kc in range(c.KC):
            yt, yc = self.work()
            P.dma("sp", yt[:, 0:c.T], self.Y[kc], reads=[self.Yc[kc]], writes=[yc])
            P.op("dve", lambda e, yt=yt, kc=kc: e.scalar_tensor_tensor(out=yt[:, 0:c.T], in0=yt[:, 0:c.T], scalar=self.modv(i, sc, kc),
                                                                     in1=self.RSTD[:, 0:c.T], op0=ALU.mult, op1=ALU.mult),
                 reads=[yc, self.MODc, self.RSTDc], writes=[yc])
            P.op("act", lambda e, yt=yt, kc=kc: e.activation(out=self.HB[:, kc, :], in_=yt[:, 0:c.T], func=AF.Identity,
                                                            bias=self.modv(i, sh, kc), scale=1.0),
                 reads=[yc, self.MODc], writes=[self.HBc])

    def post(self, i, s, last):
        P = self.P
        c = self.cfg
        ga = "ga_m" if s == 0 else "ga_f"
        self.rstd_from_sq(self.SQO, self.SQOc, self.RSTDO, self.RSTDOc, c.D)
        for kc in range(c.KC):
            ot, oc = self.work()
            yt, yc = self.work()
            P.dma("sp", ot[:, 0:c.T], self.O[kc], reads=[self.Oc[kc]], writes=[oc])
            P.dma("sp", yt[:, 0:c.T], self.Y[kc], reads=[self.Yc[kc]], writes=[yc])
            P.op("dve", lambda e, ot=ot, kc=kc: e.scalar_tensor_tensor(out=ot[:, 0:c.T], in0=ot[:, 0:c.T], scalar=self.modv(i, ga, kc),
                                                                     in1=self.RSTDO[:, 0:c.T], op0=ALU.mult, op1=ALU.mult),
                 reads=[oc, self.MODc, self.RSTDOc], writes=[oc])
            P.op("dve", lambda e, ot=ot, yt=yt: e.tensor_tensor(out=yt[:, 0:c.T], in0=yt[:, 0:c.T], in1=ot[:, 0:c.T], op=ALU.add),
                 reads=[oc, yc], writes=[yc])
            dst = self.YOUT[kc] if last else self.Y[kc]
            P.dma("sp", dst, yt[:, 0:c.T], reads=[yc], writes=[self.Yc[kc]])
            if not last:
                self.sq_accum(yt[:, 0:c.T], yc, self.SQY, self.SQYc, first=(kc == 0))
        if not last:
            self.rstd_from_sq(self.SQY, self.SQYc, self.RSTD, self.RSTDc, c.D)

    def out_epilogue(self, ps, pc, m, bias_ap):
        P = self.P
        c = self.cfg
        ot, oc = self.work()
        if bias_ap is None:
            P.op("act", lambda e: e.activation(out=ot[:, 0:c.T], in_=ps[:, 0:c.T], func=AF.Identity), reads=[pc], writes=[oc])
        else:
            P.op("act", lambda e: e.activation(out=ot[:, 0:c.T], in_=ps[:, 0:c.T], func=AF.Identity, bias=bias_ap, scale=1.0),
                 reads=[pc, self.constc], writes=[oc])
        P.dma("sp", self.O[m], ot[:, 0:c.T], reads=[oc], writes=[self.Oc[m]])
        self.sq_accum(ot[:, 0:c.T], oc, self.SQO, self.SQOc, first=(m == 0))

    def halo_fill(self, eng, buf, cell, h):
        P = self.P
        c = self.cfg
        S = c.SEG
        n = c.NSEG
        fS = self.sv("flags", 0)
        if n > 1:
            P.op(eng, lambda e: e.tensor_scalar(out=buf[:, 1:n, 0:h], in0=buf[:, 0:n - 1, S:S + h], scalar1=fS, scalar2=None, op0=ALU.mult),
                 reads=[cell, self.constc], writes=[cell])
            P.op(eng, lambda e: e.tensor_scalar(out=buf[:, 0:n - 1, S + h:S + 2 * h], in0=buf[:, 1:n, h:2 * h], scalar1=fS, scalar2=None, op0=ALU.mult),
                 reads=[cell, self.constc], writes=[cell])

    def ffn(self, i, last):
        P = self.P
        c = self.cfg
        S = c.SEG
        n = c.NSEG
        self.prenorm(i, 1)
        self.a2_switch(self.ABc)
        wup = self.W["ffn_w_up"][i]
        PW = 2
        views = {}
        for m in range(c.FC):
            if m % PW == 0:
                npair = min(PW, c.FC - m)
                va = self.load_w(wup, 0, c.D, m * 128, npair * 128)
                vg = self.load_w(wup, 0, c.D, c.DFF + m * 128, npair * 128)
            ml = m % PW
            res = []
            for half, (view, wc) in enumerate((va, vg)):
                ps, pc = self.psum()
                self.matmul_acc(ps, pc, view, wc, ml * 128, c.KC, self.hb_rhs)
                ch = half * c.FC + m
                pad, padc = self.work()
                pv = pad[:, 0:n * (S + 2)].rearrange("p (s w) -> p s w", s=n)
                P.op("act", lambda e, pv=pv, ps=ps: e.activation(out=pv[:, :, 1:S + 1], in_=ps[:, 0:c.T].rearrange("p (s w) -> p s w", s=n),
                                                               func=AF.Identity), reads=[pc], writes=[padc])
                P.op("dve", lambda e, pv=pv: e.memset(pv[:, 0, 0:1], 0.0), reads=[padc], writes=[padc])
                P.op("dve", lambda e, pv=pv: e.memset(pv[:, n - 1, S + 1:S + 2], 0.0), reads=[padc], writes=[padc])
                self.halo_fill("dve", pv, padc, 1)
                cv, cvc = self.work()
                cvv = cv[:, 0:c.T].rearrange("p (s w) -> p s w", s=n)
                P.op("dve", lambda e, pv=pv, cvv=cvv, ch=ch: e.tensor_scalar(out=cvv, in0=pv[:, :, 0:S], scalar1=self.sv("ffn_w%d_0" % i, ch),
                                                                           scalar2=self.sv("ffn_b%d" % i, ch), op0=ALU.mult, op1=ALU.add),
                     reads=[padc, self.constc], writes=[cvc])
                for k in (1, 2):
                    P.op("dve", lambda e, pv=pv, cvv=cvv, ch=ch, k=k: e.scalar_tensor_tensor(out=cvv, in0=pv[:, :, k:k + S],
                                                                                           scalar=self.sv("ffn_w%d_%d" % (i, k), ch),
                                                                                           in1=cvv, op0=ALU.mult, op1=ALU.add),
                         reads=[padc, cvc, self.constc], writes=[cvc])
                res.append((cv, cvc))
            (ca, cac), (cg, cgc) = res
            P.op("act", lambda e, cg=cg: e.activation(out=cg[:, 0:c.T], in_=cg[:, 0:c.T], func=AF.Silu), reads=[cgc], writes=[cgc])
            ab, abc = self.abuf()
            P.op("dve", lambda e, ca=ca, cg=cg, ab=ab: e.tensor_tensor(out=ab[:, 0:c.T], in0=ca[:, 0:c.T], in1=cg[:, 0:c.T], op=ALU.mult),
                 reads=[cac, cgc], writes=[abc])
            P.dma("sp", self.ACTD[m], ab[:, 0:c.T], reads=[abc], writes=[self.ACTDc[m]])
        wdn = self.W["ffn_w_down"][i]
        KG = 11 if c.FC % 11 == 0 else c.FC
        for m0 in range(0, c.KC, 2):
            ms = [m for m in (m0, m0 + 1) if m < c.KC]
            pss = [self.psum() for _ in ms]
            for g0 in range(0, c.FC, KG):
                wv = [self.load_w(wdn, g0 * 128, KG * 128, m * 128, 128) for m in ms]
                for kk in range(KG):
                    kc = g0 + kk
                    slot = self.hb_slot()
                    P.dma("sp", self.HB[:, slot, :], self.ACTD[kc], reads=[self.ACTDc[kc]], writes=[self.HBsc[slot]])
                    for (ps, pc), (view, wc) in zip(pss, wv):
                        for tb in range(c.TB):
                            P.op("pe", lambda e, o=ps[:, tb * c.TBW:(tb + 1) * c.TBW], l=view[:, kk, 0:128],
                                 r=self.HB[:, slot, tb * c.TBW:(tb + 1) * c.TBW], st=(kc == 0), sp_=(kc == c.FC - 1):
                                 e.matmul(o, lhsT=l, rhs=r, start=st, stop=sp_),
                                 reads=[wc, self.HBsc[slot]], writes=[pc], inc=(tb == c.TB - 1))
            for m, (ps, pc) in zip(ms, pss):
                self.out_epilogue(ps, pc, m, None)
        self.hb_release()
        self.post(i, 1, last)

    def hb_slot(self):
        if not self.hb_slot_mode:
            for k in range(self.cfg.KC):
                self.HBsc[k].w = self.HBc.w
                self.HBsc[k].r = dict(self.HBc.r)
            self.hb_slot_mode = True
            self.hb_rr = 0
        s = self.hb_rr
        self.hb_rr = (self.hb_rr + 1) % self.cfg.KC
        return s

    def hb_release(self):
        if self.hb_slot_mode:
            r = {}
            w = self.HBc.w
            for k in range(self.cfg.KC):
                cl = self.HBsc[k]
                for a, i in cl.r.items():
                    if r.get(a, 0) < i:
                        r[a] = i
                if cl.w is not None:
                    if r.get(cl.w[0], 0) < cl.w[1]:
                        r[cl.w[0]] = cl.w[1]
            self.HBc.r = r
            self.hb_slot_mode = False

    def a2_switch(self, new_cells):
        self.P.handoff(self.A2cells, new_cells)
        self.A2cells = list(new_cells)

    def a2_conv_view(self):
        self.a2_switch([self.UPc, self.DGc])
        self.P.op("dve", lambda e: e.memset(self.UPF, 0.0), writes=[self.UPc])

    def abuf(self):
        i = self.ab_rr
        self.ab_rr = (self.ab_rr + 1) % len(self.AB)
        return self.AB[i], self.ABc[i]

    def conformer(self, i, j):
        P = self.P
        c = self.cfg
        S = c.SEG
        n = c.NSEG
        H = (c.CW - 1) // 2
        PADW = S + 2 * H
        self.prenorm(i, 0)
        self.a2_conv_view()
        w1 = self.W["cv_w_pw1"][j]
        PW = 2
        for m in range(c.KC):
            if m % PW == 0:
                npair = min(PW, c.KC - m)
                va = self.load_w(w1, 0, c.D, m * 128, npair * 128)
                vg = self.load_w(w1, 0, c.D, c.D + m * 128, npair * 128)
            ml = m % PW
            psa, pca = self.psum()
            self.matmul_acc(psa, pca, va[0], va[1], ml * 128, c.KC, self.hb_rhs)
            psg, pcg = self.psum()
            self.matmul_acc(psg, pcg, vg[0], vg[1], ml * 128, c.KC, self.hb_rhs)
            at, atc = self.work()
            gt, gtc = self.work()
            P.op("act", lambda e, at=at, psa=psa, m=m: e.activation(out=at[:, 0:c.T], in_=psa[:, 0:c.T], func=AF.Identity,
                                                                  bias=self.sv("cv_b1_%d" % j, m), scale=1.0),
                 reads=[pca, self.constc], writes=[atc])
            P.op("act", lambda e, gt=gt, psg=psg, m=m: e.activation(out=gt[:, 0:c.T], in_=psg[:, 0:c.T], func=AF.Sigmoid,
                                                                  bias=self.sv("cv_b1_%d" % j, c.KC + m), scale=1.0),
                 reads=[pcg, self.constc], writes=[gtc])
            up = self.UP
            P.op("dve", lambda e, at=at, gt=gt: e.tensor_tensor(out=up[:, :, H:H + S], in0=at[:, 0:c.T].rearrange("p (s w) -> p s w", s=n),
                                                              in1=gt[:, 0:c.T].rearrange("p (s w) -> p s w", s=n), op=ALU.mult),
                 reads=[atc, gtc], writes=[self.UPc])
            self.halo_fill("dve", up, self.UPc, H)
            for k in range(c.CW):
                P.op("dve", lambda e, k=k, m=m: e.tensor_scalar(out=self.DG[:, k, :], in0=self.IDB[:], scalar1=self.sv("cv_w%d_%d" % (j, k), m),
                                                              scalar2=None, op0=ALU.mult),
                     reads=[self.constc, self.DGc], writes=[self.DGc])
            psc, pcc = self.psum()
            for s_ in range(n):
                for k in range(c.CW):
                    P.op("pe", lambda e, s_=s_, k=k, psc=psc: e.matmul(psc[:, s_ * S:(s_ + 1) * S], lhsT=self.DG[:, k, :], rhs=up[:, s_, k:k + S],
                                                                    start=(k == 0), stop=(k == c.CW - 1)),
                         reads=[self.DGc, self.UPc], writes=[pcc], inc=(s_ == n - 1 and k == c.CW - 1))
            ct, ctc = self.work()
            P.op("act", lambda e, ct=ct, psc=psc, m=m: e.activation(out=ct[:, 0:c.T], in_=psc[:, 0:c.T], func=AF.Identity,
                                                                  bias=self.sv("cv_bdw_%d" % j, m), scale=1.0),
                 reads=[pcc, self.constc], writes=[ctc])
            P.dma("sp", self.CV[m], ct[:, 0:c.T], reads=[ctc], writes=[self.CVc[m]])
            if m == 0:
                P.op("dve", lambda e, ct=ct: e.tensor_copy(out=self.SQY[:, 0:c.T], in_=ct[:, 0:c.T]), reads=[ctc], writes=[self.SQYc])
            else:
                P.op("dve", lambda e, ct=ct: e.tensor_tensor(out=self.SQY[:, 0:c.T], in0=self.SQY[:, 0:c.T], in1=ct[:, 0:c.T], op=ALU.add),
                     reads=[ctc, self.SQYc], writes=[self.SQYc])
            self.sq_accum(ct[:, 0:c.T], ctc, self.SQO, self.SQOc, first=(m == 0))
        psm, pcm = self.colsum_bcast(self.SQY, self.SQYc)
        mean, meanc = self.SQY, self.SQYc
        P.op("act", lambda e: e.activation(out=mean[:, 0:c.T], in_=psm[:, 0:c.T], func=AF.Identity, scale=1.0 / c.D),
             reads=[pcm], writes=[meanc])
        psq, pcq = self.colsum_bcast(self.SQO, self.SQOc)
        var, varc = self.SQO, self.SQOc
        msq, msqc = self.work()
        P.op("dve", lambda e: e.tensor_tensor(out=msq[:, 0:c.T], in0=mean[:, 0:c.T], in1=mean[:, 0:c.T], op=ALU.mult),
             reads=[meanc], writes=[msqc])
        P.op("dve", lambda e: e.scalar_tensor_tensor(out=var[:, 0:c.T], in0=psq[:, 0:c.T], scalar=1.0 / c.D, in1=msq[:, 0:c.T],
                                                     op0=ALU.mult, op1=ALU.subtract), reads=[pcq, msqc], writes=[varc])
        P.op("dve", lambda e: e.tensor_scalar(out=var[:, 0:c.T], in0=var[:, 0:c.T], scalar1=EPS, scalar2=None, op0=ALU.add),
             reads=[varc], writes=[varc])
        P.op("act", lambda e: e.activation(out=var[:, 0:c.T], in_=var[:, 0:c.T], func=AF.Sqrt), reads=[varc], writes=[varc])
        P.op("dve", lambda e: e.reciprocal(out=var[:, 0:c.T], in_=var[:, 0:c.T]), reads=[varc], writes=[varc])
        rln = var
        for m in range(c.KC):
            ct, ctc = self.work()
            P.dma("sp", ct[:, 0:c.T], self.CV[m], reads=[self.CVc[m]], writes=[ctc])
            P.op("dve", lambda e, ct=ct: e.tensor_tensor(out=ct[:, 0:c.T], in0=ct[:, 0:c.T], in1=mean[:, 0:c.T], op=ALU.subtract),
                 reads=[ctc, meanc], writes=[ctc])
            P.op("dve", lambda e, ct=ct: e.tensor_tensor(out=ct[:, 0:c.T], in0=ct[:, 0:c.T], in1=rln[:, 0:c.T], op=ALU.mult),
                 reads=[ctc, varc], writes=[ctc])
            P.op("act", lambda e, ct=ct, m=m: e.activation(out=self.HB[:, m, :], in_=ct[:, 0:c.T], func=AF.Silu,
                                                         bias=self.sv("cv_lnb_%d" % j, m), scale=self.sv("cv_lng_%d" % j, m)),
                 reads=[ctc, self.constc], writes=[self.HBc])
        w2 = self.W["cv_w_pw2"][j]
        CWD = min(256, c.D)
        for m in range(c.KC):
            if (m * 128) % CWD == 0:
                v2 = self.load_w(w2, 0, c.D, m * 128, CWD)
            ps, pc = self.psum()
            self.matmul_acc(ps, pc, v2[0], v2[1], (m * 128) % CWD, c.KC, self.hb_rhs)
            self.out_epilogue(ps, pc, m, self.sv("cv_b2_%d" % j, m))
        self.post(i, 0, False)

    def ssd(self, i, j):
        P = self.P
        c = self.cfg
        S = c.SEG
        n = c.NSEG
        DI, SH, SG = c.DI, c.SH, c.SG
        HPG = SH // SG
        GW = HPG * 64
        XC = DI // 128
        XPG = GW // 128
        NCH = c.T // 128
        NDT = 2 * SH
        NF = NCH * NDT
        NXBC = XC + 2 * SG
        CPS = S // 128
        HBAT = min(4, HPG)
        w_in = self.W["ssd_w_in"][j]
        off_x = DI
        off_dt = 2 * DI + 2 * SG * 128
        fS = self.sv("flags", 0)
        IDB = self.IDB
        U, LW, SL, SU = (self.CF[:, k, :] for k in (2, 3, 4, 5))

        def b_last(ap, k):
            return ap.unsqueeze(2).to_broadcast([128, ap.shape[1], k])

        def b_mid(ap, k):
            return ap.unsqueeze(1).to_broadcast([128, k, ap.shape[1]])

        self.prenorm(i, 0)
        self.a2_conv_view()
        wdt, wdtc = self.load_w(w_in, 0, c.D, off_dt, NDT)
        psd, pcd = self.psum()
        for ch in range(NCH):
            for kc in range(c.KC):
                P.op("pe", lambda e, ch=ch, kc=kc: e.matmul(psd[:, ch * NDT:(ch + 1) * NDT], lhsT=self.HB[:, kc, ch * 128:(ch + 1) * 128],
                                                          rhs=wdt[:, kc, 0:NDT], start=(kc == 0), stop=(kc == c.KC - 1)),
                     reads=[self.HBc, wdtc], writes=[pcd], inc=(kc == c.KC - 1))
        dtw, dtwc = self.RSTDO, self.RSTDOc
        P.op("dve", lambda e: e.tensor_tensor(out=dtw[:, 0:NF].rearrange("p (c h) -> p c h", c=NCH),
                                              in0=psd[:, 0:NF].rearrange("p (c h) -> p c h", c=NCH),
                                              in1=b_mid(self.sv("ssd_dtb%d" % j), NCH), op=ALU.add),
             reads=[pcd, self.constc], writes=[dtwc])
        P.op("act", lambda e: e.activation(out=dtw[:, 0:NF], in_=dtw[:, 0:NF], func=AF.Exp), reads=[dtwc], writes=[dtwc])
        P.op("act", lambda e: e.activation(out=dtw[:, 0:NF], in_=dtw[:, 0:NF], func=AF.Ln, bias=self.CF[:, 0, 0:1], scale=1.0),
             reads=[dtwc, self.constc], writes=[dtwc])
        if getattr(self, 'stop_at', None) == 'p0a':
            self.bailed = True
            return
        CPT = c.T // 256
        for cb in range(DI // 256):
            wz, wzc = self.load_w(w_in, 0, c.D, cb * 256, 256)
            for c0 in range(0, NCH, CPT):
                ps, pc = self.psum()
                for cc in range(CPT):
                    ch = c0 + cc
                    for kc in range(c.KC):
                        P.op("pe", lambda e, ch=ch, cc=cc, kc=kc, ps=ps: e.matmul(ps[:, cc * 256:(cc + 1) * 256],
                                                                                lhsT=self.HB[:, kc, ch * 128:(ch + 1) * 128],
                                                                                rhs=wz[:, kc, 0:256], start=(kc == 0), stop=(kc == c.KC - 1)),
                             reads=[self.HBc, wzc], writes=[pc], inc=(kc == c.KC - 1))
                zt, ztc = self.work()
                ztb = zt[:, 0:c.T // 2].bitcast(BF16)
                P.op("act", lambda e, ztb=ztb, ps=ps: e.activation(out=ztb, in_=ps[:, 0:c.T], func=AF.Silu), reads=[pc], writes=[ztc])
                P.dma("sp", self.ZT[c0:c0 + CPT, :, cb * 256:(cb + 1) * 256].rearrange("c p m -> p c m"),
                      ztb.rearrange("p (c m) -> p c m", c=CPT), reads=[ztc], writes=[[self.ZTc[c0 + q][cb] for q in range(CPT)]])
        if getattr(self, 'stop_at', None) == 'p0b':
            self.bailed = True
            return
        H = (c.SCW - 1) // 2
        up5 = self.UPF[:, 0:n * (S + 2 * H)].rearrange("p (s w) -> p s w", s=n)
        for m in range(NXBC):
            if m % 2 == 0:
                wx = self.load_w(w_in, 0, c.D, off_x + m * 128, min(2, NXBC - m) * 128)
            ps, pc = self.psum()
            self.matmul_acc(ps, pc, wx[0], wx[1], (m % 2) * 128, c.KC, self.hb_rhs)
            P.op("act", lambda e, ps=ps: e.activation(out=up5[:, :, H:H + S], in_=ps[:, 0:c.T].rearrange("p (s w) -> p s w", s=n),
                                                    func=AF.Identity), reads=[pc], writes=[self.UPc])
            self.halo_fill("dve", up5, self.UPc, H)
            for k in range(c.SCW):
                P.op("dve", lambda e, k=k, m=m: e.tensor_scalar(out=self.DG[:, k, :], in0=IDB[:], scalar1=self.sv("ssd_cw%d_%d" % (j, k), m),
                                                              scalar2=None, op0=ALU.mult),
                     reads=[self.constc, self.DGc], writes=[self.DGc])
            ps2, pc2 = self.psum()
            for s_ in range(n):
                for k in range(c.SCW):
                    P.op("pe", lambda e, s_=s_, k=k, ps2=ps2: e.matmul(ps2[:, s_ * S:(s_ + 1) * S], lhsT=self.DG[:, k, :], rhs=up5[:, s_, k:k + S],
                                                                    start=(k == 0), stop=(k == c.SCW - 1)),
                         reads=[self.DGc, self.UPc], writes=[pc2], inc=(s_ == n - 1 and k == c.SCW - 1))
            xt, xtc = self.work()
            xtb = xt[:, 0:c.T // 2].bitcast(BF16)
            P.op("act", lambda e, xtb=xtb, ps2=ps2, m=m: e.activation(out=xtb, in_=ps2[:, 0:c.T], func=AF.Silu,
                                                                    bias=self.sv("ssd_cb%d" % j, m), scale=1.0),
                 reads=[pc2, self.constc], writes=[xtc])
            P.dma("sp", self.XBC[m], xtb, reads=[xtc], writes=[self.XBCc[m]])
        if getattr(self, 'stop_at', None) == 'p0c':
            self.bailed = True
            return
        self.hb_release()
        AF32 = self.ARENA[:, :].bitcast(F32)
        tl = [AF32[:, k * NF:(k + 1) * NF] for k in range(5)]
        tlc = [P.cell() for _ in range(5)]
        DT, DTA, ACS, DTE, DCH = tl
        DTc, DTAc, ACSc, DTEc, DCHc = tlc
        boff = 5 * NF * 2
        big = [self.ARENA[:, boff + k * DI: boff + (k + 1) * DI] for k in range(3)]
        bigc = [P.cell() for _ in range(3)]
        P.handoff([self.HBc], tlc + bigc)
        v3 = lambda t: t.rearrange("p (c h) -> p c h", c=NCH)
        P.op("dve", lambda e: e.tensor_copy(out=DT, in_=dtw[:, 0:NF]), reads=[dtwc], writes=[DTc])
        aw, awc = self.work()
        P.op("act", lambda e: e.activation(out=aw[:, 0:NDT], in_=self.sv("ssd_alog%d" % j), func=AF.Exp), reads=[self.constc], writes=[awc])
        P.op("dve", lambda e: e.scalar_tensor_tensor(out=v3(DTA), in0=v3(DT), scalar=-1.0, in1=b_mid(aw[:, 0:NDT], NCH),
                                                     op0=ALU.mult, op1=ALU.mult), reads=[DTc, awc], writes=[DTAc])
        if getattr(self, 'stop_at', None) == 'q1':
            self.bailed = True
            return
        NB = (NF + 511) // 512

        def fmm(lhsT, dst_ps, dst_pc):
            for b_ in range(NB):
                w_ = min(512, NF - b_ * 512)
                P.op("pe", lambda e, b_=b_, w_=w_: e.matmul(dst_ps[:, b_ * 512:b_ * 512 + w_], lhsT=lhsT, rhs=DTA[:, b_ * 512:b_ * 512 + w_],
                                                          start=True, stop=True),
                     reads=[DTAc, self.constc], writes=[dst_pc], inc=(b_ == NB - 1))
        psF, pcF = self.psum()
        fmm(U, psF, pcF)
        P.op("act", lambda e: e.activation(out=v3(ACS)[:, :, 0:SH], in_=v3(psF[:, 0:NF])[:, :, 0:SH], func=AF.Identity), reads=[pcF], writes=[ACSc])
        if getattr(self, 'stop_at', None) == 'q2':
            self.bailed = True
            return
        psB, pcB = self.psum()
        fmm(LW, psB, pcB)
        P.op("act", lambda e: e.activation(out=v3(ACS)[:, :, SH:NDT], in_=v3(psB[:, 0:NF])[:, :, SH:NDT], func=AF.Identity), reads=[pcB], writes=[ACSc])
        if getattr(self, 'stop_at', None) == 'q3':
            self.bailed = True
            return
        psT, pcT = self.psum()
        fmm(self.ONES_F, psT, pcT)
        P.op("act", lambda e: e.activation(out=DCH, in_=psT[:, 0:NF], func=AF.Exp), reads=[pcT], writes=[DCHc])
        if getattr(self, 'stop_at', None) == 'r1':
            self.bailed = True
            return
        P.op("dve", lambda e: e.tensor_tensor(out=DTE, in0=psT[:, 0:NF], in1=ACS, op=ALU.subtract), reads=[pcT, ACSc], writes=[DTEc])
        if getattr(self, 'stop_at', None) == 'r2':
            self.bailed = True
            return
        P.op("act", lambda e: e.activation(out=DTE, in_=DTE, func=AF.Exp), reads=[DTEc], writes=[DTEc])
        if getattr(self, 'stop_at', None) == 'r3':
            self.bailed = True
            return
        P.op("dve", lambda e: e.tensor_tensor(out=DTE, in0=DTE, in1=DT, op=ALU.mult), reads=[DTEc, DTc], writes=[DTEc])
        if getattr(self, 'stop_at', None) == 'r4':
            self.bailed = True
            return
        P.op("act", lambda e: e.activation(out=ACS, in_=ACS, func=AF.Exp), reads=[ACSc], writes=[ACSc])
        EACS, EACSc = ACS, ACSc
        if getattr(self, 'stop_at', None) == 'q4':
            self.bailed = True
            return
        DGD = self.A2[:, 0:XC * 128].rearrange("p (x m) -> p x m", x=XC)
        DGDc = P.cell()
        XCV = self.A2[:, XC * 128:2 * XC * 128].rearrange("p (x m) -> p x m", x=XC)
        XCVc = P.cell()
        BCV = self.A2[:, 2 * XC * 128:2 * XC * 128 + 2 * SG * 128].rearrange("p (x m) -> p x m", x=2 * SG)
        BCVc = P.cell()
        GMB = self.GMB[:, :].rearrange("p (d m) -> p d m", d=2)
        GMBc = P.cell()
        self.a2_switch([DGDc, XCVc, BCVc])
        for x in range(XC):
            P.op("dve", lambda e, x=x: e.tensor_scalar(out=DGD[:, x, :], in0=IDB[:], scalar1=self.sv("ssd_dch%d" % j, x), scalar2=None, op0=ALU.mult),
                 reads=[self.constc, DGDc], writes=[DGDc])
        WBF = self.WBALL[:, :].bitcast(F32)
        wbc = [P.cell() for _ in range(4)]
        P.handoff(self.WBc, wbc)
        stg = [self.WBALL[:, k * DI:(k + 1) * DI] for k in range(4)]

        def load_chunk_fm(ch, first, cnt):
            v, tc_ = (XCV, XCVc) if first == 0 else (BCV, BCVc)
            P.dma("sp", v, self.XBC[first:first + cnt, :, ch * 128:(ch + 1) * 128].rearrange("x p m -> p x m"),
                  reads=[self.XBCc[first:first + cnt]], writes=[tc_])
            return v, tc_

        if getattr(self, 'stop_at', None) == 'p1':
            self.bailed = True
            return
        for ch in range(NCH):
            xcv, xcc = load_chunk_fm(ch, 0, XC)
            bcv, bcc = load_chunk_fm(ch, XC, 2 * SG)
            for g in range(SG):
                psx, pcx = self.psum_bank()
                for jj in range(XPG):
                    P.op("pe", lambda e, jj=jj, g=g, psx=psx: e.matmul(psx[:, jj * 128:(jj + 1) * 128], lhsT=xcv[:, g * XPG + jj, :], rhs=IDB[:],
                                                                    start=True, stop=True),
                         reads=[xcc, self.constc], writes=[pcx], inc=(jj == XPG - 1))
                xv = psx[:, 0:GW].rearrange("p (h q) -> p h q", h=HPG)
                for d in range(2):
                    h0 = d * SH + g * HPG
                    P.op("dve", lambda e, d=d, h0=h0, xv=xv, g=g: e.tensor_tensor(out=stg[d][:, g * GW:(g + 1) * GW].rearrange("p (h q) -> p h q", h=HPG),
                                                                               in0=xv, in1=b_last(v3(DT)[:, ch, h0:h0 + HPG], 64), op=ALU.mult),
                         reads=[pcx, DTc], writes=[wbc[d]])
                    P.op("dve", lambda e, d=d, h0=h0, xv=xv, g=g: e.tensor_tensor(out=stg[2 + d][:, g * GW:(g + 1) * GW].rearrange("p (h q) -> p h q", h=HPG),
                                                                               in0=xv, in1=b_last(v3(DTE)[:, ch, h0:h0 + HPG], 64), op=ALU.mult),
                         reads=[pcx, DTEc], writes=[wbc[2 + d]])
                psb, pcb = self.psum_bank()
                P.op("pe", lambda e, g=g, psb=psb: e.matmul(psb[:, 0:128], lhsT=bcv[:, g, :], rhs=IDB[:], start=True, stop=True),
                     reads=[bcc, self.constc], writes=[pcb])
                bt, btc = self.work()
                btb = bt[:, 0:64].bitcast(BF16)
                P.op("act", lambda e, btb=btb, psb=psb: e.activation(out=btb, in_=psb[:, 0:128], func=AF.Identity), reads=[pcb], writes=[btc])
                for d in range(2):
                    pss, pcs = self.psum_bank()
                    P.op("pe", lambda e, d=d, g=g, pss=pss, btb=btb: e.matmul(pss[:, 0:GW], lhsT=btb, rhs=stg[2 + d][:, g * GW:(g + 1) * GW], start=True, stop=True),
                         reads=[btc, wbc[2 + d]], writes=[pcs])
                    ev, evc = self.work()
                    P.op("act", lambda e, ev=ev, pss=pss: e.activation(out=ev[:, 0:GW], in_=pss[:, 0:GW], func=AF.Identity), reads=[pcs], writes=[evc])
                    P.dma("sp", self.CS[d, ch][:, g * GW:(g + 1) * GW], ev[:, 0:GW], reads=[evc], writes=[self.CSc[d][ch][g]])
            for d in range(2):
                P.dma("sp", self.XDT[d, ch], stg[d], reads=[wbc[d]], writes=[self.XDTc[d][ch]])
        if getattr(self, 'stop_at', None) == 'pA':
            self.bailed = True
            return
        HD = DI // 2
        HH = SH // 2
        st = [[WBF[:, (d * 2 + hf) * HD:(d * 2 + hf + 1) * HD] for hf in range(2)] for d in range(2)]
        stc = [[P.cell() for _ in range(2)] for _ in range(2)]
        P.handoff(wbc, stc)
        for d in range(2):
            order = list(range(NCH)) if d == 0 else list(range(NCH - 1, -1, -1))
            for hf in range(2):
                P.dma("sp", st[d][hf], self.H0T[d][:, hf * HD:(hf + 1) * HD], writes=[stc[d][hf]])
            for ch in order:
                for hf in range(2):
                    sc_, scc = st[d][hf], stc[d][hf]
                    sb_, sbc = self.work()
                    sbb = sb_[:, 0:HD // 2].bitcast(BF16)
                    P.op("act", lambda e, sbb=sbb, sc_=sc_: e.activation(out=sbb, in_=sc_, func=AF.Identity), reads=[scc], writes=[sbc])
                    P.dma("sp", self.SIN[d, ch][:, hf * HD:(hf + 1) * HD], sbb, reads=[sbc], writes=[self.SINc[d][ch][hf]])
                    cs_, csc = self.work()
                    P.dma("sp", cs_[:, 0:HD], self.CS[d, ch][:, hf * HD:(hf + 1) * HD], reads=[self.CSc[d][ch]], writes=[csc])
                    h0 = d * SH + hf * HH
                    P.op("dve", lambda e, sc_=sc_, h0=h0, ch=ch: e.tensor_tensor(out=sc_.rearrange("p (h q) -> p h q", h=HH),
                                                                               in0=sc_.rearrange("p (h q) -> p h q", h=HH),
                                                                               in1=b_last(v3(DCH)[:, ch, h0:h0 + HH], 64), op=ALU.mult),
                         reads=[scc, DCHc], writes=[scc])
                    P.op("dve", lambda e, sc_=sc_, cs_=cs_: e.tensor_tensor(out=sc_, in0=sc_, in1=cs_[:, 0:HD], op=ALU.add),
                         reads=[scc, csc], writes=[scc])
                    seg_end = (ch % CPS == CPS - 1) if d == 0 else (ch % CPS == 0)
                    if seg_end:
                        P.dma("sp", self.NEWST[ch // CPS, d][:, hf * HD:(hf + 1) * HD], sc_, reads=[scc], writes=[P.cell()])
                        P.op("dve", lambda e, sc_=sc_: e.tensor_scalar(out=sc_, in0=sc_, scalar1=fS, scalar2=None, op0=ALU.mult),
                             reads=[scc, self.constc], writes=[scc])
        if getattr(self, 'stop_at', None) == 'pA2':
            self.bailed = True
            return
        YG = WBF[:, 0:DI]
        YGc = P.cell()
        xdt = [self.WBALL[:, 2 * DI + d * DI: 2 * DI + (d + 1) * DI] for d in range(2)]
        xdtc = [P.cell() for _ in range(2)]
        P.handoff(stc, [YGc] + xdtc)
        ZTs, SIN0, SIN1 = big
        ZTsc, SIN0c, SIN1c = bigc
        sins, sinsc = (SIN0, SIN1), (SIN0c, SIN1c)
        TRI = (U, LW)
        STR = (SL, SU)
        for ch in range(NCH):
            xcv, xcc = load_chunk_fm(ch, 0, XC)
            bcv, bcc = load_chunk_fm(ch, XC, 2 * SG)
            P.dma("sp", ZTs, self.ZT[ch], reads=[self.ZTc[ch]], writes=[ZTsc])
            for d in range(2):
                P.dma("sp", xdt[d], self.XDT[d, ch], reads=[self.XDTc[d][ch]], writes=[xdtc[d]])
                P.dma("sp", sins[d], self.SIN[d, ch], reads=[self.SINc[d][ch]], writes=[sinsc[d]])
            for g in range(SG):
                psg, pcg = self.psum_bank()
                P.op("pe", lambda e, g=g, psg=psg: e.matmul(psg[:, 0:128], lhsT=bcv[:, g, :], rhs=bcv[:, SG + g, :], start=True, stop=True),
                     reads=[bcc], writes=[pcg])
                gmb, gmc = GMB, GMBc
                for d in range(2):
                    P.op("dve", lambda e, d=d, psg=psg, gmb=gmb: e.tensor_tensor(out=gmb[:, d, :], in0=psg[:, 0:128], in1=TRI[d], op=ALU.mult),
                         reads=[pcg, self.constc], writes=[gmc])
                psy, pcy = self.psum_bank()
                for jj in range(XPG):
                    x = g * XPG + jj
                    P.op("pe", lambda e, jj=jj, x=x, psy=psy: e.matmul(psy[:, jj * 128:(jj + 1) * 128], lhsT=xcv[:, x, :], rhs=DGD[:, x, :],
                                                                    start=(jj == 0), stop=False),
                         reads=[xcc, DGDc], writes=[pcy], inc=False)
                for d in range(2):
                    for hb in range(HPG // HBAT):
                        hl = hb * HBAT
                        h0 = d * SH + g * HPG + hl
                        dec, decc = self.work()
                        decv = dec[:, 0:HBAT * 128].rearrange("p (h m) -> p h m", h=HBAT)
                        P.op("dve", lambda e, d=d, h0=h0, decv=decv: e.tensor_tensor(out=decv, in0=b_mid(STR[d], HBAT),
                                                                                   in1=b_last(v3(DTA)[:, ch, h0:h0 + HBAT], 128), op=ALU.mult),
                             reads=[DTAc, self.constc], writes=[decc])
                        psd_, pcd_ = self.psum_bank()
                        for hh in range(HBAT):
                            P.op("pe", lambda e, hh=hh, d=d, decv=decv, psd_=psd_: e.matmul(psd_[:, hh * 128:(hh + 1) * 128], lhsT=decv[:, hh, :], rhs=TRI[d],
                                                                                         start=True, stop=True),
                                 reads=[decc, self.constc], writes=[pcd_], inc=(hh == HBAT - 1))
                        et, etc_ = self.work()
                        etb = et[:, 0:HBAT * 64].bitcast(BF16).rearrange("p (h m) -> p h m", h=HBAT)
                        P.op("act", lambda e, etb=etb, psd_=psd_: e.activation(out=etb, in_=psd_[:, 0:HBAT * 128].rearrange("p (h m) -> p h m", h=HBAT), func=AF.Exp),
                             reads=[pcd_], writes=[etc_])
                        P.op("dve", lambda e, etb=etb, d=d, gmb=gmb: e.tensor_tensor(out=etb, in0=etb, in1=b_mid(gmb[:, d, :], HBAT), op=ALU.mult),
                             reads=[etc_, gmc], writes=[etc_])
                        for hh in range(HBAT):
                            col = (hl + hh) * 64
                            P.op("pe", lambda e, hh=hh, col=col, d=d, etb=etb, psy=psy, g=g: e.matmul(psy[:, col:col + 64], lhsT=etb[:, hh, :],
                                                                                                   rhs=xdt[d][:, g * GW + col: g * GW + col + 64],
                                                                                                   start=False, stop=False),
                                 reads=[etc_, xdtc[d]], writes=[pcy], inc=False)
                    pso, pco = self.psum_bank()
                    P.op("pe", lambda e, d=d, g=g, pso=pso: e.matmul(pso[:, 0:GW], lhsT=bcv[:, SG + g, :], rhs=sins[d][:, g * GW:(g + 1) * GW], start=True, stop=True),
                         reads=[bcc, sinsc[d]], writes=[pco])
                    yo, yoc = self.work()
                    yob = yo[:, 0:GW // 2].bitcast(BF16)
                    h0g = d * SH + g * HPG
                    P.op("dve", lambda e, yob=yob, pso=pso, h0g=h0g: e.tensor_tensor(out=yob.rearrange("p (h q) -> p h q", h=HPG),
                                                                                   in0=pso[:, 0:GW].rearrange("p (h q) -> p h q", h=HPG),
                                                                                   in1=b_last(v3(EACS)[:, ch, h0g:h0g + HPG], 64), op=ALU.mult),
                         reads=[pco, EACSc], writes=[yoc])
                    P.op("pe", lambda e, yob=yob, psy=psy, d=d: e.matmul(psy[:, 0:GW], lhsT=IDB[:], rhs=yob, start=False, stop=(d == 1)),
                         reads=[yoc, self.constc], writes=[pcy], inc=(d == 1))
                P.op("dve", lambda e, psy=psy, g=g: e.tensor_tensor(out=YG[:, g * GW:(g + 1) * GW], in0=psy[:, 0:GW], in1=ZTs[:, g * GW:(g + 1) * GW], op=ALU.mult),
                     reads=[pcy, ZTsc], writes=[YGc])
            if self.debug:
                P.dma("sp", self.DBG[ch], YG, reads=[YGc], writes=[P.cell()])
            sq, sqc = self.work()
            ss, ssc = self.work()
            P.op("act", lambda e, sq=sq, ss=ss: e.activation(out=sq[:, 0:DI // 2].bitcast(BF16), in_=YG, func=AF.Square, accum_out=ss[:, 0:1]),
                 reads=[YGc], writes=[sqc, ssc])
            P.op("dve", lambda e, ss=ss: e.tensor_scalar(out=ss[:, 0:1], in0=ss[:, 0:1], scalar1=1.0 / DI, scalar2=EPS, op0=ALU.mult, op1=ALU.add),
                 reads=[ssc], writes=[ssc])
            P.op("act", lambda e, ss=ss: e.activation(out=ss[:, 0:1], in_=ss[:, 0:1], func=AF.Sqrt), reads=[ssc], writes=[ssc])
            P.op("dve", lambda e, ss=ss: e.reciprocal(out=ss[:, 0:1], in_=ss[:, 0:1]), reads=[ssc], writes=[ssc])
            yn, ync = self.work()
            ynb = yn[:, 0:DI // 2].bitcast(BF16)
            P.op("act", lambda e, ynb=ynb, ss=ss: e.activation(out=ynb, in_=YG, func=AF.Identity, scale=ss[:, 0:1]), reads=[YGc, ssc], writes=[ync])
            yt, ytc = self.work()
            ytb = yt[:, 0:XC * 64].bitcast(BF16).rearrange("p (x m) -> p x m", x=XC)
            XB = min(4, XC)
            for x0 in range(0, XC, XB):
                pst, pct = self.psum_bank()
                for xx in range(XB):
                    P.op("pe", lambda e, xx=xx, x0=x0, pst=pst, ynb=ynb: e.matmul(pst[:, xx * 128:(xx + 1) * 128], lhsT=ynb[:, (x0 + xx) * 128:(x0 + xx + 1) * 128],
                                                                               rhs=IDB[:], start=True, stop=True),
                         reads=[ync, self.constc], writes=[pct], inc=(xx == XB - 1))
                P.op("dve", lambda e, x0=x0, pst=pst, ytb=ytb: e.tensor_tensor(out=ytb[:, x0:x0 + XB, :], in0=pst[:, 0:XB * 128].rearrange("p (x m) -> p x m", x=XB),
                                                                            in1=b_last(self.sv("ssd_ng%d" % j)[:, x0:x0 + XB], 128), op=ALU.mult),
                     reads=[pct, self.constc], writes=[ytc])
            P.dma("sp", self.YF[0:XC, :, ch * 128:(ch + 1) * 128].rearrange("x p m -> p x m"), ytb, reads=[ytc], writes=[self.YFc[ch]])
        if getattr(self, 'stop_at', None) == 'pB':
            self.bailed = True
            return
        P.handoff(tlc + bigc, [self.HBc])
        P.handoff([YGc] + xdtc, self.WBc)
        wout = self.W["ssd_w_out"][j]
        for m0 in range(0, c.KC, 2):
            ms = [m for m in (m0, m0 + 1) if m < c.KC]
            pss = [self.psum() for _ in ms]
            wv = [self.load_w(wout, 0, DI, m * 128, 128) for m in ms]
            for kc in range(XC):
                slot = self.hb_slot()
                P.dma("sp", self.HB[:, slot, :], self.YF[kc], reads=[self.YFc], writes=[self.HBsc[slot]])
                for (ps, pc), (view, wc) in zip(pss, wv):
                    for tb in range(c.TB):
                        P.op("pe", lambda e, o=ps[:, tb * c.TBW:(tb + 1) * c.TBW], l=view[:, kc, 0:128],
                             r=self.HB[:, slot, tb * c.TBW:(tb + 1) * c.TBW], st=(kc == 0), sp_=(kc == XC - 1):
                             e.matmul(o, lhsT=l, rhs=r, start=st, stop=sp_),
                             reads=[wc, self.HBsc[slot]], writes=[pc], inc=(tb == c.TB - 1))
            for m, (ps, pc) in zip(ms, pss):
                self.out_epilogue(ps, pc, m, None)
        self.hb_release()
        self.post(i, 0, False)

    def attention(self, i, j):
        P = self.P
        c = self.cfg
        NH, NKV = c.NH, c.NKV
        KVG = NH // NKV
        T, PAST = c.T, c.PAST
        NT, PT = T // 128, PAST // 128
        NTK = NT + PT
        CPS = c.SEG // 128
        SPQ = c.TBW // c.SEG
        wq = self.W["attn_w_qkv"][j]
        zero_col = self.CF[:, 7, 0:1]
        offb = self.sv("flags", 1)
        cacheb = self.sv("flags", 2)
        scale = 128.0 ** -0.5
        self.prenorm(i, 0)
        COS, COSc, SIN, SINc = self.RSTD, self.RSTDc, self.RSTDO, self.RSTDOc
        P.dma("sp", COS[:, 0:T], self.ROPE[0], writes=[COSc])
        P.dma("sp", SIN[:, 0:T], self.ROPE[1], writes=[SINc])

        def norm_rope(ps, pc, gain_col, raw_out=None, raw_cell=None):
            xt, xc = self.work()
            P.op("act", lambda e: e.activation(out=xt[:, 0:T], in_=ps[:, 0:T], func=AF.Identity), reads=[pc], writes=[xc])
            sq, sqc = self.work()
            P.op("act", lambda e: e.activation(out=sq[:, 0:T], in_=xt[:, 0:T], func=AF.Square), reads=[xc], writes=[sqc])
            ps2, pc2 = self.colsum_bcast(sq, sqc)
            P.op("dve", lambda e: e.tensor_scalar(out=sq[:, 0:T], in0=ps2[:, 0:T], scalar1=1.0 / 128, scalar2=EPS, op0=ALU.mult, op1=ALU.add),
                 reads=[pc2], writes=[sqc])
            P.op("act", lambda e: e.activation(out=sq[:, 0:T], in_=sq[:, 0:T], func=AF.Sqrt), reads=[sqc], writes=[sqc])
            P.op("dve", lambda e: e.reciprocal(out=sq[:, 0:T], in_=sq[:, 0:T]), reads=[sqc], writes=[sqc])
            P.op("dve", lambda e: e.scalar_tensor_tensor(out=xt[:, 0:T], in0=xt[:, 0:T], scalar=gain_col, in1=sq[:, 0:T], op0=ALU.mult, op1=ALU.mult),
                 reads=[xc, sqc, self.constc], writes=[xc])
            if raw_out is not None:
                P.dma("sp", raw_out, xt[:, 0:T], reads=[xc], writes=[raw_cell])
            xb, xbc = self.work()
            xbb = xb[:, 0:T // 2].bitcast(BF16)
            P.op("act", lambda e: e.activation(out=xbb, in_=xt[:, 0:T], func=AF.Identity), reads=[xc], writes=[xbc])
            ps3, pc3 = self.psum()
            for tb in range(c.TB):
                P.op("pe", lambda e, tb=tb: e.matmul(ps3[:, tb * c.TBW:(tb + 1) * c.TBW], lhsT=self.ROTB[:], rhs=xbb[:, tb * c.TBW:(tb + 1) * c.TBW],
                                                   start=True, stop=True), reads=[xbc, self.constc], writes=[pc3], inc=(tb == c.TB - 1))
            P.op("dve", lambda e: e.tensor_tensor(out=xt[:, 0:T], in0=xt[:, 0:T], in1=COS[:, 0:T], op=ALU.mult), reads=[xc, COSc], writes=[xc])
            P.op("dve", lambda e: e.tensor_tensor(out=sq[:, 0:T], in0=ps3[:, 0:T], in1=SIN[:, 0:T], op=ALU.mult), reads=[pc3, SINc], writes=[sqc])
            P.op("dve", lambda e: e.tensor_tensor(out=xbb, in0=xt[:, 0:T], in1=sq[:, 0:T], op=ALU.add), reads=[xc, sqc], writes=[xbc])
            return xbb, xbc

        for kv in range(NKV):
            wv_, wc_ = self.load_w(wq, 0, c.D, (NH + kv) * 128, 128)
            ps, pc = self.psum()
            self.matmul_acc(ps, pc, wv_, wc_, 0, c.KC, self.hb_rhs)
            kb, kbc = norm_rope(ps, pc, self.sv("att_kn%d" % j, 0), raw_out=self.NEWK[kv], raw_cell=P.cell())
            P.dma("sp", self.KD[kv][:, 0:T], kb, reads=[kbc], writes=[self.KDc[kv]])
            ck, ckc = self.work()
            P.dma("sp", ck[:, 0:PAST], self.CACHEK[kv], writes=[ckc])
            cb, cbc = self.work()
            cbb = cb[:, 0:PAST // 2].bitcast(BF16)
            P.op("act", lambda e, cbb=cbb, ck=ck: e.activation(out=cbb, in_=ck[:, 0:PAST], func=AF.Identity), reads=[ckc], writes=[cbc])
            P.dma("sp", self.KD[kv][:, T:T + PAST], cbb, reads=[cbc], writes=[self.KDc[kv]])
        VW = NKV * 128
        VH = min(256, VW)
        for v0 in range(0, VW, VH):
            wv_, wc_ = self.load_w(wq, 0, c.D, (NH + NKV) * 128 + v0, VH)
            for tt in range(NT):
                psv, pcv = self.psum_bank()
                for kc in range(c.KC):
                    P.op("pe", lambda e, tt=tt, kc=kc, psv=psv: e.matmul(psv[:, 0:VH], lhsT=self.HB[:, kc, tt * 128:(tt + 1) * 128], rhs=wv_[:, kc, 0:VH],
                                                                      start=(kc == 0), stop=(kc == c.KC - 1)),
                         reads=[self.HBc, wc_], writes=[pcv], inc=(kc == c.KC - 1))
                vt, vtc = self.work()
                P.op("act", lambda e, vt=vt, psv=psv: e.activation(out=vt[:, 0:VH], in_=psv[:, 0:VH], func=AF.Identity), reads=[pcv], writes=[vtc])
                P.dma("sp", self.NEWV[tt * 128:(tt + 1) * 128, v0:v0 + VH], vt[:, 0:VH], reads=[vtc], writes=[P.cell()])
                vb, vbc = self.work()
                vbb = vb[:, 0:VH // 2].bitcast(BF16)
                P.op("dve", lambda e, vbb=vbb, vt=vt: e.tensor_copy(out=vbb, in_=vt[:, 0:VH]), reads=[vtc], writes=[vbc])
                P.dma("sp", self.VD[tt][:, v0:v0 + VH], vbb, reads=[vbc], writes=[self.VDc[tt]])
        for pt in range(PT):
            cv_, cvc = self.work()
            P.dma("sp", cv_[:, 0:VW], self.CACHEV[pt * 128:(pt + 1) * 128, :], writes=[cvc])
            cb, cbc = self.work()
            cbb = cb[:, 0:VW // 2].bitcast(BF16)
            P.op("dve", lambda e, cbb=cbb, cv_=cv_: e.tensor_copy(out=cbb, in_=cv_[:, 0:VW]), reads=[cvc], writes=[cbc])
            P.dma("sp", self.VD[NT + pt], cbb, reads=[cbc], writes=[self.VDc[NT + pt]])
        for h in range(NH):
            if h % 2 == 0:
                wq_ = self.load_w(wq, 0, c.D, h * 128, min(2, NH - h) * 128)
            ps, pc = self.psum()
            self.matmul_acc(ps, pc, wq_[0], wq_[1], (h % 2) * 128, c.KC, self.hb_rhs)
            qb, qbc = norm_rope(ps, pc, self.sv("att_qn%d" % j, 0))
            P.dma("sp", self.QD[h], qb, reads=[qbc], writes=[self.QDc[h]])
        self.hb_release()
        A = self.ARENA
        o_ = 0
        KB = A[:, o_:o_ + NKV * (T + PAST)].rearrange("p (k t) -> p k t", k=NKV); o_ += NKV * (T + PAST)
        VB = A[:, o_:o_ + NTK * VW].rearrange("p (t v) -> p t v", t=NTK); o_ += NTK * VW
        QB = [A[:, o_ + k * T:o_ + (k + 1) * T] for k in range(2)]; o_ += 2 * T
        EB = [A[:, o_ + k * 512:o_ + (k + 1) * 512] for k in range(4)]; o_ += 4 * 512
        AO = [A[:, o_ + k * T:o_ + (k + 1) * T] for k in range(1)]; o_ += T
        KBc, VBc = P.cell(), P.cell()
        QBc = [P.cell() for _ in QB]
        EBc = [P.cell() for _ in EB]
        AOc = [P.cell() for _ in AO]
        P.handoff([self.HBc], [KBc, VBc] + QBc + EBc + AOc)
        for kv in range(NKV):
            P.dma("sp", KB[:, kv, :], self.KD[kv], reads=[self.KDc[kv]], writes=[KBc])
        for tk in range(NTK):
            P.dma("sp", VB[:, tk, :], self.VD[tk], reads=[self.VDc[tk]], writes=[VBc])
        eb_rr = 0
        NQC = T // c.TBW
        for h in range(NH):
            kv = h // KVG
            qb, qbc = QB[h % 2], QBc[h % 2]
            P.dma("sp", qb, self.QD[h], reads=[self.QDc[h]], writes=[qbc])
            ao, aoc = AO[0], AOc[0]
            for qc in range(NQC):
                par = (h * NQC + qc) % 2
                pso, pco = self.PS[1][:, (2 * par) * 512:(2 * par + 1) * 512], self.PSc[1][2 * par]
                psl, pcl = self.PS[1][:, (2 * par + 1) * 512:(2 * par + 2) * 512], self.PSc[1][2 * par + 1]
                for tk in range(NTK):
                    sb_ = tk % 4
                    pss, pcs = self.PS[0][:, sb_ * 512:(sb_ + 1) * 512], self.PSc[0][sb_]
                    P.op("pe", lambda e, tk=tk, pss=pss, qb=qb, qc=qc, kv=kv: e.matmul(pss[:, 0:c.TBW], lhsT=KB[:, kv, tk * 128:(tk + 1) * 128],
                                                                                 rhs=qb[:, qc * c.TBW:(qc + 1) * c.TBW], start=True, stop=True),
                         reads=[KBc, qbc], writes=[pcs])
                    eb, ebc = EB[eb_rr], EBc[eb_rr]
                    eb_rr = (eb_rr + 1) % 4
                    if tk >= NT:
                        segs = [(0, c.TBW, cacheb)]
                    else:
                        sk = tk // CPS
                        segs = []
                        for jh in range(SPQ):
                            bias = zero_col if (qc * SPQ + jh == sk) else offb
                            segs.append((jh * c.SEG, (jh + 1) * c.SEG, bias))
                        merged = [segs[0]]
                        for sgm in segs[1:]:
                            if sgm[2] is merged[-1][2]:
                                merged[-1] = (merged[-1][0], sgm[1], sgm[2])
                            else:
                                merged.append(sgm)
                        segs = merged
                    for (a0, a1, bias) in segs:
                        P.op("act", lambda e, a0=a0, a1=a1, bias=bias, eb=eb, pss=pss: e.activation(out=eb[:, a0:a1], in_=pss[:, a0:a1], func=AF.Exp,
                                                                                           bias=bias, scale=scale),
                             reads=[pcs, self.constc], writes=[ebc])
                    P.op("pe", lambda e, tk=tk, eb=eb, pso=pso, kv=kv: e.matmul(pso[:, 0:c.TBW], lhsT=VB[:, tk, kv * 128:(kv + 1) * 128], rhs=eb[:, 0:c.TBW],
                                                                          start=(tk == 0), stop=(tk == NTK - 1)),
                         reads=[VBc, ebc], writes=[pco], inc=False)
                    P.op("pe", lambda e, tk=tk, eb=eb, psl=psl: e.matmul(psl[:, 0:c.TBW], lhsT=self.ONESB[:], rhs=eb[:, 0:c.TBW],
                                                                      start=(tk == 0), stop=(tk == NTK - 1)),
                         reads=[self.constc, ebc], writes=[pcl])
                rc, rcc = self.work()
                P.op("dve", lambda e, rc=rc, psl=psl: e.reciprocal(out=rc[:, 0:c.TBW], in_=psl[:, 0:c.TBW]), reads=[pcl], writes=[rcc])
                P.op("dve", lambda e, rc=rc, pso=pso, ao=ao, qc=qc: e.tensor_tensor(out=ao[:, qc * c.TBW:(qc + 1) * c.TBW], in0=pso[:, 0:c.TBW], in1=rc[:, 0:c.TBW], op=ALU.mult),
                     reads=[pco, pcl, rcc], writes=[aoc])
            P.dma("sp", self.AOD[h], ao, reads=[aoc], writes=[self.AODc[h]])
        P.handoff([KBc, VBc] + QBc + EBc + AOc, [self.HBc])
        wo = self.W["attn_w_o"][j]
        for m0 in range(0, c.KC, 2):
            ms = [m for m in (m0, m0 + 1) if m < c.KC]
            pss_ = [self.psum() for _ in ms]
            wv2 = [self.load_w(wo, 0, NH * 128, m * 128, 128) for m in ms]
            for kc in range(NH):
                slot = self.hb_slot()
                P.dma("sp", self.HB[:, slot, :], self.AOD[kc], reads=[self.AODc[kc]], writes=[self.HBsc[slot]])
                for (ps, pc), (view, wc) in zip(pss_, wv2):
                    for tb in range(c.TB):
                        P.op("pe", lambda e, o=ps[:, tb * c.TBW:(tb + 1) * c.TBW], l=view[:, kc, 0:128],
                             r=self.HB[:, slot, tb * c.TBW:(tb + 1) * c.TBW], st=(kc == 0), sp_=(kc == NH - 1):
                             e.matmul(o, lhsT=l, rhs=r, start=st, stop=sp_),
                             reads=[wc, self.HBsc[slot]], writes=[pc], inc=(tb == c.TB - 1))
            for m, (ps, pc) in zip(ms, pss_):
                self.out_epilogue(ps, pc, m, None)
        self.hb_release()
        self.post(i, 0, False)

    def build(self):
        c = self.cfg
        nc = bass.Bass("TRN2", target_bir_lowering=False)
        self.nc = nc
        es = ExitStack()
        self.es = es
        P = Prog(nc, es)
        self.P = P

        def din(name, shape):
            return nc.dram_tensor(name, list(shape), F32, kind="ExternalInput").ap()

        def dscr(name, shape, dt):
            return nc.dram_tensor(name, list(shape), dt, kind="Internal").ap()

        self.XIN = din("xin", (c.KC, 128, c.T))
        self.SMALL = din("small", (128, self.NSMALL))
        self.CONSTF = din("constf", (128, 8, 128))
        self.W = {}
        self.W["w_mod"] = din("w_mod", (c.DEPTH, c.D, 6 * c.D))
        self.W["cv_w_pw1"] = din("cv_w_pw1", (c.NCONV, c.D, 2 * c.D))
        self.W["cv_w_pw2"] = din("cv_w_pw2", (c.NCONV, c.D, c.D))
        self.W["ffn_w_up"] = din("ffn_w_up", (c.DEPTH, c.D, 2 * c.DFF))
        self.W["ffn_w_down"] = din("ffn_w_down", (c.DEPTH, c.DFF, c.D))
        NCH, XC, NXBC = c.T // 128, c.DI // 128, c.DI // 128 + 2 * c.SG
        if c.NSSD:
            self.W["ssd_w_in"] = din("ssd_w_in", (c.NSSD, c.D, 2 * c.DI + 2 * c.SG * 128 + 2 * c.SH))
            self.W["ssd_w_out"] = din("ssd_w_out", (c.NSSD, c.DI, c.D))
            self.H0T = din("h0t", (2, 128, c.DI))
            self.NEWST = nc.dram_tensor("newst", [c.NSEG, 2, 128, c.DI], F32, kind="ExternalOutput").ap()
            self.ZT = dscr("ZT", (NCH, 128, c.DI), BF16)
            self.XBC = dscr("XBC", (NXBC, 128, c.T), BF16)
            self.XDT = dscr("XDT", (2, NCH, 128, c.DI), BF16)
            self.CS = dscr("CS", (2, NCH, 128, c.DI), F32)
            self.SIN = dscr("SIN", (2, NCH, 128, c.DI), BF16)
            self.YF = dscr("YF", (XC, 128, c.T), BF16)
            self.ZTc = [[P.cell() for _ in range(c.DI // 256)] for _ in range(NCH)]
            self.XBCc = [P.cell() for _ in range(NXBC)]
            self.XDTc = [[P.cell() for _ in range(NCH)] for _ in range(2)]
            self.CSc = [[[P.cell() for _ in range(c.SG)] for _ in range(NCH)] for _ in range(2)]
            self.SINc = [[[P.cell() for _ in range(2)] for _ in range(NCH)] for _ in range(2)]
            self.YFc = [P.cell() for _ in range(NCH)]
        if c.NATT:
            NT_, PT_ = c.T // 128, c.PAST // 128
            self.W["attn_w_qkv"] = din("attn_w_qkv", (c.NATT, c.D, (c.NH + 2 * c.NKV) * 128))
            self.W["attn_w_o"] = din("attn_w_o", (c.NATT, c.NH * 128, c.D))
            self.ROPE = din("rope", (2, 128, c.T))
            self.CACHEK = din("cachek", (c.NKV, 128, c.PAST))
            self.CACHEV = din("cachev", (c.PAST, c.NKV * 128))
            self.NEWK = nc.dram_tensor("newk", [c.NKV, 128, c.T], F32, kind="ExternalOutput").ap()
            self.NEWV = nc.dram_tensor("newv", [c.T, c.NKV * 128], F32, kind="ExternalOutput").ap()
            self.KD = dscr("KD", (c.NKV, 128, c.T + c.PAST), BF16)
            self.VD = dscr("VD", (NT_ + PT_, 128, c.NKV * 128), BF16)
            self.QD = dscr("QD", (c.NH, 128, c.T), BF16)
            self.AOD = dscr("AOD", (c.NH, 128, c.T), BF16)
            self.KDc = [P.cell() for _ in range(c.NKV)]
            self.VDc = [P.cell() for _ in range(NT_ + PT_)]
            self.QDc = [P.cell() for _ in range(c.NH)]
            self.AODc = [P.cell() for _ in range(c.NH)]
        if self.debug:
            self.DBG = nc.dram_tensor("dbg", [NCH, 128, c.DI], F32, kind="ExternalOutput").ap()
        self.YOUT = nc.dram_tensor("yout", [c.KC, 128, c.T], F32, kind="ExternalOutput").ap()
        self.Y = dscr("Y", (c.KC, 128, c.T), F32)
        self.O = dscr("O", (c.KC, 128, c.T), F32)
        self.CV = dscr("CV", (c.KC, 128, c.T), F32)
        self.ACTD = dscr("ACTD", (c.FC, 128, c.T), BF16)
        self.Yc = [P.cell() for _ in range(c.KC)]
        self.Oc = [P.cell() for _ in range(c.KC)]
        self.CVc = [P.cell() for _ in range(c.KC)]
        self.ACTDc = [P.cell() for _ in range(c.FC)]

        def sb(name, shape, dt):
            return es.enter_context(nc.sbuf_tensor(name, list(shape), dt))

        H = (c.CW - 1) // 2
        NCH_, NDT_ = c.T // 128, 2 * c.SH
        att_el = c.NKV * (c.T + c.PAST) + (c.T // 128 + c.PAST // 128) * c.NKV * 128 + 2 * c.T + 4 * 512 + c.T
        arena_el = max(c.KC * c.T, 10 * NCH_ * NDT_ + 3 * c.DI, att_el if c.NATT else 0)
        self.ARENA = sb("ARENA", (128, arena_el), BF16)
        self.HB = self.ARENA[:, 0:c.KC * c.T].rearrange("p (k t) -> p k t", k=c.KC)
        self.HBc = P.cell()
        self.HBsc = [P.cell() for _ in range(c.KC)]
        self.hb_slot_mode = False
        self.WSLOT = 4096
        self.WBALL = sb("WBALL", (128, 4 * self.WSLOT), BF16)
        self.WB = [self.WBALL[:, k * self.WSLOT:(k + 1) * self.WSLOT] for k in range(4)]
        self.WBc = [P.cell() for _ in self.WB]
        self.wb_rr = 0
        self.WK = [sb("WK%d" % k, (128, c.T + 64), F32) for k in range(5)]
        self.WKc = [P.cell() for _ in self.WK]
        self.wk_rr = 0
        XC_ = c.DI // 128
        upn = c.NSEG * (c.SEG + 2 * H)
        a2_el = max(upn + c.CW * 128, 2 * c.T, 2 * XC_ * 128 + 2 * c.SG * 128)
        self.A2 = sb("A2", (128, a2_el), BF16)
        self.AB = [self.A2[:, k * c.T:(k + 1) * c.T] for k in range(2)]
        self.ABc = [P.cell() for _ in self.AB]
        self.ab_rr = 0
        self.UPF = self.A2[:, 0:upn]
        self.UP = self.UPF.rearrange("p (s w) -> p s w", s=c.NSEG)
        self.UPc = P.cell()
        self.DG = self.A2[:, upn:upn + c.CW * 128].rearrange("p (k m) -> p k m", k=c.CW)
        self.DGc = P.cell()
        self.A2cells = []
        self.GMB = sb("GMB", (128, 256), BF16)
        self.RSTD = sb("RSTD", (128, c.T), F32)
        self.RSTDc = P.cell()
        self.RSTDO = sb("RSTDO", (128, c.T), F32)
        self.RSTDOc = P.cell()
        self.SQY, self.SQYc = self.RSTD, self.RSTDc
        self.SQO, self.SQOc = self.RSTDO, self.RSTDOc
        self.SV = sb("SV", (128, self.NSMALL), F32)
        self.CF = sb("CF", (128, 8, 128), F32)
        self.ONES_F = self.CF[:, 0, :]
        self.IDB = sb("IDB", (128, 128), BF16)
        self.ROTB = sb("ROTB", (128, 128), BF16)
        self.ONESB = sb("ONESB", (128, 128), BF16)
        self.SC = sb("SC", (128, c.KC), BF16)
        self.MOD = sb("MOD", (128, c.DEPTH, 6 * c.KC), F32)
        self.MODc = P.cell()
        self.constc = P.cell()
        self.PS = [es.enter_context(nc.psum_tensor("PS%d" % k, [128, 2048], F32)) for k in range(2)]
        self.PSc = [[Cell(excl=True) for _ in range(4)] for _ in self.PS]
        self.ps_rr = 0
        self.psb_rr = 0

        P.dma("sp", self.SV[:], self.SMALL, writes=[self.constc])
        P.dma("sp", self.CF[:], self.CONSTF, writes=[self.constc])
        P.op("dve", lambda e: e.tensor_copy(out=self.IDB[:], in_=self.CF[:, 1, :]), reads=[self.constc], writes=[self.constc])
        P.op("dve", lambda e: e.tensor_copy(out=self.ROTB[:], in_=self.CF[:, 6, :]), reads=[self.constc], writes=[self.constc])
        P.op("dve", lambda e: e.tensor_copy(out=self.ONESB[:], in_=self.CF[:, 0, :]), reads=[self.constc], writes=[self.constc])
        P.op("act", lambda e: e.activation(out=self.SC[:], in_=self.sv("cond"), func=AF.Silu), reads=[self.constc], writes=[self.constc])
        for kc in range(c.KC):
            xt, xc = self.work()
            P.dma("sp", xt[:, 0:c.T], self.XIN[kc], writes=[xc])
            P.dma("sp", self.Y[kc], xt[:, 0:c.T], reads=[xc], writes=[self.Yc[kc]])
            self.sq_accum(xt[:, 0:c.T], xc, self.SQY, self.SQYc, first=(kc == 0))
        self.rstd_from_sq(self.SQY, self.SQYc, self.RSTD, self.RSTDc, c.D)
        for i in range(self.n_layers):
            self.modulation(i)
        for i in range(self.n_layers):
            kind, j = i % 3, i // 3
            last = (i == self.n_layers - 1)
            if kind == 0:
                self.conformer(i, j)
            elif kind == 1:
                self.ssd(i, j)
            else:
                self.attention(i, j)
            if getattr(self, 'bailed', False):
                break
            self.ffn(i, last)
        P.finish()
        es.close()
        return nc

    @property
    def NSMALL(self):
        return self.sp_layout.n


def const_f():
    cf = np.zeros((128, 8, 128), np.float32)
    cf[:, 0, :] = 1.0
    cf[:, 1, :] = np.eye(128, dtype=np.float32)
    cf[:, 2, :] = np.triu(np.ones((128, 128), np.float32))
    cf[:, 3, :] = np.tril(np.ones((128, 128), np.float32))
    cf[:, 4, :] = np.tril(np.ones((128, 128), np.float32), -1)
    cf[:, 5, :] = np.triu(np.ones((128, 128), np.float32), 1)
    for i_ in range(64):
        cf[2 * i_ + 1, 6, 2 * i_] = -1.0
        cf[2 * i_, 6, 2 * i_ + 1] = 1.0
    return cf


def core_plan(cfg, n_prompt, n_sample):
    plan = [("s", b) for b in range(n_sample)]
    per = cfg.NSEG
    for s0 in range(0, n_prompt, per):
        plan.append(("p", s0))
    return plan


def make_in_maps(cfg, inputs, plan):
    c = cfg
    names = ["w_mod", "cv_w_pw1", "cv_w_pw2", "ffn_w_up", "ffn_w_down"]
    if c.NSSD:
        names += ["ssd_w_in", "ssd_w_out"]
    if c.NATT:
        names += ["attn_w_qkv", "attn_w_o"]
    shared = {k: np.ascontiguousarray(np.asarray(inputs[k], np.float32)) for k in names}
    cf = const_f()
    maps = []
    for kind, idx in plan:
        if kind == "s":
            x = np.asarray(inputs["x_sample"][idx], np.float32)
            cond = np.asarray(inputs["c"][idx], np.float32)
            flags = [1.0, 0.0, 0.0]
        else:
            x = np.asarray(inputs["x_prompt"][idx:idx + c.NSEG], np.float32).reshape(c.T, c.D)
            cond = np.asarray(inputs["c_ctx"], np.float32)
            flags = [0.0, NEG, NEG]
        sp = small_layout(c, inputs, cond=cond, flags=flags)
        m = dict(shared)
        m["xin"] = fm(x)
        if c.NSSD:
            if kind == "s":
                st = np.asarray(inputs["state_ssd"][idx, 0], np.float32)
                m["h0t"] = np.ascontiguousarray(st.reshape(2, c.DI, 128).transpose(0, 2, 1))
            else:
                m["h0t"] = np.zeros((2, 128, c.DI), np.float32)
        if c.NATT:
            rope = np.zeros((2, 128, c.T), np.float32)
            if kind == "s":
                rows = c.T // c.GRID_W
                pos_row = np.repeat(np.arange(rows, dtype=np.float32), c.GRID_W)
                pos_col = np.tile(np.arange(c.GRID_W, dtype=np.float32), rows)
                inv = (np.float32(10000.0) ** (-np.arange(32, dtype=np.float32) / np.float32(32))).astype(np.float32)
                ang = np.concatenate([pos_row[:, None] * inv, pos_col[:, None] * inv], axis=-1).astype(np.float32)
                rope[0] = np.repeat(np.cos(ang).T, 2, axis=0)
                rope[1] = np.repeat(np.sin(ang).T, 2, axis=0)
                ck = np.asarray(inputs["cache_k"][idx, 0], np.float32)
                m["cachek"] = np.ascontiguousarray(ck.transpose(1, 2, 0))
                m["cachev"] = np.ascontiguousarray(np.asarray(inputs["cache_v"][idx, 0], np.float32).reshape(c.PAST, c.NKV * 128))
            else:
                rope[0] = 1.0
                m["cachek"] = np.zeros((c.NKV, 128, c.PAST), np.float32)
                m["cachev"] = np.zeros((c.PAST, c.NKV * 128), np.float32)
            m["rope"] = rope
        m["small"] = sp.build()
        m["constf"] = cf
        maps.append(m)
    return maps


def run(cfg, inputs, n_cores=None, n_layers=None, trace=False, debug=False):
    c = cfg
    n_prompt = inputs["x_prompt"].shape[0]
    n_sample = inputs["x_sample"].shape[0]
    plan = core_plan(c, n_prompt, n_sample)
    n_cores = len(plan) if n_cores is None else n_cores
    maps = make_in_maps(c, inputs, plan)
    in_maps = [maps[k % len(maps)] for k in range(n_cores)]
    b = Builder(c, n_layers=n_layers, debug=debug)
    import os
    if os.environ.get("STOP_AT"):
        b.stop_at = os.environ["STOP_AT"]
    nc = b.build()
    res = run_bass_kernel_spmd(nc, in_maps, core_ids=list(range(n_cores)), trace=trace)
    if debug:
        global DBG_RES
        DBG_RES = res
    if trace:
        print("exec_time_ns", res.exec_time_ns)
    yp = np.zeros((n_prompt, c.SEG, c.D), np.float32)
    ys = np.zeros((n_sample, c.T, c.D), np.float32)
    nst = np.zeros((n_prompt, 1, 2, c.SH, 64, 128), np.float32) if c.NSSD else None
    nk = np.zeros((n_prompt, 1, c.SEG, c.NKV, 128), np.float32) if c.NATT else None
    nv = np.zeros((n_prompt, 1, c.SEG, c.NKV, 128), np.float32) if c.NATT else None
    for k, (kind, idx) in enumerate(plan):
        r = res.results[k]
        y = unfm(np.asarray(r["yout"], np.float32))
        if kind == "s":
            ys[idx] = y
        else:
            yp[idx:idx + c.NSEG] = y.reshape(c.NSEG, c.SEG, c.D)
            if c.NSSD and (n_layers is None or n_layers >= 2):
                ns = np.asarray(r["newst"], np.float32)
                nst[idx:idx + c.NSEG, 0] = ns.transpose(0, 1, 3, 2).reshape(c.NSEG, 2, c.SH, 64, 128)
            if c.NATT and (n_layers is None or n_layers >= 3):
                k_ = np.asarray(r["newk"], np.float32)
                nk[idx:idx + c.NSEG, 0] = k_.transpose(2, 0, 1).reshape(c.NSEG, c.SEG, c.NKV, 128)
                v_ = np.asarray(r["newv"], np.float32)
                nv[idx:idx + c.NSEG, 0] = v_.reshape(c.NSEG, c.SEG, c.NKV, 128)
    return yp, ys, nst, nk, nv


N_CORES = 4


def kernel(**inputs):
    cfg = Cfg()
    inputs = {k: np.asarray(v) for k, v in inputs.items()}
    yp, ys, nst, nk, nv = run(cfg, inputs, n_cores=N_CORES)
    return (yp, ys, nst, nk, nv)
```

```python
import numpy as np
import ml_dtypes
from contextlib import ExitStack
import concourse.bass as bass
import concourse.mybir as mybir
from concourse.bass_utils import run_bass_kernel_spmd

F32 = mybir.dt.float32
BF16 = mybir.dt.bfloat16
AF = mybir.ActivationFunctionType
ALU = mybir.AluOpType
EPS = 1e-6
NEG = -30000.0


class Cfg:
    def __init__(self, **kw):
        self.D = 2048
        self.T = 2048
        self.SEG = 256
        self.DFF = 5632
        self.CW = 31
        self.DEPTH = 4
        self.DI = 4096
        self.SH = 64
        self.SG = 8
        self.SCW = 5
        self.NH = 16
        self.NKV = 4
        self.PAST = 512
        self.GRID_W = 64
        self.NCORES = 8
        for k, v in kw.items():
            setattr(self, k, v)
        self.KC = self.D // 128
        self.FC = self.DFF // 128
        self.NSEG = self.T // self.SEG
        self.TB = max(1, self.T // 512)
        self.TBW = min(512, self.T)
        self.NCONV = (self.DEPTH + 2) // 3
        self.NSSD = (self.DEPTH + 1) // 3
        self.NATT = self.DEPTH // 3


class Cell:
    __slots__ = ("w", "r", "excl")

    def __init__(self, excl=False):
        self.w = None
        self.r = {}
        self.excl = excl


class Agent:
    __slots__ = ("sem", "step", "count")

    def __init__(self, sem, step):
        self.sem = sem
        self.step = step
        self.count = 0


class Prog:
    def __init__(self, nc, es, n_lanes=32):
        self.nc = nc
        self.es = es
        self.eng = {"pe": nc.tensor, "act": nc.scalar, "dve": nc.vector, "pool": nc.gpsimd, "sp": nc.sync}
        self.agents = {}
        for n in self.eng:
            self.agents[n] = Agent(es.enter_context(nc.semaphore("sem_" + n)), 1)
        self.n_lanes = n_lanes
        for i in range(n_lanes):
            self.agents["L%d" % i] = Agent(es.enter_context(nc.semaphore("sem_L%d" % i)), 16)
        self.waited = {n: {} for n in self.eng}
        self.lane_rr = 0
        self.lane_rr_pool = 0
        self.ninstr = 0

    def cell(self):
        return Cell()

    def _wait(self, e, agent, idx):
        if idx <= 0 or self.waited[e].get(agent, 0) >= idx:
            return
        a = self.agents[agent]
        self.eng[e].wait_ge(a.sem, idx * a.step)
        self.waited[e][agent] = idx

    @staticmethod
    def _flat(cells):
        out = []
        for c in cells:
            if isinstance(c, (list, tuple)):
                out.extend(Prog._flat(c))
            else:
                out.append(c)
        return out

    def handoff(self, from_cells, to_cells):
        r = {}
        for c in self._flat(from_cells):
            if c.w is not None and r.get(c.w[0], 0) < c.w[1]:
                r[c.w[0]] = c.w[1]
            for a, i in c.r.items():
                if r.get(a, 0) < i:
                    r[a] = i
        for t in self._flat(to_cells):
            t.w = None
            t.r = dict(r)

    def _deps(self, e, reads, writes, skip_self):
        need = {}
        for c in reads:
            if c.w is not None:
                if need.get(c.w[0], 0) < c.w[1]:
                    need[c.w[0]] = c.w[1]
            if c.excl:
                for a, i in c.r.items():
                    if a != e and need.get(a, 0) < i:
                        need[a] = i
        for c in writes:
            if c.w is not None:
                if need.get(c.w[0], 0) < c.w[1]:
                    need[c.w[0]] = c.w[1]
            for a, i in c.r.items():
                if need.get(a, 0) < i:
                    need[a] = i
        for a, i in need.items():
            if a == e and skip_self:
                continue
            self._wait(e, a, i)

    def op(self, e, fn, reads=(), writes=(), inc=True):
        reads = self._flat(reads)
        writes = self._flat(writes)
        self._deps(e, reads, writes, skip_self=(e == "pe"))
        ins = fn(self.eng[e])
        A = self.agents[e]
        if inc:
            A.count += 1
            ins.then_inc(A.sem, 1)
            idx = A.count
        else:
            idx = A.count + 1
        for c in reads:
            if c.r.get(e, 0) < idx:
                c.r[e] = idx
        for c in writes:
            c.w = (e, idx)
            c.r = {}
        self.ninstr += 1
        return ins

    def dma(self, q, out, in_, reads=(), writes=()):
        if q == "pool":
            lane = "L%d" % (self.n_lanes - 8 + self.lane_rr_pool)
            self.lane_rr_pool = (self.lane_rr_pool + 1) % 8
        else:
            lane = "L%d" % self.lane_rr
            self.lane_rr = (self.lane_rr + 1) % (self.n_lanes - 8)
        L = self.agents[lane]
        reads = self._flat(reads)
        writes = self._flat(writes)
        self._deps(q, reads, writes, skip_self=False)
        self._wait(q, lane, L.count)
        ins = self.eng[q].dma_start(out=out, in_=in_)
        L.count += 1
        ins.then_inc(L.sem, 16)
        for c in reads:
            c.r[lane] = L.count
        for c in writes:
            c.w = (lane, L.count)
            c.r = {}
        self.ninstr += 1
        return ins

    def finish(self):
        for i in range(self.n_lanes):
            L = self.agents["L%d" % i]
            self._wait("sp", "L%d" % i, L.count)
        for n in ("pe", "act", "dve", "pool"):
            self._wait("sp", n, self.agents[n].count)


def pp(v):
    v = np.asarray(v, np.float32)
    return np.ascontiguousarray(v.reshape(-1, 128).T)


def fm(x):
    T, C = x.shape
    return np.ascontiguousarray(x.T.reshape(C // 128, 128, T))


def unfm(y):
    c, p, T = y.shape
    return np.ascontiguousarray(y.reshape(c * p, T).T)


class SmallPack:
    def __init__(self):
        self.off = {}
        self.arrs = []
        self.n = 0

    def add(self, name, arr):
        arr = np.asarray(arr, np.float32)
        assert arr.shape[0] == 128 and arr.ndim == 2, (name, arr.shape)
        self.off[name] = (self.n, arr.shape[1])
        self.arrs.append(arr)
        self.n += arr.shape[1]

    def build(self):
        return np.ascontiguousarray(np.concatenate(self.arrs, axis=1))


def small_layout(cfg, inputs=None, cond=None, flags=None):
    c = cfg
    sp = SmallPack()
    sp.add("cond", pp(cond) if cond is not None else np.zeros((128, c.KC), np.float32))
    fl = np.zeros((128, 8), np.float32)
    if flags is not None:
        fl[:, :len(flags)] = np.asarray(flags, np.float32)[None, :]
    sp.add("flags", fl)

    def g(name, shape):
        if inputs is None:
            return np.zeros(shape, np.float32)
        return np.asarray(inputs[name], np.float32)

    for i in range(c.DEPTH):
        sp.add("b_mod%d" % i, pp(g("b_mod", (c.DEPTH, 6 * c.D))[i]))
        npre = g("norm_pre", (c.DEPTH, 2, c.D))[i]
        npo = g("norm_post", (c.DEPTH, 2, c.D))[i]
        sp.add("npre%d_0" % i, pp(npre[0]))
        sp.add("npre%d_1" % i, pp(npre[1]))
        sp.add("npost%d_0" % i, pp(npo[0]))
        sp.add("npost%d_1" % i, pp(npo[1]))
        sp.add("ffn_b%d" % i, pp(g("ffn_b_dw", (c.DEPTH, 2 * c.DFF))[i]))
        wdw = g("ffn_w_dw", (c.DEPTH, 3, 2 * c.DFF))[i]
        for k in range(3):
            sp.add("ffn_w%d_%d" % (i, k), pp(wdw[k]))
    for j in range(c.NCONV):
        sp.add("cv_b1_%d" % j, pp(g("cv_b_pw1", (c.NCONV, 2 * c.D))[j]))
        wdw = g("cv_w_dw", (c.NCONV, c.CW, c.D))[j]
        for k in range(c.CW):
            sp.add("cv_w%d_%d" % (j, k), pp(wdw[k]))
        sp.add("cv_bdw_%d" % j, pp(g("cv_b_dw", (c.NCONV, c.D))[j]))
        sp.add("cv_lng_%d" % j, pp(g("cv_ln_g", (c.NCONV, c.D))[j]))
        sp.add("cv_lnb_%d" % j, pp(g("cv_ln_b", (c.NCONV, c.D))[j]))
        sp.add("cv_b2_%d" % j, pp(g("cv_b_pw2", (c.NCONV, c.D))[j]))
    for j in range(c.NATT):
        sp.add("att_qn%d" % j, g("attn_q_norm", (c.NATT, 128))[j].reshape(128, 1))
        sp.add("att_kn%d" % j, g("attn_k_norm", (c.NATT, 128))[j].reshape(128, 1))
    ncd = c.DI + 2 * c.SG * 128
    for j in range(c.NSSD):
        wc = g("ssd_w_conv", (c.NSSD, c.SCW, ncd))[j]
        for k in range(c.SCW):
            sp.add("ssd_cw%d_%d" % (j, k), pp(wc[k]))
        sp.add("ssd_cb%d" % j, pp(g("ssd_b_conv", (c.NSSD, ncd))[j]))
        sp.add("ssd_ng%d" % j, pp(g("ssd_norm_g", (c.NSSD, c.DI))[j]))
        sp.add("ssd_dch%d" % j, pp(np.repeat(g("ssd_d", (c.NSSD, c.SH))[j], 64)))
        sp.add("ssd_dtb%d" % j, np.broadcast_to(g("ssd_dt_bias", (c.NSSD, 2, c.SH))[j].reshape(1, -1), (128, 2 * c.SH)))
        sp.add("ssd_alog%d" % j, np.broadcast_to(g("ssd_a_log", (c.NSSD, 2, c.SH))[j].reshape(1, -1), (128, 2 * c.SH)))
    return sp


class Builder:
    def __init__(self, cfg, debug=False, n_layers=None):
        self.cfg = cfg
        self.debug = debug
        self.n_layers = cfg.DEPTH if n_layers is None else n_layers
        self.sp_layout = small_layout(cfg)

    def sv(self, name, col=None, n=1):
        o, w = self.sp_layout.off[name]
        if col is None:
            return self.SV[:, o:o + w]
        return self.SV[:, o + col:o + col + n]

    def work(self):
        i = self.wk_rr
        self.wk_rr = (self.wk_rr + 1) % len(self.WK)
        return self.WK[i], self.WKc[i]

    def wslot(self):
        i = self.wb_rr
        self.wb_rr = (self.wb_rr + 1) % len(self.WB)
        return self.WB[i], self.WBc[i]

    def psum(self):
        i = self.ps_rr
        self.ps_rr = (self.ps_rr + 1) % 2
        return self.PS[i], self.PSc[i]

    def psum_bank(self):
        c = self.cfg
        k = self.psb_rr
        self.psb_rr = (self.psb_rr + 1) % 8
        i, b = k // 4, k % 4
        return self.PS[i][:, b * 512:(b + 1) * 512], self.PSc[i][b]

    def load_w(self, w_ap, r0, nrows, c0, ncols):
        P = self.P
        slot, cell = self.wslot()
        kcn = nrows // 128
        assert kcn * ncols <= self.WSLOT
        view = slot[:, 0:kcn * ncols].rearrange("p (k m) -> p k m", k=kcn)
        src = w_ap[r0:r0 + nrows, c0:c0 + ncols].rearrange("(k p) m -> p k m", p=128)
        P.dma("pool", view, src, writes=[cell])
        return view, cell

    def matmul_acc(self, ps, pscell, wview, wcell, mcol, kcn, rhs_fn, first=True, last=True, kc0=0, ktot=None):
        P = self.P
        c = self.cfg
        ktot = kcn if ktot is None else ktot
        for kc in range(kcn):
            for tb in range(c.TB):
                rhs, rcells = rhs_fn(kc, tb)
                st = first and (kc0 + kc == 0)
                sp_ = last and (kc0 + kc == ktot - 1)
                fin = (kc == kcn - 1) and (tb == c.TB - 1)
                P.op("pe",
                     lambda e, o=ps[:, tb * c.TBW:(tb + 1) * c.TBW], l=wview[:, kc, mcol:mcol + 128], r=rhs, st=st, sp_=sp_:
                     e.matmul(o, lhsT=l, rhs=r, start=st, stop=sp_),
                     reads=[wcell] + list(rcells), writes=[pscell], inc=fin)

    def hb_rhs(self, kc, tb):
        c = self.cfg
        return self.HB[:, kc, tb * c.TBW:(tb + 1) * c.TBW], [self.HBc]

    def colsum_bcast(self, acc, acc_cell):
        P = self.P
        c = self.cfg
        ps, pc = self.psum()
        for tb in range(c.TB):
            P.op("pe", lambda e, o=ps[:, tb * c.TBW:(tb + 1) * c.TBW], r=acc[:, tb * c.TBW:(tb + 1) * c.TBW]:
                 e.matmul(o, lhsT=self.ONES_F[:], rhs=r, start=True, stop=True),
                 reads=[acc_cell, self.constc], writes=[pc], inc=(tb == c.TB - 1))
        return ps, pc

    def rstd_from_sq(self, acc, acc_cell, out, out_cell, n):
        P = self.P
        c = self.cfg
        ps, pc = self.colsum_bcast(acc, acc_cell)
        P.op("dve", lambda e: e.tensor_scalar(out=out[:, 0:c.T], in0=ps[:, 0:c.T], scalar1=1.0 / n, scalar2=EPS,
                                              op0=ALU.mult, op1=ALU.add), reads=[pc], writes=[out_cell])
        P.op("act", lambda e: e.activation(out=out[:, 0:c.T], in_=out[:, 0:c.T], func=AF.Sqrt),
             reads=[out_cell], writes=[out_cell])
        P.op("dve", lambda e: e.reciprocal(out=out[:, 0:c.T], in_=out[:, 0:c.T]), reads=[out_cell], writes=[out_cell])

    def sq_accum(self, t, tcell, acc, acc_cell, first):
        P = self.P
        c = self.cfg
        if first:
            P.op("act", lambda e: e.activation(out=acc[:, 0:c.T], in_=t, func=AF.Square), reads=[tcell], writes=[acc_cell])
        else:
            sq, sqc = self.work()
            P.op("act", lambda e: e.activation(out=sq[:, 0:c.T], in_=t, func=AF.Square), reads=[tcell], writes=[sqc])
            P.op("dve", lambda e: e.tensor_tensor(out=acc[:, 0:c.T], in0=acc[:, 0:c.T], in1=sq[:, 0:c.T], op=ALU.add),
                 reads=[sqc, acc_cell], writes=[acc_cell])

    def modulation(self, i):
        P = self.P
        c = self.cfg
        w = self.W["w_mod"][i]
        ps, pc = self.psum()
        noc = 6 * c.KC
        cw = min(256, 6 * c.D)
        for cb in range(6 * c.D // cw):
            view, wc = self.load_w(w, 0, c.D, cb * cw, cw)
            for ml in range(cw // 128):
                oc = cb * (cw // 128) + ml
                for kc in range(c.KC):
                    P.op("pe", lambda e, o=ps[:, oc:oc + 1], l=view[:, kc, ml * 128:(ml + 1) * 128], r=self.SC[:, kc:kc + 1],
                         st=(kc == 0), sp_=(kc == c.KC - 1): e.matmul(o, lhsT=l, rhs=r, start=st, stop=sp_),
                         reads=[wc, self.constc], writes=[pc], inc=(kc == c.KC - 1))
        mod = self.MOD[:, i, :]
        P.op("dve", lambda e: e.tensor_tensor(out=mod, in0=ps[:, 0:noc], in1=self.sv("b_mod%d" % i), op=ALU.add),
             reads=[pc, self.constc], writes=[self.MODc])
        for s in range(2):
            sc = self.MOD[:, i, (3 * s + 1) * c.KC:(3 * s + 2) * c.KC]
            ga = self.MOD[:, i, (3 * s + 2) * c.KC:(3 * s + 3) * c.KC]
            P.op("dve", lambda e, sc=sc, s=s: e.scalar_tensor_tensor(out=sc, in0=sc, scalar=1.0, in1=self.sv("npre%d_%d" % (i, s)),
                                                                   op0=ALU.add, op1=ALU.mult),
                 reads=[self.MODc, self.constc], writes=[self.MODc])
            P.op("dve", lambda e, ga=ga, s=s: e.tensor_tensor(out=ga, in0=ga, in1=self.sv("npost%d_%d" % (i, s)), op=ALU.mult),
                 reads=[self.MODc, self.constc], writes=[self.MODc])

    def modv(self, i, which, col):
        c = self.cfg
        idx = {"sh_m": 0, "sc_m": 1, "ga_m": 2, "sh_f": 3, "sc_f": 4, "ga_f": 5}[which]
        return self.MOD[:, i, idx * c.KC + col: idx * c.KC + col + 1]

    def prenorm(self, i, s):
        P = self.P
        c = self.cfg
        sc = "sc_m" if s == 0 else "sc_f"
        sh = "sh_m" if s == 0 else "sh_f"
        for kc in range(c.KC):
            yt, yc = self.work()
            P.dma("sp", yt[:, 0:c.T], self.Y[kc], reads=[self.Yc[kc]], writes=[yc])
            P.op("dve", lambda e, yt=yt, kc=kc: e.scalar_tensor_tensor(out=yt[:, 0:c.T], in0=yt[:, 0:c.T], scalar=self.modv(i, sc, kc),
                                                                     in1=self.RSTD[:, 0:c.T], op0=ALU.mult, op1=ALU.mult),
                 reads=[yc, self.MODc, self.RSTDc], writes=[yc])
            P.op("act", lambda e, yt=yt, kc=kc: e.activation(out=self.HB[:, kc, :], in_=yt[:, 0:c.T], func=AF.Identity,
                                                            bias=self.modv(i, sh, kc), scale=1.0),
                 reads=[yc, self.MODc], writes=[self.HBc])

    def post(self, i, s, last):
        P = self.P
        c = self.cfg
        ga = "ga_m" if s == 0 else "ga_f"
        self.rstd_from_sq(self.SQO, self.SQOc, self.RSTDO, self.RSTDOc, c.D)
        for kc in range(c.KC):
            ot, oc = self.work()
            yt, yc = self.work()
            P.dma("sp", ot[:, 0:c.T], self.O[kc], reads=[self.Oc[kc]], writes=[oc])
            P.dma("sp", yt[:, 0:c.T], self.Y[kc], reads=[self.Yc[kc]], writes=[yc])
            P.op("dve", lambda e, ot=ot, kc=kc: e.scalar_tensor_tensor(out=ot[:, 0:c.T], in0=ot[:, 0:c.T], scalar=self.modv(i, ga, kc),
                                                                     in1=self.RSTDO[:, 0:c.T], op0=ALU.mult, op1=ALU.mult),
                 reads=[oc, self.MODc, self.RSTDOc], writes=[oc])
            P.op("dve", lambda e, ot=ot, yt=yt: e.tensor_tensor(out=yt[:, 0:c.T], in0=yt[:, 0:c.T], in1=ot[:, 0:c.T], op=ALU.add),
                 reads=[oc, yc], writes=[yc])
            dst = self.YOUT[kc] if last else self.Y[kc]
            P.dma("sp", dst, yt[:, 0:c.T], reads=[yc], writes=[self.Yc[kc]])
            if not last:
                self.sq_accum(yt[:, 0:c.T], yc, self.SQY, self.SQYc, first=(kc == 0))
        if not last:
            self.rstd_from_sq(self.SQY, self.SQYc, self.RSTD, self.RSTDc, c.D)

    def out_epilogue(self, ps, pc, m, bias_ap):
        P = self.P
        c = self.cfg
        ot, oc = self.work()
        if bias_ap is None:
            P.op("act", lambda e: e.activation(out=ot[:, 0:c.T], in_=ps[:, 0:c.T], func=AF.Identity), reads=[pc], writes=[oc])
        else:
            P.op("act", lambda e: e.activation(out=ot[:, 0:c.T], in_=ps[:, 0:c.T], func=AF.Identity, bias=bias_ap, scale=1.0),
                 reads=[pc, self.constc], writes=[oc])
        P.dma("sp", self.O[m], ot[:, 0:c.T], reads=[oc], writes=[self.Oc[m]])
        self.sq_accum(ot[:, 0:c.T], oc, self.SQO, self.SQOc, first=(m == 0))

    def halo_fill(self, eng, buf, cell, h):
        P = self.P
        c = self.cfg
        S = c.SEG
        n = c.NSEG
        fS = self.sv("flags", 0)
        if n > 1:
            P.op(eng, lambda e: e.tensor_scalar(out=buf[:, 1:n, 0:h], in0=buf[:, 0:n - 1, S:S + h], scalar1=fS, scalar2=None, op0=ALU.mult),
                 reads=[cell, self.constc], writes=[cell])
            P.op(eng, lambda e: e.tensor_scalar(out=buf[:, 0:n - 1, S + h:S + 2 * h], in0=buf[:, 1:n, h:2 * h], scalar1=fS, scalar2=None, op0=ALU.mult),
                 reads=[cell, self.constc], writes=[cell])

    def ffn(self, i, last):
        P = self.P
        c = self.cfg
        S = c.SEG
        n = c.NSEG
        self.prenorm(i, 1)
        self.a2_switch(self.ABc)
        wup = self.W["ffn_w_up"][i]
        PW = 2
        views = {}
        for m in range(c.FC):
            if m % PW == 0:
                npair = min(PW, c.FC - m)
                va = self.load_w(wup, 0, c.D, m * 128, npair * 128)
                vg = self.load_w(wup, 0, c.D, c.DFF + m * 128, npair * 128)
            ml = m % PW
            res = []
            for half, (view, wc) in enumerate((va, vg)):
                ps, pc = self.psum()
                self.matmul_acc(ps, pc, view, wc, ml * 128, c.KC, self.hb_rhs)
                ch = half * c.FC + m
                pad, padc = self.work()
                pv = pad[:, 0:n * (S + 2)].rearrange("p (s w) -> p s w", s=n)
                P.op("act", lambda e, pv=pv, ps=ps: e.activation(out=pv[:, :, 1:S + 1], in_=ps[:, 0:c.T].rearrange("p (s w) -> p s w", s=n),
                                                               func=AF.Identity), reads=[pc], writes=[padc])
                P.op("dve", lambda e, pv=pv: e.memset(pv[:, 0, 0:1], 0.0), reads=[padc], writes=[padc])
                P.op("dve", lambda e, pv=pv: e.memset(pv[:, n - 1, S + 1:S + 2], 0.0), reads=[padc], writes=[padc])
                self.halo_fill("dve", pv, padc, 1)
                cv, cvc = self.work()
                cvv = cv[:, 0:c.T].rearrange("p (s w) -> p s w", s=n)
                P.op("dve", lambda e, pv=pv, cvv=cvv, ch=ch: e.tensor_scalar(out=cvv, in0=pv[:, :, 0:S], scalar1=self.sv("ffn_w%d_0" % i, ch),
                                                                           scalar2=self.sv("ffn_b%d" % i, ch), op0=ALU.mult, op1=ALU.add),
                     reads=[padc, self.constc], writes=[cvc])
                for k in (1, 2):
                    P.op("dve", lambda e, pv=pv, cvv=cvv, ch=ch, k=k: e.scalar_tensor_tensor(out=cvv, in0=pv[:, :, k:k + S],
                                                                                           scalar=self.sv("ffn_w%d_%d" % (i, k), ch),
                                                                                           in1=cvv, op0=ALU.mult, op1=ALU.add),
                         reads=[padc, cvc, self.constc], writes=[cvc])
                res.append((cv, cvc))
            (ca, cac), (cg, cgc) = res
            P.op("act", lambda e, cg=cg: e.activation(out=cg[:, 0:c.T], in_=cg[:, 0:c.T], func=AF.Silu), reads=[cgc], writes=[cgc])
            ab, abc = self.abuf()
            P.op("dve", lambda e, ca=ca, cg=cg, ab=ab: e.tensor_tensor(out=ab[:, 0:c.T], in0=ca[:, 0:c.T], in1=cg[:, 0:c.T], op=ALU.mult),
                 reads=[cac, cgc], writes=[abc])
            P.dma("sp", self.ACTD[m], ab[:, 0:c.T], reads=[abc], writes=[self.ACTDc[m]])
        wdn = self.W["ffn_w_down"][i]
        KG = 11 if c.FC % 11 == 0 else c.FC
        for m0 in range(0, c.KC, 2):
            ms = [m for m in (m0, m0 + 1) if m < c.KC]
            pss = [self.psum() for _ in ms]
            for g0 in range(0, c.FC, KG):
                wv = [self.load_w(wdn, g0 * 128, KG * 128, m * 128, 128) for m in ms]
                for kk in range(KG):
                    kc = g0 + kk
                    slot = self.hb_slot()
                    P.dma("sp", self.HB[:, slot, :], self.ACTD[kc], reads=[self.ACTDc[kc]], writes=[self.HBsc[slot]])
                    for (ps, pc), (view, wc) in zip(pss, wv):
                        for tb in range(c.TB):
                            P.op("pe", lambda e, o=ps[:, tb * c.TBW:(tb + 1) * c.TBW], l=view[:, kk, 0:128],
                                 r=self.HB[:, slot, tb * c.TBW:(tb + 1) * c.TBW], st=(kc == 0), sp_=(kc == c.FC - 1):
                                 e.matmul(o, lhsT=l, rhs=r, start=st, stop=sp_),
                                 reads=[wc, self.HBsc[slot]], writes=[pc], inc=(tb == c.TB - 1))
            for m, (ps, pc) in zip(ms, pss):
                self.out_epilogue(ps, pc, m, None)
        self.hb_release()
        self.post(i, 1, last)

    def hb_slot(self):
        if not self.hb_slot_mode:
            for k in range(self.cfg.KC):
                self.HBsc[k].w = self.HBc.w
                self.HBsc[k].r = dict(self.HBc.r)
            self.hb_slot_mode = True
            self.hb_rr = 0
        s = self.hb_rr
        self.hb_rr = (self.hb_rr + 1) % self.cfg.KC
        return s

    def hb_release(self):
        if self.hb_slot_mode:
            r = {}
            w = self.HBc.w
            for k in range(self.cfg.KC):
                cl = self.HBsc[k]
                for a, i in cl.r.items():
                    if r.get(a, 0) < i:
                        r[a] = i
                if cl.w is not None:
                    if r.get(cl.w[0], 0) < cl.w[1]:
                        r[cl.w[0]] = cl.w[1]
            self.HBc.r = r
            self.hb_slot_mode = False

    def a2_switch(self, new_cells):
        self.P.handoff(self.A2cells, new_cells)
        self.A2cells = list(new_cells)

    def a2_conv_view(self):
        self.a2_switch([self.UPc, self.DGc])
        self.P.op("dve", lambda e: e.memset(self.UPF, 0.0), writes=[self.UPc])

    def abuf(self):
        i = self.ab_rr
        self.ab_rr = (self.ab_rr + 1) % len(self.AB)
        return self.AB[i], self.ABc[i]

    def conformer(self, i, j):
        P = self.P
        c = self.cfg
        S = c.SEG
        n = c.NSEG
        H = (c.CW - 1) // 2
        PADW = S + 2 * H
        self.prenorm(i, 0)
        self.a2_conv_view()
        w1 = self.W["cv_w_pw1"][j]
        PW = 2
        for m in range(c.KC):
            if m % PW == 0:
                npair = min(PW, c.KC - m)
                va = self.load_w(w1, 0, c.D, m * 128, npair * 128)
                vg = self.load_w(w1, 0, c.D, c.D + m * 128, npair * 128)
            ml = m % PW
            psa, pca = self.psum()
            self.matmul_acc(psa, pca, va[0], va[1], ml * 128, c.KC, self.hb_rhs)
            psg, pcg = self.psum()
            self.matmul_acc(psg, pcg, vg[0], vg[1], ml * 128, c.KC, self.hb_rhs)
            at, atc = self.work()
            gt, gtc = self.work()
            P.op("act", lambda e, at=at, psa=psa, m=m: e.activation(out=at[:, 0:c.T], in_=psa[:, 0:c.T], func=AF.Identity,
                                                                  bias=self.sv("cv_b1_%d" % j, m), scale=1.0),
                 reads=[pca, self.constc], writes=[atc])
            P.op("act", lambda e, gt=gt, psg=psg, m=m: e.activation(out=gt[:, 0:c.T], in_=psg[:, 0:c.T], func=AF.Sigmoid,
                                                                  bias=self.sv("cv_b1_%d" % j, c.KC + m), scale=1.0),
                 reads=[pcg, self.constc], writes=[gtc])
            up = self.UP
            P.op("dve", lambda e, at=at, gt=gt: e.tensor_tensor(out=up[:, :, H:H + S], in0=at[:, 0:c.T].rearrange("p (s w) -> p s w", s=n),
                                                              in1=gt[:, 0:c.T].rearrange("p (s w) -> p s w", s=n), op=ALU.mult),
                 reads=[atc, gtc], writes=[self.UPc])
            self.halo_fill("dve", up, self.UPc, H)
            for k in range(c.CW):
                P.op("dve", lambda e, k=k, m=m: e.tensor_scalar(out=self.DG[:, k, :], in0=self.IDB[:], scalar1=self.sv("cv_w%d_%d" % (j, k), m),
                                                              scalar2=None, op0=ALU.mult),
                     reads=[self.constc, self.DGc], writes=[self.DGc])
            psc, pcc = self.psum()
            for s_ in range(n):
                for k in range(c.CW):
                    P.op("pe", lambda e, s_=s_, k=k, psc=psc: e.matmul(psc[:, s_ * S:(s_ + 1) * S], lhsT=self.DG[:, k, :], rhs=up[:, s_, k:k + S],
                                                                    start=(k == 0), stop=(k == c.CW - 1)),
                         reads=[self.DGc, self.UPc], writes=[pcc], inc=(s_ == n - 1 and k == c.CW - 1))
            ct, ctc = self.work()
            P.op("act", lambda e, ct=ct, psc=psc, m=m: e.activation(out=ct[:, 0:c.T], in_=psc[:, 0:c.T], func=AF.Identity,
                                                                  bias=self.sv("cv_bdw_%d" % j, m), scale=1.0),
                 reads=[pcc, self.constc], writes=[ctc])
            P.dma("sp", self.CV[m], ct[:, 0:c.T], reads=[ctc], writes=[self.CVc[m]])
            if m == 0:
                P.op("dve", lambda e, ct=ct: e.tensor_copy(out=self.SQY[:, 0:c.T], in_=ct[:, 0:c.T]), reads=[ctc], writes=[self.SQYc])
            else:
                P.op("dve", lambda e, ct=ct: e.tensor_tensor(out=self.SQY[:, 0:c.T], in0=self.SQY[:, 0:c.T], in1=ct[:, 0:c.T], op=ALU.add),
                     reads=[ctc, self.SQYc], writes=[self.SQYc])
            self.sq_accum(ct[:, 0:c.T], ctc, self.SQO, self.SQOc, first=(m == 0))
        psm, pcm = self.colsum_bcast(self.SQY, self.SQYc)
        mean, meanc = self.SQY, self.SQYc
        P.op("act", lambda e: e.activation(out=mean[:, 0:c.T], in_=psm[:, 0:c.T], func=AF.Identity, scale=1.0 / c.D),
             reads=[pcm], writes=[meanc])
        psq, pcq = self.colsum_bcast(self.SQO, self.SQOc)
        var, varc = self.SQO, self.SQOc
        msq, msqc = self.work()
        P.op("dve", lambda e: e.tensor_tensor(out=msq[:, 0:c.T], in0=mean[:, 0:c.T], in1=mean[:, 0:c.T], op=ALU.mult),
             reads=[meanc], writes=[msqc])
        P.op("dve", lambda e: e.scalar_tensor_tensor(out=var[:, 0:c.T], in0=psq[:, 0:c.T], scalar=1.0 / c.D, in1=msq[:, 0:c.T],
                                                     op0=ALU.mult, op1=ALU.subtract), reads=[pcq, msqc], writes=[varc])
        P.op("dve", lambda e: e.tensor_scalar(out=var[:, 0:c.T], in0=var[:, 0:c.T], scalar1=EPS, scalar2=None, op0=ALU.add),
             reads=[varc], writes=[varc])
        P.op("act", lambda e: e.activation(out=var[:, 0:c.T], in_=var[:, 0:c.T], func=AF.Sqrt), reads=[varc], writes=[varc])
        P.op("dve", lambda e: e.reciprocal(out=var[:, 0:c.T], in_=var[:, 0:c.T]), reads=[varc], writes=[varc])
        rln = var
        for m in range(c.KC):
            ct, ctc = self.work()
            P.dma("sp", ct[:, 0:c.T], self.CV[m], reads=[self.CVc[m]], writes=[ctc])
            P.op("dve", lambda e, ct=ct: e.tensor_tensor(out=ct[:, 0:c.T], in0=ct[:, 0:c.T], in1=mean[:, 0:c.T], op=ALU.subtract),
                 reads=[ctc, meanc], writes=[ctc])
            P.op("dve", lambda e, ct=ct: e.tensor_tensor(out=ct[:, 0:c.T], in0=ct[:, 0:c.T], in1=rln[:, 0:c.T], op=ALU.mult),
                 reads=[ctc, varc], writes=[ctc])
            P.op("act", lambda e, ct=ct, m=m: e.activation(out=self.HB[:, m, :], in_=ct[:, 0:c.T], func=AF.Silu,
                                                         bias=self.sv("cv_lnb_%d" % j, m), scale=self.sv("cv_lng_%d" % j, m)),
                 reads=[ctc, self.constc], writes=[self.HBc])
        w2 = self.W["cv_w_pw2"][j]
        CWD = min(256, c.D)
        for m in range(c.KC):
            if (m * 128) % CWD == 0:
                v2 = self.load_w(w2, 0, c.D, m * 128, CWD)
            ps, pc = self.psum()
            self.matmul_acc(ps, pc, v2[0], v2[1], (m * 128) % CWD, c.KC, self.hb_rhs)
            self.out_epilogue(ps, pc, m, self.sv("cv_b2_%d" % j, m))
        self.post(i, 0, False)

    def ssd(self, i, j):
        P = self.P
        c = self.cfg
        S = c.SEG
        n = c.NSEG
        DI, SH, SG = c.DI, c.SH, c.SG
        HPG = SH // SG
        GW = HPG * 64
        XC = DI // 128
        XPG = GW // 128
        NCH = c.T // 128
        NDT = 2 * SH
        NF = NCH * NDT
        NXBC = XC + 2 * SG
        CPS = S // 128
        HBAT = min(4, HPG)
        w_in = self.W["ssd_w_in"][j]
        off_x = DI
        off_dt = 2 * DI + 2 * SG * 128
        fS = self.sv("flags", 0)
        IDB = self.IDB
        U, LW, SL, SU = (self.CF[:, k, :] for k in (2, 3, 4, 5))

        def b_last(ap, k):
            return ap.unsqueeze(2).to_broadcast([128, ap.shape[1], k])

        def b_mid(ap, k):
            return ap.unsqueeze(1).to_broadcast([128, k, ap.shape[1]])

        self.prenorm(i, 0)
        self.a2_conv_view()
        wdt, wdtc = self.load_w(w_in, 0, c.D, off_dt, NDT)
        psd, pcd = self.psum()
        for ch in range(NCH):
            for kc in range(c.KC):
                P.op("pe", lambda e, ch=ch, kc=kc: e.matmul(psd[:, ch * NDT:(ch + 1) * NDT], lhsT=self.HB[:, kc, ch * 128:(ch + 1) * 128],
                                                          rhs=wdt[:, kc, 0:NDT], start=(kc == 0), stop=(kc == c.KC - 1)),
                     reads=[self.HBc, wdtc], writes=[pcd], inc=(kc == c.KC - 1))
        dtw, dtwc = self.RSTDO, self.RSTDOc
        P.op("dve", lambda e: e.tensor_tensor(out=dtw[:, 0:NF].rearrange("p (c h) -> p c h", c=NCH),
                                              in0=psd[:, 0:NF].rearrange("p (c h) -> p c h", c=NCH),
                                              in1=b_mid(self.sv("ssd_dtb%d" % j), NCH), op=ALU.add),
             reads=[pcd, self.constc], writes=[dtwc])
        P.op("act", lambda e: e.activation(out=dtw[:, 0:NF], in_=dtw[:, 0:NF], func=AF.Exp), reads=[dtwc], writes=[dtwc])
        P.op("act", lambda e: e.activation(out=dtw[:, 0:NF], in_=dtw[:, 0:NF], func=AF.Ln, bias=self.CF[:, 0, 0:1], scale=1.0),
             reads=[dtwc, self.constc], writes=[dtwc])
        if getattr(self, 'stop_at', None) == 'p0a':
            self.bailed = True
            return
        CPT = c.T // 256
        for cb in range(DI // 256):
            wz, wzc = self.load_w(w_in, 0, c.D, cb * 256, 256)
            for c0 in range(0, NCH, CPT):
                ps, pc = self.psum()
                for cc in range(CPT):
                    ch = c0 + cc
                    for kc in range(c.KC):
                        P.op("pe", lambda e, ch=ch, cc=cc, kc=kc, ps=ps: e.matmul(ps[:, cc * 256:(cc + 1) * 256],
                                                                                lhsT=self.HB[:, kc, ch * 128:(ch + 1) * 128],
                                                                                rhs=wz[:, kc, 0:256], start=(kc == 0), stop=(kc == c.KC - 1)),
                             reads=[self.HBc, wzc], writes=[pc], inc=(kc == c.KC - 1))
                zt, ztc = self.work()
                ztb = zt[:, 0:c.T // 2].bitcast(BF16)
                P.op("act", lambda e, ztb=ztb, ps=ps: e.activation(out=ztb, in_=ps[:, 0:c.T], func=AF.Silu), reads=[pc], writes=[ztc])
                P.dma("sp", self.ZT[c0:c0 + CPT, :, cb * 256:(cb + 1) * 256].rearrange("c p m -> p c m"),
                      ztb.rearrange("p (c m) -> p c m", c=CPT), reads=[ztc], writes=[[self.ZTc[c0 + q][cb] for q in range(CPT)]])
        if getattr(self, 'stop_at', None) == 'p0b':
            self.bailed = True
            return
        H = (c.SCW - 1) // 2
        up5 = self.UPF[:, 0:n * (S + 2 * H)].rearrange("p (s w) -> p s w", s=n)
        for m in range(NXBC):
            if m % 2 == 0:
                wx = self.load_w(w_in, 0, c.D, off_x + m * 128, min(2, NXBC - m) * 128)
            ps, pc = self.psum()
            self.matmul_acc(ps, pc, wx[0], wx[1], (m % 2) * 128, c.KC, self.hb_rhs)
            P.op("act", lambda e, ps=ps: e.activation(out=up5[:, :, H:H + S], in_=ps[:, 0:c.T].rearrange("p (s w) -> p s w", s=n),
                                                    func=AF.Identity), reads=[pc], writes=[self.UPc])
            self.halo_fill("dve", up5, self.UPc, H)
            for k in range(c.SCW):
                P.op("dve", lambda e, k=k, m=m: e.tensor_scalar(out=self.DG[:, k, :], in0=IDB[:], scalar1=self.sv("ssd_cw%d_%d" % (j, k), m),
                                                              scalar2=None, op0=ALU.mult),
                     reads=[self.constc, self.DGc], writes=[self.DGc])
            ps2, pc2 = self.psum()
            for s_ in range(n):
                for k in range(c.SCW):
                    P.op("pe", lambda e, s_=s_, k=k, ps2=ps2: e.matmul(ps2[:, s_ * S:(s_ + 1) * S], lhsT=self.DG[:, k, :], rhs=up5[:, s_, k:k + S],
                                                                    start=(k == 0), stop=(k == c.SCW - 1)),
                         reads=[self.DGc, self.UPc], writes=[pc2], inc=(s_ == n - 1 and k == c.SCW - 1))
            xt, xtc = self.work()
            xtb = xt[:, 0:c.T // 2].bitcast(BF16)
            P.op("act", lambda e, xtb=xtb, ps2=ps2, m=m: e.activation(out=xtb, in_=ps2[:, 0:c.T], func=AF.Silu,
                                                                    bias=self.sv("ssd_cb%d" % j, m), scale=1.0),
                 reads=[pc2, self.constc], writes=[xtc])
            P.dma("sp", self.XBC[m], xtb, reads=[xtc], writes=[self.XBCc[m]])
        if getattr(self, 'stop_at', None) == 'p0c':
            self.bailed = True
            return
        self.hb_release()
        AF32 = self.ARENA[:, :].bitcast(F32)
        tl = [AF32[:, k * NF:(k + 1) * NF] for k in range(5)]
        tlc = [P.cell() for _ in range(5)]
        DT, DTA, ACS, DTE, DCH = tl
        DTc, DTAc, ACSc, DTEc, DCHc = tlc
        boff = 5 * NF * 2
        big = [self.ARENA[:, boff + k * DI: boff + (k + 1) * DI] for k in range(3)]
        bigc = [P.cell() for _ in range(3)]
        P.handoff([self.HBc], tlc + bigc)
        v3 = lambda t: t.rearrange("p (c h) -> p c h", c=NCH)
        P.op("dve", lambda e: e.tensor_copy(out=DT, in_=dtw[:, 0:NF]), reads=[dtwc], writes=[DTc])
        aw, awc = self.work()
        P.op("act", lambda e: e.activation(out=aw[:, 0:NDT], in_=self.sv("ssd_alog%d" % j), func=AF.Exp), reads=[self.constc], writes=[awc])
        P.op("dve", lambda e: e.scalar_tensor_tensor(out=v3(DTA), in0=v3(DT), scalar=-1.0, in1=b_mid(aw[:, 0:NDT], NCH),
                                                     op0=ALU.mult, op1=ALU.mult), reads=[DTc, awc], writes=[DTAc])
        if getattr(self, 'stop_at', None) == 'q1':
            self.bailed = True
            return
        NB = (NF + 511) // 512

        def fmm(lhsT, dst_ps, dst_pc):
            for b_ in range(NB):
                w_ = min(512, NF - b_ * 512)
                P.op("pe", lambda e, b_=b_, w_=w_: e.matmul(dst_ps[:, b_ * 512:b_ * 512 + w_], lhsT=lhsT, rhs=DTA[:, b_ * 512:b_ * 512 + w_],
                                                          start=True, stop=True),
                     reads=[DTAc, self.constc], writes=[dst_pc], inc=(b_ == NB - 1))
        psF, pcF = self.psum()
        fmm(U, psF, pcF)
        P.op("act", lambda e: e.activation(out=v3(ACS)[:, :, 0:SH], in_=v3(psF[:, 0:NF])[:, :, 0:SH], func=AF.Identity), reads=[pcF], writes=[ACSc])
        if getattr(self, 'stop_at', None) == 'q2':
            self.bailed = True
            return
        psB, pcB = self.psum()
        fmm(LW, psB, pcB)
        P.op("act", lambda e: e.activation(out=v3(ACS)[:, :, SH:NDT], in_=v3(psB[:, 0:NF])[:, :, SH:NDT], func=AF.Identity), reads=[pcB], writes=[ACSc])
        if getattr(self, 'stop_at', None) == 'q3':
            self.bailed = True
            return
        psT, pcT = self.psum()
        fmm(self.ONES_F, psT, pcT)
        P.op("act", lambda e: e.activation(out=DCH, in_=psT[:, 0:NF], func=AF.Exp), reads=[pcT], writes=[DCHc])
        if getattr(self, 'stop_at', None) == 'r1':
            self.bailed = True
            return
        P.op("dve", lambda e: e.tensor_tensor(out=DTE, in0=psT[:, 0:NF], in1=ACS, op=ALU.subtract), reads=[pcT, ACSc], writes=[DTEc])
        if getattr(self, 'stop_at', None) == 'r2':
            self.bailed = True
            return
        P.op("act", lambda e: e.activation(out=DTE, in_=DTE, func=AF.Exp), reads=[DTEc], writes=[DTEc])
        if getattr(self, 'stop_at', None) == 'r3':
            self.bailed = True
            return
        P.op("dve", lambda e: e.tensor_tensor(out=DTE, in0=DTE, in1=DT, op=ALU.mult), reads=[DTEc, DTc], writes=[DTEc])
        if getattr(self, 'stop_at', None) == 'r4':
            self.bailed = True
            return
        P.op("act", lambda e: e.activation(out=ACS, in_=ACS, func=AF.Exp), reads=[ACSc], writes=[ACSc])
        EACS, EACSc = ACS, ACSc
        if getattr(self, 'stop_at', None) == 'q4':
            self.bailed = True
            return
        DGD = self.A2[:, 0:XC * 128].rearrange("p (x m) -> p x m", x=XC)
        DGDc = P.cell()
        XCV = self.A2[:, XC * 128:2 * XC * 128].rearrange("p (x m) -> p x m", x=XC)
        XCVc = P.cell()
        BCV = self.A2[:, 2 * XC * 128:2 * XC * 128 + 2 * SG * 128].rearrange("p (x m) -> p x m", x=2 * SG)
        BCVc = P.cell()
        GMB = self.GMB[:, :].rearrange("p (d m) -> p d m", d=2)
        GMBc = P.cell()
        self.a2_switch([DGDc, XCVc, BCVc])
        for x in range(XC):
            P.op("dve", lambda e, x=x: e.tensor_scalar(out=DGD[:, x, :], in0=IDB[:], scalar1=self.sv("ssd_dch%d" % j, x), scalar2=None, op0=ALU.mult),
                 reads=[self.constc, DGDc], writes=[DGDc])
        WBF = self.WBALL[:, :].bitcast(F32)
        wbc = [P.cell() for _ in range(4)]
        P.handoff(self.WBc, wbc)
        stg = [self.WBALL[:, k * DI:(k + 1) * DI] for k in range(4)]

        def load_chunk_fm(ch, first, cnt):
            v, tc_ = (XCV, XCVc) if first == 0 else (BCV, BCVc)
            P.dma("sp", v, self.XBC[first:first + cnt, :, ch * 128:(ch + 1) * 128].rearrange("x p m -> p x m"),
                  reads=[self.XBCc[first:first + cnt]], writes=[tc_])
            return v, tc_

        if getattr(self, 'stop_at', None) == 'p1':
            self.bailed = True
            return
        for ch in range(NCH):
            xcv, xcc = load_chunk_fm(ch, 0, XC)
            bcv, bcc = load_chunk_fm(ch, XC, 2 * SG)
            def stage1(g):
                psx, pcx = self.psum_bank()
                for jj in range(XPG):
                    P.op("pe", lambda e, jj=jj, g=g, psx=psx: e.matmul(psx[:, jj * 128:(jj + 1) * 128], lhsT=xcv[:, g * XPG + jj, :], rhs=IDB[:],
                                                                    start=(jj == 0), stop=True),
                         reads=[xcc, self.constc], writes=[pcx], inc=(jj == XPG - 1))
                xv = psx[:, 0:GW].rearrange("p (h q) -> p h q", h=HPG)
                for d in range(2):
                    h0 = d * SH + g * HPG
                    P.op("dve", lambda e, d=d, h0=h0, xv=xv, g=g: e.tensor_tensor(out=stg[d][:, g * GW:(g + 1) * GW].rearrange("p (h q) -> p h q", h=HPG),
                                                                               in0=xv, in1=b_last(v3(DT)[:, ch, h0:h0 + HPG], 64), op=ALU.mult),
                         reads=[pcx, DTc], writes=[wbc[d]])
                    P.op("dve", lambda e, d=d, h0=h0, xv=xv, g=g: e.tensor_tensor(out=stg[2 + d][:, g * GW:(g + 1) * GW].rearrange("p (h q) -> p h q", h=HPG),
                                                                               in0=xv, in1=b_last(v3(DTE)[:, ch, h0:h0 + HPG], 64), op=ALU.mult),
                         reads=[pcx, DTEc], writes=[wbc[2 + d]])
                psb, pcb = self.psum_bank()
                P.op("pe", lambda e, g=g, psb=psb: e.matmul(psb[:, 0:128], lhsT=bcv[:, g, :], rhs=IDB[:], start=True, stop=True),
                     reads=[bcc, self.constc], writes=[pcb])
                bt, btc = self.work()
                btb = bt[:, 0:64].bitcast(BF16)
                P.op("act", lambda e, btb=btb, psb=psb: e.activation(out=btb, in_=psb[:, 0:128], func=AF.Identity), reads=[pcb], writes=[btc])
                return btb, btc

            def stage2(g, btb, btc):
                for d in range(2):
                    pss, pcs = self.psum_bank()
                    P.op("pe", lambda e, d=d, g=g, pss=pss, btb=btb: e.matmul(pss[:, 0:GW], lhsT=btb, rhs=stg[2 + d][:, g * GW:(g + 1) * GW], start=True, stop=True),
                         reads=[btc, wbc[2 + d]], writes=[pcs])
                    ev, evc = self.work()
                    P.op("act", lambda e, ev=ev, pss=pss: e.activation(out=ev[:, 0:GW], in_=pss[:, 0:GW], func=AF.Identity), reads=[pcs], writes=[evc])
                    P.dma("sp", self.CS[d, ch][:, g * GW:(g + 1) * GW], ev[:, 0:GW], reads=[evc], writes=[self.CSc[d][ch][g]])

            prev = None
            for g in range(SG):
                cur = (g,) + stage1(g)
                if prev is not None:
                    stage2(*prev)
                prev = cur
            stage2(*prev)
            for d in range(2):
                P.dma("sp", self.XDT[d, ch], stg[d], reads=[wbc[d]], writes=[self.XDTc[d][ch]])
        if getattr(self, 'stop_at', None) == 'pA':
            self.bailed = True
            return
        HD = DI // 2
        HH = SH // 2
        st = [[WBF[:, (d * 2 + hf) * HD:(d * 2 + hf + 1) * HD] for hf in range(2)] for d in range(2)]
        stc = [[P.cell() for _ in range(2)] for _ in range(2)]
        P.handoff(wbc, stc)
        for d in range(2):
            for hf in range(2):
                P.dma("sp", st[d][hf], self.H0T[d][:, hf * HD:(hf + 1) * HD], writes=[stc[d][hf]])
        for k_ in range(NCH):
            for d in range(2):
                ch = k_ if d == 0 else NCH - 1 - k_
                for hf in range(2):
                    sc_, scc = st[d][hf], stc[d][hf]
                    sb_, sbc = self.work()
                    sbb = sb_[:, 0:HD // 2].bitcast(BF16)
                    P.op("act", lambda e, sbb=sbb, sc_=sc_: e.activation(out=sbb, in_=sc_, func=AF.Identity), reads=[scc], writes=[sbc])
                    P.dma("sp", self.SIN[d, ch][:, hf * HD:(hf + 1) * HD], sbb, reads=[sbc], writes=[self.SINc[d][ch][hf]])
                    cs_, csc = self.work()
                    P.dma("sp", cs_[:, 0:HD], self.CS[d, ch][:, hf * HD:(hf + 1) * HD], reads=[self.CSc[d][ch]], writes=[csc])
                    h0 = d * SH + hf * HH
                    P.op("dve", lambda e, sc_=sc_, h0=h0, ch=ch: e.tensor_tensor(out=sc_.rearrange("p (h q) -> p h q", h=HH),
                                                                               in0=sc_.rearrange("p (h q) -> p h q", h=HH),
                                                                               in1=b_last(v3(DCH)[:, ch, h0:h0 + HH], 64), op=ALU.mult),
                         reads=[scc, DCHc], writes=[scc])
                    P.op("dve", lambda e, sc_=sc_, cs_=cs_: e.tensor_tensor(out=sc_, in0=sc_, in1=cs_[:, 0:HD], op=ALU.add),
                         reads=[scc, csc], writes=[scc])
                    seg_end = (ch % CPS == CPS - 1) if d == 0 else (ch % CPS == 0)
                    if seg_end:
                        P.dma("sp", self.NEWST[ch // CPS, d][:, hf * HD:(hf + 1) * HD], sc_, reads=[scc], writes=[P.cell()])
                        P.op("dve", lambda e, sc_=sc_: e.tensor_scalar(out=sc_, in0=sc_, scalar1=fS, scalar2=None, op0=ALU.mult),
                             reads=[scc, self.constc], writes=[scc])
        if getattr(self, 'stop_at', None) == 'pA2':
            self.bailed = True
            return
        YG = WBF[:, 0:DI]
        YGc = P.cell()
        xdt = [self.WBALL[:, 2 * DI + d * DI: 2 * DI + (d + 1) * DI] for d in range(2)]
        xdtc = [P.cell() for _ in range(2)]
        P.handoff(stc, [YGc] + xdtc)
        ZTs, SIN0, SIN1 = big
        ZTsc, SIN0c, SIN1c = bigc
        sins, sinsc = (SIN0, SIN1), (SIN0c, SIN1c)
        TRI = (U, LW)
        STR = (SL, SU)
        for ch in range(NCH):
            xcv, xcc = load_chunk_fm(ch, 0, XC)
            bcv, bcc = load_chunk_fm(ch, XC, 2 * SG)
            P.dma("sp", ZTs, self.ZT[ch], reads=[self.ZTc[ch]], writes=[ZTsc])
            for d in range(2):
                P.dma("sp", xdt[d], self.XDT[d, ch], reads=[self.XDTc[d][ch]], writes=[xdtc[d]])
                P.dma("sp", sins[d], self.SIN[d, ch], reads=[self.SINc[d][ch]], writes=[sinsc[d]])
            for g in range(SG):
                psg, pcg = self.psum_bank()
                P.op("pe", lambda e, g=g, psg=psg: e.matmul(psg[:, 0:128], lhsT=bcv[:, g, :], rhs=bcv[:, SG + g, :], start=True, stop=True),
                     reads=[bcc], writes=[pcg])
                gmb, gmc = GMB, GMBc
                for d in range(2):
                    P.op("dve", lambda e, d=d, psg=psg, gmb=gmb: e.tensor_tensor(out=gmb[:, d, :], in0=psg[:, 0:128], in1=TRI[d], op=ALU.mult),
                         reads=[pcg, self.constc], writes=[gmc])
                psy, pcy = self.psum_bank()
                for jj in range(XPG):
                    x = g * XPG + jj
                    P.op("pe", lambda e, jj=jj, x=x, psy=psy: e.matmul(psy[:, jj * 128:(jj + 1) * 128], lhsT=xcv[:, x, :], rhs=DGD[:, x, :],
                                                                    start=(jj == 0), stop=False),
                         reads=[xcc, DGDc], writes=[pcy], inc=False)
                NBT = HPG // HBAT
                decw, decwc = self.work()
                etw, etwc = self.work()
                etall = etw[:, 0:2 * NBT * HBAT * 64].bitcast(BF16)
                batches = []
                for d in range(2):
                    for hb in range(NBT):
                        bi = d * NBT + hb
                        hl = hb * HBAT
                        h0 = d * SH + g * HPG + hl
                        decv = decw[:, bi * HBAT * 128:(bi + 1) * HBAT * 128].rearrange("p (h m) -> p h m", h=HBAT)
                        P.op("dve", lambda e, d=d, h0=h0, decv=decv: e.tensor_tensor(out=decv, in0=b_mid(STR[d], HBAT),
                                                                                   in1=b_last(v3(DTA)[:, ch, h0:h0 + HBAT], 128), op=ALU.mult),
                             reads=[DTAc, self.constc], writes=[decwc])
                        psd_, pcd_ = self.psum_bank()
                        for hh in range(HBAT):
                            P.op("pe", lambda e, hh=hh, d=d, decv=decv, psd_=psd_: e.matmul(psd_[:, hh * 128:(hh + 1) * 128], lhsT=decv[:, hh, :], rhs=TRI[d],
                                                                                         start=(hh == 0), stop=True),
                                 reads=[decwc, self.constc], writes=[pcd_], inc=(hh == HBAT - 1))
                        etb = etall[:, bi * HBAT * 128:(bi + 1) * HBAT * 128].rearrange("p (h m) -> p h m", h=HBAT)
                        P.op("act", lambda e, etb=etb, psd_=psd_: e.activation(out=etb, in_=psd_[:, 0:HBAT * 128].rearrange("p (h m) -> p h m", h=HBAT), func=AF.Exp),
                             reads=[pcd_], writes=[etwc])
                        P.op("dve", lambda e, etb=etb, d=d, gmb=gmb: e.tensor_tensor(out=etb, in0=etb, in1=b_mid(gmb[:, d, :], HBAT), op=ALU.mult),
                             reads=[etwc, gmc], writes=[etwc])
                        batches.append((d, hl, etb))
                yos = []
                for d in range(2):
                    pso, pco = self.psum_bank()
                    P.op("pe", lambda e, d=d, g=g, pso=pso: e.matmul(pso[:, 0:GW], lhsT=bcv[:, SG + g, :], rhs=sins[d][:, g * GW:(g + 1) * GW], start=True, stop=True),
                         reads=[bcc, sinsc[d]], writes=[pco])
                    yo, yoc = self.work()
                    yob = yo[:, 0:GW // 2].bitcast(BF16)
                    h0g = d * SH + g * HPG
                    P.op("dve", lambda e, yob=yob, pso=pso, h0g=h0g: e.tensor_tensor(out=yob.rearrange("p (h q) -> p h q", h=HPG),
                                                                                   in0=pso[:, 0:GW].rearrange("p (h q) -> p h q", h=HPG),
                                                                                   in1=b_last(v3(EACS)[:, ch, h0g:h0g + HPG], 64), op=ALU.mult),
                         reads=[pco, EACSc], writes=[yoc])
                    yos.append((yob, yoc))
                for (d, hl, etb) in batches:
                    for hh in range(HBAT):
                        col = (hl + hh) * 64
                        P.op("pe", lambda e, hh=hh, col=col, d=d, etb=etb, psy=psy, g=g: e.matmul(psy[:, col:col + 64], lhsT=etb[:, hh, :],
                                                                                               rhs=xdt[d][:, g * GW + col: g * GW + col + 64],
                                                                                               start=False, stop=False),
                             reads=[etwc, xdtc[d]], writes=[pcy], inc=False)
                for d in range(2):
                    yob, yoc = yos[d]
                    P.op("pe", lambda e, yob=yob, psy=psy, d=d: e.matmul(psy[:, 0:GW], lhsT=IDB[:], rhs=yob, start=False, stop=(d == 1)),
                         reads=[yoc, self.constc], writes=[pcy], inc=(d == 1))
                P.op("dve", lambda e, psy=psy, g=g: e.tensor_tensor(out=YG[:, g * GW:(g + 1) * GW], in0=psy[:, 0:GW], in1=ZTs[:, g * GW:(g + 1) * GW], op=ALU.mult),
                     reads=[pcy, ZTsc], writes=[YGc])
            if self.debug:
                P.dma("sp", self.DBG[ch], YG, reads=[YGc], writes=[P.cell()])
            sq, sqc = self.work()
            ss, ssc = self.work()
            P.op("act", lambda e, sq=sq, ss=ss: e.activation(out=sq[:, 0:DI // 2].bitcast(BF16), in_=YG, func=AF.Square, accum_out=ss[:, 0:1]),
                 reads=[YGc], writes=[sqc, ssc])
            P.op("dve", lambda e, ss=ss: e.tensor_scalar(out=ss[:, 0:1], in0=ss[:, 0:1], scalar1=1.0 / DI, scalar2=EPS, op0=ALU.mult, op1=ALU.add),
                 reads=[ssc], writes=[ssc])
            P.op("act", lambda e, ss=ss: e.activation(out=ss[:, 0:1], in_=ss[:, 0:1], func=AF.Sqrt), reads=[ssc], writes=[ssc])
            P.op("dve", lambda e, ss=ss: e.reciprocal(out=ss[:, 0:1], in_=ss[:, 0:1]), reads=[ssc], writes=[ssc])
            yn, ync = self.work()
            ynb = yn[:, 0:DI // 2].bitcast(BF16)
            P.op("act", lambda e, ynb=ynb, ss=ss: e.activation(out=ynb, in_=YG, func=AF.Identity, scale=ss[:, 0:1]), reads=[YGc, ssc], writes=[ync])
            yt, ytc = self.work()
            ytb = yt[:, 0:XC * 64].bitcast(BF16).rearrange("p (x m) -> p x m", x=XC)
            XB = min(4, XC)
            for x0 in range(0, XC, XB):
                pst, pct = self.psum_bank()
                for xx in range(XB):
                    P.op("pe", lambda e, xx=xx, x0=x0, pst=pst, ynb=ynb: e.matmul(pst[:, xx * 128:(xx + 1) * 128], lhsT=ynb[:, (x0 + xx) * 128:(x0 + xx + 1) * 128],
                                                                               rhs=IDB[:], start=True, stop=True),
                         reads=[ync, self.constc], writes=[pct], inc=(xx == XB - 1))
                P.op("dve", lambda e, x0=x0, pst=pst, ytb=ytb: e.tensor_tensor(out=ytb[:, x0:x0 + XB, :], in0=pst[:, 0:XB * 128].rearrange("p (x m) -> p x m", x=XB),
                                                                            in1=b_last(self.sv("ssd_ng%d" % j)[:, x0:x0 + XB], 128), op=ALU.mult),
                     reads=[pct, self.constc], writes=[ytc])
            P.dma("sp", self.YF[0:XC, :, ch * 128:(ch + 1) * 128].rearrange("x p m -> p x m"), ytb, reads=[ytc], writes=[self.YFc[ch]])
        if getattr(self, 'stop_at', None) == 'pB':
            self.bailed = True
            return
        P.handoff(tlc + bigc, [self.HBc])
        P.handoff([YGc] + xdtc, self.WBc)
        wout = self.W["ssd_w_out"][j]
        for m0 in range(0, c.KC, 2):
            ms = [m for m in (m0, m0 + 1) if m < c.KC]
            pss = [self.psum() for _ in ms]
            wv = [self.load_w(wout, 0, DI, m * 128, 128) for m in ms]
            for kc in range(XC):
                slot = self.hb_slot()
                P.dma("sp", self.HB[:, slot, :], self.YF[kc], reads=[self.YFc], writes=[self.HBsc[slot]])
                for (ps, pc), (view, wc) in zip(pss, wv):
                    for tb in range(c.TB):
                        P.op("pe", lambda e, o=ps[:, tb * c.TBW:(tb + 1) * c.TBW], l=view[:, kc, 0:128],
                             r=self.HB[:, slot, tb * c.TBW:(tb + 1) * c.TBW], st=(kc == 0), sp_=(kc == XC - 1):
                             e.matmul(o, lhsT=l, rhs=r, start=st, stop=sp_),
                             reads=[wc, self.HBsc[slot]], writes=[pc], inc=(tb == c.TB - 1))
            for m, (ps, pc) in zip(ms, pss):
                self.out_epilogue(ps, pc, m, None)
        self.hb_release()
        self.post(i, 0, False)

    def attention(self, i, j):
        P = self.P
        c = self.cfg
        NH, NKV = c.NH, c.NKV
        KVG = NH // NKV
        T, PAST = c.T, c.PAST
        NT, PT = T // 128, PAST // 128
        NTK = NT + PT
        CPS = c.SEG // 128
        SPQ = c.TBW // c.SEG
        wq = self.W["attn_w_qkv"][j]
        zero_col = self.CF[:, 7, 0:1]
        offb = self.sv("flags", 1)
        cacheb = self.sv("flags", 2)
        scale = 128.0 ** -0.5
        self.prenorm(i, 0)
        COS, COSc, SIN, SINc = self.RSTD, self.RSTDc, self.RSTDO, self.RSTDOc
        P.dma("sp", COS[:, 0:T], self.ROPE[0], writes=[COSc])
        P.dma("sp", SIN[:, 0:T], self.ROPE[1], writes=[SINc])

        def norm_rope(ps, pc, gain_col, raw_out=None, raw_cell=None):
            xt, xc = self.work()
            P.op("act", lambda e: e.activation(out=xt[:, 0:T], in_=ps[:, 0:T], func=AF.Identity), reads=[pc], writes=[xc])
            sq, sqc = self.work()
            P.op("act", lambda e: e.activation(out=sq[:, 0:T], in_=xt[:, 0:T], func=AF.Square), reads=[xc], writes=[sqc])
            ps2, pc2 = self.colsum_bcast(sq, sqc)
            P.op("dve", lambda e: e.tensor_scalar(out=sq[:, 0:T], in0=ps2[:, 0:T], scalar1=1.0 / 128, scalar2=EPS, op0=ALU.mult, op1=ALU.add),
                 reads=[pc2], writes=[sqc])
            P.op("act", lambda e: e.activation(out=sq[:, 0:T], in_=sq[:, 0:T], func=AF.Sqrt), reads=[sqc], writes=[sqc])
            P.op("dve", lambda e: e.reciprocal(out=sq[:, 0:T], in_=sq[:, 0:T]), reads=[sqc], writes=[sqc])
            P.op("dve", lambda e: e.scalar_tensor_tensor(out=xt[:, 0:T], in0=xt[:, 0:T], scalar=gain_col, in1=sq[:, 0:T], op0=ALU.mult, op1=ALU.mult),
                 reads=[xc, sqc, self.constc], writes=[xc])
            if raw_out is not None:
                P.dma("sp", raw_out, xt[:, 0:T], reads=[xc], writes=[raw_cell])
            xb, xbc = self.work()
            xbb = xb[:, 0:T // 2].bitcast(BF16)
            P.op("act", lambda e: e.activation(out=xbb, in_=xt[:, 0:T], func=AF.Identity), reads=[xc], writes=[xbc])
            ps3, pc3 = self.psum()
            for tb in range(c.TB):
                P.op("pe", lambda e, tb=tb: e.matmul(ps3[:, tb * c.TBW:(tb + 1) * c.TBW], lhsT=self.ROTB[:], rhs=xbb[:, tb * c.TBW:(tb + 1) * c.TBW],
                                                   start=True, stop=True), reads=[xbc, self.constc], writes=[pc3], inc=(tb == c.TB - 1))
            P.op("dve", lambda e: e.tensor_tensor(out=xt[:, 0:T], in0=xt[:, 0:T], in1=COS[:, 0:T], op=ALU.mult), reads=[xc, COSc], writes=[xc])
            P.op("dve", lambda e: e.tensor_tensor(out=sq[:, 0:T], in0=ps3[:, 0:T], in1=SIN[:, 0:T], op=ALU.mult), reads=[pc3, SINc], writes=[sqc])
            P.op("dve", lambda e: e.tensor_tensor(out=xbb, in0=xt[:, 0:T], in1=sq[:, 0:T], op=ALU.add), reads=[xc, sqc], writes=[xbc])
            return xbb, xbc

        for kv in range(NKV):
            wv_, wc_ = self.load_w(wq, 0, c.D, (NH + kv) * 128, 128)
            ps, pc = self.psum()
            self.matmul_acc(ps, pc, wv_, wc_, 0, c.KC, self.hb_rhs)
            kb, kbc = norm_rope(ps, pc, self.sv("att_kn%d" % j, 0), raw_out=self.NEWK[kv], raw_cell=P.cell())
            P.dma("sp", self.KD[kv][:, 0:T], kb, reads=[kbc], writes=[self.KDc[kv]])
            ck, ckc = self.work()
            P.dma("sp", ck[:, 0:PAST], self.CACHEK[kv], writes=[ckc])
            cb, cbc = self.work()
            cbb = cb[:, 0:PAST // 2].bitcast(BF16)
            P.op("act", lambda e, cbb=cbb, ck=ck: e.activation(out=cbb, in_=ck[:, 0:PAST], func=AF.Identity), reads=[ckc], writes=[cbc])
            P.dma("sp", self.KD[kv][:, T:T + PAST], cbb, reads=[cbc], writes=[self.KDc[kv]])
        VW = NKV * 128
        VH = min(256, VW)
        for v0 in range(0, VW, VH):
            wv_, wc_ = self.load_w(wq, 0, c.D, (NH + NKV) * 128 + v0, VH)
            for tt in range(NT):
                psv, pcv = self.psum_bank()
                for kc in range(c.KC):
                    P.op("pe", lambda e, tt=tt, kc=kc, psv=psv: e.matmul(psv[:, 0:VH], lhsT=self.HB[:, kc, tt * 128:(tt + 1) * 128], rhs=wv_[:, kc, 0:VH],
                                                                      start=(kc == 0), stop=(kc == c.KC - 1)),
                         reads=[self.HBc, wc_], writes=[pcv], inc=(kc == c.KC - 1))
                vt, vtc = self.work()
                P.op("act", lambda e, vt=vt, psv=psv: e.activation(out=vt[:, 0:VH], in_=psv[:, 0:VH], func=AF.Identity), reads=[pcv], writes=[vtc])
                P.dma("sp", self.NEWV[tt * 128:(tt + 1) * 128, v0:v0 + VH], vt[:, 0:VH], reads=[vtc], writes=[P.cell()])
                vb, vbc = self.work()
                vbb = vb[:, 0:VH // 2].bitcast(BF16)
                P.op("dve", lambda e, vbb=vbb, vt=vt: e.tensor_copy(out=vbb, in_=vt[:, 0:VH]), reads=[vtc], writes=[vbc])
                P.dma("sp", self.VD[tt][:, v0:v0 + VH], vbb, reads=[vbc], writes=[self.VDc[tt]])
        for pt in range(PT):
            cv_, cvc = self.work()
            P.dma("sp", cv_[:, 0:VW], self.CACHEV[pt * 128:(pt + 1) * 128, :], writes=[cvc])
            cb, cbc = self.work()
            cbb = cb[:, 0:VW // 2].bitcast(BF16)
            P.op("dve", lambda e, cbb=cbb, cv_=cv_: e.tensor_copy(out=cbb, in_=cv_[:, 0:VW]), reads=[cvc], writes=[cbc])
            P.dma("sp", self.VD[NT + pt], cbb, reads=[cbc], writes=[self.VDc[NT + pt]])
        for h in range(NH):
            if h % 2 == 0:
                wq_ = self.load_w(wq, 0, c.D, h * 128, min(2, NH - h) * 128)
            ps, pc = self.psum()
            self.matmul_acc(ps, pc, wq_[0], wq_[1], (h % 2) * 128, c.KC, self.hb_rhs)
            qb, qbc = norm_rope(ps, pc, self.sv("att_qn%d" % j, 0))
            P.dma("sp", self.QD[h], qb, reads=[qbc], writes=[self.QDc[h]])
        self.hb_release()
        A = self.ARENA
        o_ = 0
        KB = A[:, o_:o_ + NKV * (T + PAST)].rearrange("p (k t) -> p k t", k=NKV); o_ += NKV * (T + PAST)
        VB = A[:, o_:o_ + NTK * VW].rearrange("p (t v) -> p t v", t=NTK); o_ += NTK * VW
        QB = [A[:, o_ + k * T:o_ + (k + 1) * T] for k in range(2)]; o_ += 2 * T
        EB = [A[:, o_ + k * 512:o_ + (k + 1) * 512] for k in range(4)]; o_ += 4 * 512
        AO = [A[:, o_ + k * T:o_ + (k + 1) * T] for k in range(1)]; o_ += T
        KBc, VBc = P.cell(), P.cell()
        QBc = [P.cell() for _ in QB]
        EBc = [P.cell() for _ in EB]
        AOc = [P.cell() for _ in AO]
        P.handoff([self.HBc], [KBc, VBc] + QBc + EBc + AOc)
        for kv in range(NKV):
            P.dma("sp", KB[:, kv, :], self.KD[kv], reads=[self.KDc[kv]], writes=[KBc])
        for tk in range(NTK):
            P.dma("sp", VB[:, tk, :], self.VD[tk], reads=[self.VDc[tk]], writes=[VBc])
        self._eb_rr = 0
        NQC = T // c.TBW
        for h in range(NH):
            kv = h // KVG
            qb, qbc = QB[h % 2], QBc[h % 2]
            P.dma("sp", qb, self.QD[h], reads=[self.QDc[h]], writes=[qbc])
            ao, aoc = AO[0], AOc[0]
            for qc in range(NQC):
                par = (h * NQC + qc) % 2
                pso, pco = self.PS[1][:, (2 * par) * 512:(2 * par + 1) * 512], self.PSc[1][2 * par]
                psl, pcl = self.PS[1][:, (2 * par + 1) * 512:(2 * par + 2) * 512], self.PSc[1][2 * par + 1]
                def emit_s(tk):
                    sb_ = tk % 4
                    pss, pcs = self.PS[0][:, sb_ * 512:(sb_ + 1) * 512], self.PSc[0][sb_]
                    P.op("pe", lambda e, tk=tk, pss=pss, qb=qb, qc=qc, kv=kv: e.matmul(pss[:, 0:c.TBW], lhsT=KB[:, kv, tk * 128:(tk + 1) * 128],
                                                                                 rhs=qb[:, qc * c.TBW:(qc + 1) * c.TBW], start=True, stop=True),
                         reads=[KBc, qbc], writes=[pcs])
                    k_ = self._eb_rr
                    self._eb_rr = (k_ + 1) % 4
                    eb, ebc = EB[k_], EBc[k_]
                    if tk >= NT:
                        segs = [(0, c.TBW, cacheb)]
                    else:
                        sk = tk // CPS
                        segs = []
                        for jh in range(SPQ):
                            bias = zero_col if (qc * SPQ + jh == sk) else offb
                            segs.append((jh * c.SEG, (jh + 1) * c.SEG, bias))
                        merged = [segs[0]]
                        for sgm in segs[1:]:
                            if sgm[2] is merged[-1][2]:
                                merged[-1] = (merged[-1][0], sgm[1], sgm[2])
                            else:
                                merged.append(sgm)
                        segs = merged
                    for (a0, a1, bias) in segs:
                        P.op("act", lambda e, a0=a0, a1=a1, bias=bias, eb=eb, pss=pss: e.activation(out=eb[:, a0:a1], in_=pss[:, a0:a1], func=AF.Exp,
                                                                                           bias=bias, scale=scale),
                             reads=[pcs, self.constc], writes=[ebc])
                    return eb, ebc

                def emit_pv(tk, eb, ebc):
                    P.op("pe", lambda e, tk=tk, eb=eb, pso=pso, kv=kv: e.matmul(pso[:, 0:c.TBW], lhsT=VB[:, tk, kv * 128:(kv + 1) * 128], rhs=eb[:, 0:c.TBW],
                                                                          start=(tk == 0), stop=(tk == NTK - 1)),
                         reads=[VBc, ebc], writes=[pco], inc=False)
                    P.op("pe", lambda e, tk=tk, eb=eb, psl=psl: e.matmul(psl[:, 0:c.TBW], lhsT=self.ONESB[:], rhs=eb[:, 0:c.TBW],
                                                                      start=(tk == 0), stop=(tk == NTK - 1)),
                         reads=[self.constc, ebc], writes=[pcl])

                pend = []
                for tk in range(NTK):
                    pend.append((tk,) + emit_s(tk))
                    if len(pend) > 2:
                        emit_pv(*pend.pop(0))
                while pend:
                    emit_pv(*pend.pop(0))
                rc, rcc = self.work()
                P.op("dve", lambda e, rc=rc, psl=psl: e.reciprocal(out=rc[:, 0:c.TBW], in_=psl[:, 0:c.TBW]), reads=[pcl], writes=[rcc])
                P.op("dve", lambda e, rc=rc, pso=pso, ao=ao, qc=qc: e.tensor_tensor(out=ao[:, qc * c.TBW:(qc + 1) * c.TBW], in0=pso[:, 0:c.TBW], in1=rc[:, 0:c.TBW], op=ALU.mult),
                     reads=[pco, pcl, rcc], writes=[aoc])
            P.dma("sp", self.AOD[h], ao, reads=[aoc], writes=[self.AODc[h]])
        P.handoff([KBc, VBc] + QBc + EBc + AOc, [self.HBc])
        wo = self.W["attn_w_o"][j]
        for m0 in range(0, c.KC, 2):
            ms = [m for m in (m0, m0 + 1) if m < c.KC]
            pss_ = [self.psum() for _ in ms]
            wv2 = [self.load_w(wo, 0, NH * 128, m * 128, 128) for m in ms]
            for kc in range(NH):
                slot = self.hb_slot()
                P.dma("sp", self.HB[:, slot, :], self.AOD[kc], reads=[self.AODc[kc]], writes=[self.HBsc[slot]])
                for (ps, pc), (view, wc) in zip(pss_, wv2):
                    for tb in range(c.TB):
                        P.op("pe", lambda e, o=ps[:, tb * c.TBW:(tb + 1) * c.TBW], l=view[:, kc, 0:128],
                             r=self.HB[:, slot, tb * c.TBW:(tb + 1) * c.TBW], st=(kc == 0), sp_=(kc == NH - 1):
                             e.matmul(o, lhsT=l, rhs=r, start=st, stop=sp_),
                             reads=[wc, self.HBsc[slot]], writes=[pc], inc=(tb == c.TB - 1))
            for m, (ps, pc) in zip(ms, pss_):
                self.out_epilogue(ps, pc, m, None)
        self.hb_release()
        self.post(i, 0, False)

    def build(self):
        c = self.cfg
        nc = bass.Bass("TRN2", target_bir_lowering=False)
        self.nc = nc
        es = ExitStack()
        self.es = es
        P = Prog(nc, es)
        self.P = P

        def din(name, shape):
            return nc.dram_tensor(name, list(shape), F32, kind="ExternalInput").ap()

        def dscr(name, shape, dt):
            return nc.dram_tensor(name, list(shape), dt, kind="Internal").ap()

        self.XIN = din("xin", (c.KC, 128, c.T))
        self.SMALL = din("small", (128, self.NSMALL))
        self.CONSTF = din("constf", (128, 8, 128))
        self.W = {}
        self.W["w_mod"] = din("w_mod", (c.DEPTH, c.D, 6 * c.D))
        self.W["cv_w_pw1"] = din("cv_w_pw1", (c.NCONV, c.D, 2 * c.D))
        self.W["cv_w_pw2"] = din("cv_w_pw2", (c.NCONV, c.D, c.D))
        self.W["ffn_w_up"] = din("ffn_w_up", (c.DEPTH, c.D, 2 * c.DFF))
        self.W["ffn_w_down"] = din("ffn_w_down", (c.DEPTH, c.DFF, c.D))
        NCH, XC, NXBC = c.T // 128, c.DI // 128, c.DI // 128 + 2 * c.SG
        if c.NSSD:
            self.W["ssd_w_in"] = din("ssd_w_in", (c.NSSD, c.D, 2 * c.DI + 2 * c.SG * 128 + 2 * c.SH))
            self.W["ssd_w_out"] = din("ssd_w_out", (c.NSSD, c.DI, c.D))
            self.H0T = din("h0t", (2, 128, c.DI))
            self.NEWST = nc.dram_tensor("newst", [c.NSEG, 2, 128, c.DI], F32, kind="ExternalOutput").ap()
            self.ZT = dscr("ZT", (NCH, 128, c.DI), BF16)
            self.XBC = dscr("XBC", (NXBC, 128, c.T), BF16)
            self.XDT = dscr("XDT", (2, NCH, 128, c.DI), BF16)
            self.CS = dscr("CS", (2, NCH, 128, c.DI), F32)
            self.SIN = dscr("SIN", (2, NCH, 128, c.DI), BF16)
            self.YF = dscr("YF", (XC, 128, c.T), BF16)
            self.ZTc = [[P.cell() for _ in range(c.DI // 256)] for _ in range(NCH)]
            self.XBCc = [P.cell() for _ in range(NXBC)]
            self.XDTc = [[P.cell() for _ in range(NCH)] for _ in range(2)]
            self.CSc = [[[P.cell() for _ in range(c.SG)] for _ in range(NCH)] for _ in range(2)]
            self.SINc = [[[P.cell() for _ in range(2)] for _ in range(NCH)] for _ in range(2)]
            self.YFc = [P.cell() for _ in range(NCH)]
        if c.NATT:
            NT_, PT_ = c.T // 128, c.PAST // 128
            self.W["attn_w_qkv"] = din("attn_w_qkv", (c.NATT, c.D, (c.NH + 2 * c.NKV) * 128))
            self.W["attn_w_o"] = din("attn_w_o", (c.NATT, c.NH * 128, c.D))
            self.ROPE = din("rope", (2, 128, c.T))
            self.CACHEK = din("cachek", (c.NKV, 128, c.PAST))
            self.CACHEV = din("cachev", (c.PAST, c.NKV * 128))
            self.NEWK = nc.dram_tensor("newk", [c.NKV, 128, c.T], F32, kind="ExternalOutput").ap()
            self.NEWV = nc.dram_tensor("newv", [c.T, c.NKV * 128], F32, kind="ExternalOutput").ap()
            self.KD = dscr("KD", (c.NKV, 128, c.T + c.PAST), BF16)
            self.VD = dscr("VD", (NT_ + PT_, 128, c.NKV * 128), BF16)
            self.QD = dscr("QD", (c.NH, 128, c.T), BF16)
            self.AOD = dscr("AOD", (c.NH, 128, c.T), BF16)
            self.KDc = [P.cell() for _ in range(c.NKV)]
            self.VDc = [P.cell() for _ in range(NT_ + PT_)]
            self.QDc = [P.cell() for _ in range(c.NH)]
            self.AODc = [P.cell() for _ in range(c.NH)]
        if self.debug:
            self.DBG = nc.dram_tensor("dbg", [NCH, 128, c.DI], F32, kind="ExternalOutput").ap()
        self.YOUT = nc.dram_tensor("yout", [c.KC, 128, c.T], F32, kind="ExternalOutput").ap()
        self.Y = dscr("Y", (c.KC, 128, c.T), F32)
        self.O = dscr("O", (c.KC, 128, c.T), F32)
        self.CV = dscr("CV", (c.KC, 128, c.T), F32)
        self.ACTD = dscr("ACTD", (c.FC, 128, c.T), BF16)
        self.Yc = [P.cell() for _ in range(c.KC)]
        self.Oc = [P.cell() for _ in range(c.KC)]
        self.CVc = [P.cell() for _ in range(c.KC)]
        self.ACTDc = [P.cell() for _ in range(c.FC)]

        def sb(name, shape, dt):
            return es.enter_context(nc.sbuf_tensor(name, list(shape), dt))

        H = (c.CW - 1) // 2
        NCH_, NDT_ = c.T // 128, 2 * c.SH
        att_el = c.NKV * (c.T + c.PAST) + (c.T // 128 + c.PAST // 128) * c.NKV * 128 + 2 * c.T + 4 * 512 + c.T
        arena_el = max(c.KC * c.T, 10 * NCH_ * NDT_ + 3 * c.DI, att_el if c.NATT else 0)
        self.ARENA = sb("ARENA", (128, arena_el), BF16)
        self.HB = self.ARENA[:, 0:c.KC * c.T].rearrange("p (k t) -> p k t", k=c.KC)
        self.HBc = P.cell()
        self.HBsc = [P.cell() for _ in range(c.KC)]
        self.hb_slot_mode = False
        self.WSLOT = 4096
        self.WBALL = sb("WBALL", (128, 4 * self.WSLOT), BF16)
        self.WB = [self.WBALL[:, k * self.WSLOT:(k + 1) * self.WSLOT] for k in range(4)]
        self.WBc = [P.cell() for _ in self.WB]
        self.wb_rr = 0
        self.WK = [sb("WK%d" % k, (128, max(c.T + 64, 2112)), F32) for k in range(5)]
        self.WKc = [P.cell() for _ in self.WK]
        self.wk_rr = 0
        XC_ = c.DI // 128
        upn = c.NSEG * (c.SEG + 2 * H)
        a2_el = max(upn + c.CW * 128, 2 * c.T, 2 * XC_ * 128 + 2 * c.SG * 128)
        self.A2 = sb("A2", (128, a2_el), BF16)
        self.AB = [self.A2[:, k * c.T:(k + 1) * c.T] for k in range(2)]
        self.ABc = [P.cell() for _ in self.AB]
        self.ab_rr = 0
        self.UPF = self.A2[:, 0:upn]
        self.UP = self.UPF.rearrange("p (s w) -> p s w", s=c.NSEG)
        self.UPc = P.cell()
        self.DG = self.A2[:, upn:upn + c.CW * 128].rearrange("p (k m) -> p k m", k=c.CW)
        self.DGc = P.cell()
        self.A2cells = []
        self.GMB = sb("GMB", (128, 256), BF16)
        self.RSTD = sb("RSTD", (128, c.T), F32)
        self.RSTDc = P.cell()
        self.RSTDO = sb("RSTDO", (128, c.T), F32)
        self.RSTDOc = P.cell()
        self.SQY, self.SQYc = self.RSTD, self.RSTDc
        self.SQO, self.SQOc = self.RSTDO, self.RSTDOc
        self.SV = sb("SV", (128, self.NSMALL), F32)
        self.CF = sb("CF", (128, 8, 128), F32)
        self.ONES_F = self.CF[:, 0, :]
        self.IDB = sb("IDB", (128, 128), BF16)
        self.ROTB = sb("ROTB", (128, 128), BF16)
        self.ONESB = sb("ONESB", (128, 128), BF16)
        self.SC = sb("SC", (128, c.KC), BF16)
        self.MOD = sb("MOD", (128, c.DEPTH, 6 * c.KC), F32)
        self.MODc = P.cell()
        self.constc = P.cell()
        self.PS = [es.enter_context(nc.psum_tensor("PS%d" % k, [128, 2048], F32)) for k in range(2)]
        self.PSc = [[Cell(excl=True) for _ in range(4)] for _ in self.PS]
        self.ps_rr = 0
        self.psb_rr = 0

        P.dma("sp", self.SV[:], self.SMALL, writes=[self.constc])
        P.dma("sp", self.CF[:], self.CONSTF, writes=[self.constc])
        P.op("dve", lambda e: e.tensor_copy(out=self.IDB[:], in_=self.CF[:, 1, :]), reads=[self.constc], writes=[self.constc])
        P.op("dve", lambda e: e.tensor_copy(out=self.ROTB[:], in_=self.CF[:, 6, :]), reads=[self.constc], writes=[self.constc])
        P.op("dve", lambda e: e.tensor_copy(out=self.ONESB[:], in_=self.CF[:, 0, :]), reads=[self.constc], writes=[self.constc])
        P.op("act", lambda e: e.activation(out=self.SC[:], in_=self.sv("cond"), func=AF.Silu), reads=[self.constc], writes=[self.constc])
        for kc in range(c.KC):
            xt, xc = self.work()
            P.dma("sp", xt[:, 0:c.T], self.XIN[kc], writes=[xc])
            P.dma("sp", self.Y[kc], xt[:, 0:c.T], reads=[xc], writes=[self.Yc[kc]])
            self.sq_accum(xt[:, 0:c.T], xc, self.SQY, self.SQYc, first=(kc == 0))
        self.rstd_from_sq(self.SQY, self.SQYc, self.RSTD, self.RSTDc, c.D)
        for i in range(self.n_layers):
            self.modulation(i)
        for i in range(self.n_layers):
            kind, j = i % 3, i // 3
            last = (i == self.n_layers - 1)
            if kind == 0:
                self.conformer(i, j)
            elif kind == 1:
                self.ssd(i, j)
            else:
                self.attention(i, j)
            if getattr(self, 'bailed', False):
                break
            self.ffn(i, last)
        P.finish()
        es.close()
        return nc

    @property
    def NSMALL(self):
        return self.sp_layout.n


def const_f():
    cf = np.zeros((128, 8, 128), np.float32)
    cf[:, 0, :] = 1.0
    cf[:, 1, :] = np.eye(128, dtype=np.float32)
    cf[:, 2, :] = np.triu(np.ones((128, 128), np.float32))
    cf[:, 3, :] = np.tril(np.ones((128, 128), np.float32))
    cf[:, 4, :] = np.tril(np.ones((128, 128), np.float32), -1)
    cf[:, 5, :] = np.triu(np.ones((128, 128), np.float32), 1)
    for i_ in range(64):
        cf[2 * i_ + 1, 6, 2 * i_] = -1.0
        cf[2 * i_, 6, 2 * i_ + 1] = 1.0
    return cf


def core_plan(cfg, n_prompt, n_sample):
    plan = [("s", b) for b in range(n_sample)]
    per = cfg.NSEG
    for s0 in range(0, n_prompt, per):
        plan.append(("p", s0))
    return plan


def make_in_maps(cfg, inputs, plan):
    c = cfg
    names = ["w_mod", "cv_w_pw1", "cv_w_pw2", "ffn_w_up", "ffn_w_down"]
    if c.NSSD:
        names += ["ssd_w_in", "ssd_w_out"]
    if c.NATT:
        names += ["attn_w_qkv", "attn_w_o"]
    shared = {k: np.ascontiguousarray(np.asarray(inputs[k], np.float32)) for k in names}
    cf = const_f()
    maps = []
    for kind, idx in plan:
        if kind == "s":
            x = np.asarray(inputs["x_sample"][idx], np.float32)
            cond = np.asarray(inputs["c"][idx], np.float32)
            flags = [1.0, 0.0, 0.0]
        else:
            x = np.asarray(inputs["x_prompt"][idx:idx + c.NSEG], np.float32).reshape(c.T, c.D)
            cond = np.asarray(inputs["c_ctx"], np.float32)
            flags = [0.0, NEG, NEG]
        sp = small_layout(c, inputs, cond=cond, flags=flags)
        m = dict(shared)
        m["xin"] = fm(x)
        if c.NSSD:
            if kind == "s":
                st = np.asarray(inputs["state_ssd"][idx, 0], np.float32)
                m["h0t"] = np.ascontiguousarray(st.reshape(2, c.DI, 128).transpose(0, 2, 1))
            else:
                m["h0t"] = np.zeros((2, 128, c.DI), np.float32)
        if c.NATT:
            rope = np.zeros((2, 128, c.T), np.float32)
            if kind == "s":
                rows = c.T // c.GRID_W
                pos_row = np.repeat(np.arange(rows, dtype=np.float32), c.GRID_W)
                pos_col = np.tile(np.arange(c.GRID_W, dtype=np.float32), rows)
                inv = (np.float32(10000.0) ** (-np.arange(32, dtype=np.float32) / np.float32(32))).astype(np.float32)
                ang = np.concatenate([pos_row[:, None] * inv, pos_col[:, None] * inv], axis=-1).astype(np.float32)
                rope[0] = np.repeat(np.cos(ang).T, 2, axis=0)
                rope[1] = np.repeat(np.sin(ang).T, 2, axis=0)
                ck = np.asarray(inputs["cache_k"][idx, 0], np.float32)
                m["cachek"] = np.ascontiguousarray(ck.transpose(1, 2, 0))
                m["cachev"] = np.ascontiguousarray(np.asarray(inputs["cache_v"][idx, 0], np.float32).reshape(c.PAST, c.NKV * 128))
            else:
                rope[0] = 1.0
                m["cachek"] = np.zeros((c.NKV, 128, c.PAST), np.float32)
                m["cachev"] = np.zeros((c.PAST, c.NKV * 128), np.float32)
            m["rope"] = rope
        m["small"] = sp.build()
        m["constf"] = cf
        maps.append(m)
    return maps


def run(cfg, inputs, n_cores=None, n_layers=None, trace=False, debug=False):
    c = cfg
    n_prompt = inputs["x_prompt"].shape[0]
    n_sample = inputs["x_sample"].shape[0]
    plan = core_plan(c, n_prompt, n_sample)
    n_cores = len(plan) if n_cores is None else n_cores
    maps = make_in_maps(c, inputs, plan)
    in_maps = [maps[k % len(maps)] for k in range(n_cores)]
    b = Builder(c, n_layers=n_layers, debug=debug)
    import os
    if os.environ.get("STOP_AT"):
        b.stop_at = os.environ["STOP_AT"]
    nc = b.build()
    res = run_bass_kernel_spmd(nc, in_maps, core_ids=list(range(n_cores)), trace=trace)
    if debug:
        global DBG_RES
        DBG_RES = res
    if trace:
        print("exec_time_ns", res.exec_time_ns)
    yp = np.zeros((n_prompt, c.SEG, c.D), np.float32)
    ys = np.zeros((n_sample, c.T, c.D), np.float32)
    nst = np.zeros((n_prompt, 1, 2, c.SH, 64, 128), np.float32) if c.NSSD else None
    nk = np.zeros((n_prompt, 1, c.SEG, c.NKV, 128), np.float32) if c.NATT else None
    nv = np.zeros((n_prompt, 1, c.SEG, c.NKV, 128), np.float32) if c.NATT else None
    for k, (kind, idx) in enumerate(plan):
        r = res.results[k]
        y = unfm(np.asarray(r["yout"], np.float32))
        if kind == "s":
            ys[idx] = y
        else:
            yp[idx:idx + c.NSEG] = y.reshape(c.NSEG, c.SEG, c.D)
            if c.NSSD and (n_layers is None or n_layers >= 2):
                ns = np.asarray(r["newst"], np.float32)
                nst[idx:idx + c.NSEG, 0] = ns.transpose(0, 1, 3, 2).reshape(c.NSEG, 2, c.SH, 64, 128)
            if c.NATT and (n_layers is None or n_layers >= 3):
                k_ = np.asarray(r["newk"], np.float32)
                nk[idx:idx + c.NSEG, 0] = k_.transpose(2, 0, 1).reshape(c.NSEG, c.SEG, c.NKV, 128)
                v_ = np.asarray(r["newv"], np.float32)
                nv[idx:idx + c.NSEG, 0] = v_.reshape(c.NSEG, c.SEG, c.NKV, 128)
    return yp, ys, nst, nk, nv


N_CORES = 4


def kernel(**inputs):
    cfg = Cfg()
    inputs = {k: np.asarray(v) for k, v in inputs.items()}
    yp, ys, nst, nk, nv = run(cfg, inputs, n_cores=N_CORES)
    return (yp, ys, nst, nk, nv)
```

```python
import numpy as np
import ml_dtypes
from contextlib import ExitStack
import concourse.bass as bass
import concourse.mybir as mybir
from concourse.bass_utils import run_bass_kernel_spmd

F32 = mybir.dt.float32
BF16 = mybir.dt.bfloat16
AF = mybir.ActivationFunctionType
ALU = mybir.AluOpType
EPS = 1e-6
NEG = -30000.0


class Cfg:
    def __init__(self, **kw):
        self.D = 2048
        self.T = 2048
        self.SEG = 256
        self.DFF = 5632
        self.CW = 31
        self.DEPTH = 4
        self.DI = 4096
        self.SH = 64
        self.SG = 8
        self.SCW = 5
        self.NH = 16
        self.NKV = 4
        self.PAST = 512
        self.GRID_W = 64
        self.NCORES = 8
        for k, v in kw.items():
            setattr(self, k, v)
        self.KC = self.D // 128
        self.FC = self.DFF // 128
        self.NSEG = self.T // self.SEG
        self.TB = max(1, self.T // 512)
        self.TBW = min(512, self.T)
        self.NCONV = (self.DEPTH + 2) // 3
        self.NSSD = (self.DEPTH + 1) // 3
        self.NATT = self.DEPTH // 3


class Cell:
    __slots__ = ("w", "r", "excl")

    def __init__(self, excl=False):
        self.w = None
        self.r = {}
        self.excl = excl


class Agent:
    __slots__ = ("sem", "step", "count")

    def __init__(self, sem, step):
        self.sem = sem
        self.step = step
        self.count = 0


class Prog:
    def __init__(self, nc, es, n_lanes=32):
        self.nc = nc
        self.es = es
        self.eng = {"pe": nc.tensor, "act": nc.scalar, "dve": nc.vector, "pool": nc.gpsimd, "sp": nc.sync}
        self.agents = {}
        for n in self.eng:
            self.agents[n] = Agent(es.enter_context(nc.semaphore("sem_" + n)), 1)
        self.n_lanes = n_lanes
        for i in range(n_lanes):
            self.agents["L%d" % i] = Agent(es.enter_context(nc.semaphore("sem_L%d" % i)), 16)
        self.waited = {n: {} for n in self.eng}
        self.lane_rr = 0
        self.lane_rr_pool = 0
        self.ninstr = 0

    def cell(self):
        return Cell()

    def _wait(self, e, agent, idx):
        if idx <= 0 or self.waited[e].get(agent, 0) >= idx:
            return
        a = self.agents[agent]
        self.eng[e].wait_ge(a.sem, idx * a.step)
        self.waited[e][agent] = idx

    @staticmethod
    def _flat(cells):
        out = []
        for c in cells:
            if isinstance(c, (list, tuple)):
                out.extend(Prog._flat(c))
            else:
                out.append(c)
        return out

    def handoff(self, from_cells, to_cells):
        r = {}
        for c in self._flat(from_cells):
            if c.w is not None and r.get(c.w[0], 0) < c.w[1]:
                r[c.w[0]] = c.w[1]
            for a, i in c.r.items():
                if r.get(a, 0) < i:
                    r[a] = i
        for t in self._flat(to_cells):
            t.w = None
            t.r = dict(r)

    def _deps(self, e, reads, writes, skip_self):
        need = {}
        for c in reads:
            if c.w is not None:
                if need.get(c.w[0], 0) < c.w[1]:
                    need[c.w[0]] = c.w[1]
            if c.excl:
                for a, i in c.r.items():
                    if a != e and need.get(a, 0) < i:
                        need[a] = i
        for c in writes:
            if c.w is not None:
                if need.get(c.w[0], 0) < c.w[1]:
                    need[c.w[0]] = c.w[1]
            for a, i in c.r.items():
                if need.get(a, 0) < i:
                    need[a] = i
        for a, i in need.items():
            if a == e and skip_self:
                continue
            self._wait(e, a, i)

    def op(self, e, fn, reads=(), writes=(), inc=True):
        reads = self._flat(reads)
        writes = self._flat(writes)
        self._deps(e, reads, writes, skip_self=(e == "pe"))
        ins = fn(self.eng[e])
        A = self.agents[e]
        if inc:
            A.count += 1
            ins.then_inc(A.sem, 1)
            idx = A.count
        else:
            idx = A.count + 1
        for c in reads:
            if c.r.get(e, 0) < idx:
                c.r[e] = idx
        for c in writes:
            c.w = (e, idx)
            c.r = {}
        self.ninstr += 1
        return ins

    def dma(self, q, out, in_, reads=(), writes=()):
        if q == "pool":
            lane = "L%d" % (self.n_lanes - 8 + self.lane_rr_pool)
            self.lane_rr_pool = (self.lane_rr_pool + 1) % 8
        else:
            lane = "L%d" % self.lane_rr
            self.lane_rr = (self.lane_rr + 1) % (self.n_lanes - 8)
        L = self.agents[lane]
        reads = self._flat(reads)
        writes = self._flat(writes)
        self._deps(q, reads, writes, skip_self=False)
        self._wait(q, lane, L.count)
        ins = self.eng[q].dma_start(out=out, in_=in_)
        L.count += 1
        ins.then_inc(L.sem, 16)
        for c in reads:
            c.r[lane] = L.count
        for c in writes:
            c.w = (lane, L.count)
            c.r = {}
        self.ninstr += 1
        return ins

    def finish(self):
        for i in range(self.n_lanes):
            L = self.agents["L%d" % i]
            self._wait("sp", "L%d" % i, L.count)
        for n in ("pe", "act", "dve", "pool"):
            self._wait("sp", n, self.agents[n].count)


def pp(v):
    v = np.asarray(v, np.float32)
    return np.ascontiguousarray(v.reshape(-1, 128).T)


def fm(x):
    T, C = x.shape
    return np.ascontiguousarray(x.T.reshape(C // 128, 128, T))


def unfm(y):
    c, p, T = y.shape
    return np.ascontiguousarray(y.reshape(c * p, T).T)


class SmallPack:
    def __init__(self):
        self.off = {}
        self.arrs = []
        self.n = 0

    def add(self, name, arr):
        arr = np.asarray(arr, np.float32)
        assert arr.shape[0] == 128 and arr.ndim == 2, (name, arr.shape)
        self.off[name] = (self.n, arr.shape[1])
        self.arrs.append(arr)
        self.n += arr.shape[1]

    def build(self):
        return np.ascontiguousarray(np.concatenate(self.arrs, axis=1))


def small_layout(cfg, inputs=None, cond=None, flags=None):
    c = cfg
    sp = SmallPack()
    sp.add("cond", pp(cond) if cond is not None else np.zeros((128, c.KC), np.float32))
    fl = np.zeros((128, 8), np.float32)
    if flags is not None:
        fl[:, :len(flags)] = np.asarray(flags, np.float32)[None, :]
    sp.add("flags", fl)

    def g(name, shape):
        if inputs is None:
            return np.zeros(shape, np.float32)
        return np.asarray(inputs[name], np.float32)

    for i in range(c.DEPTH):
        sp.add("b_mod%d" % i, pp(g("b_mod", (c.DEPTH, 6 * c.D))[i]))
        npre = g("norm_pre", (c.DEPTH, 2, c.D))[i]
        npo = g("norm_post", (c.DEPTH, 2, c.D))[i]
        sp.add("npre%d_0" % i, pp(npre[0]))
        sp.add("npre%d_1" % i, pp(npre[1]))
        sp.add("npost%d_0" % i, pp(npo[0]))
        sp.add("npost%d_1" % i, pp(npo[1]))
        sp.add("ffn_b%d" % i, pp(g("ffn_b_dw", (c.DEPTH, 2 * c.DFF))[i]))
        wdw = g("ffn_w_dw", (c.DEPTH, 3, 2 * c.DFF))[i]
        for k in range(3):
            sp.add("ffn_w%d_%d" % (i, k), pp(wdw[k]))
    for j in range(c.NCONV):
        sp.add("cv_b1_%d" % j, pp(g("cv_b_pw1", (c.NCONV, 2 * c.D))[j]))
        wdw = g("cv_w_dw", (c.NCONV, c.CW, c.D))[j]
        for k in range(c.CW):
            sp.add("cv_w%d_%d" % (j, k), pp(wdw[k]))
        sp.add("cv_bdw_%d" % j, pp(g("cv_b_dw", (c.NCONV, c.D))[j]))
        sp.add("cv_lng_%d" % j, pp(g("cv_ln_g", (c.NCONV, c.D))[j]))
        sp.add("cv_lnb_%d" % j, pp(g("cv_ln_b", (c.NCONV, c.D))[j]))
        sp.add("cv_b2_%d" % j, pp(g("cv_b_pw2", (c.NCONV, c.D))[j]))
    for j in range(c.NATT):
        sp.add("att_qn%d" % j, g("attn_q_norm", (c.NATT, 128))[j].reshape(128, 1))
        sp.add("att_kn%d" % j, g("attn_k_norm", (c.NATT, 128))[j].reshape(128, 1))
    ncd = c.DI + 2 * c.SG * 128
    for j in range(c.NSSD):
        wc = g("ssd_w_conv", (c.NSSD, c.SCW, ncd))[j]
        for k in range(c.SCW):
            sp.add("ssd_cw%d_%d" % (j, k), pp(wc[k]))
        sp.add("ssd_cb%d" % j, pp(g("ssd_b_conv", (c.NSSD, ncd))[j]))
        sp.add("ssd_ng%d" % j, pp(g("ssd_norm_g", (c.NSSD, c.DI))[j]))
        sp.add("ssd_dch%d" % j, pp(np.repeat(g("ssd_d", (c.NSSD, c.SH))[j], 64)))
        sp.add("ssd_dtb%d" % j, np.broadcast_to(g("ssd_dt_bias", (c.NSSD, 2, c.SH))[j].reshape(1, -1), (128, 2 * c.SH)))
        sp.add("ssd_alog%d" % j, np.broadcast_to(g("ssd_a_log", (c.NSSD, 2, c.SH))[j].reshape(1, -1), (128, 2 * c.SH)))
    return sp


class Builder:
    def __init__(self, cfg, debug=False, n_layers=None):
        self.cfg = cfg
        self.debug = debug
        self.n_layers = cfg.DEPTH if n_layers is None else n_layers
        self.sp_layout = small_layout(cfg)

    def sv(self, name, col=None, n=1):
        o, w = self.sp_layout.off[name]
        if col is None:
            return self.SV[:, o:o + w]
        return self.SV[:, o + col:o + col + n]

    def work(self):
        i = self.wk_rr
        self.wk_rr = (self.wk_rr + 1) % len(self.WK)
        return self.WK[i], self.WKc[i]

    def wslot(self):
        i = self.wb_rr
        self.wb_rr = (self.wb_rr + 1) % len(self.WB)
        return self.WB[i], self.WBc[i]

    def psum(self):
        i = self.ps_rr
        self.ps_rr = (self.ps_rr + 1) % 2
        return self.PS[i], self.PSc[i]

    def psum_bank(self):
        c = self.cfg
        k = self.psb_rr
        self.psb_rr = (self.psb_rr + 1) % 8
        i, b = k // 4, k % 4
        return self.PS[i][:, b * 512:(b + 1) * 512], self.PSc[i][b]

    def load_w(self, w_ap, r0, nrows, c0, ncols):
        P = self.P
        slot, cell = self.wslot()
        kcn = nrows // 128
        assert kcn * ncols <= self.WSLOT
        view = slot[:, 0:kcn * ncols].rearrange("p (k m) -> p k m", k=kcn)
        src = w_ap[r0:r0 + nrows, c0:c0 + ncols].rearrange("(k p) m -> p k m", p=128)
        P.dma("pool", view, src, writes=[cell])
        return view, cell

    def matmul_acc(self, ps, pscell, wview, wcell, mcol, kcn, rhs_fn, first=True, last=True, kc0=0, ktot=None):
        P = self.P
        c = self.cfg
        ktot = kcn if ktot is None else ktot
        for kc in range(kcn):
            for tb in range(c.TB):
                rhs, rcells = rhs_fn(kc, tb)
                st = first and (kc0 + kc == 0)
                sp_ = last and (kc0 + kc == ktot - 1)
                fin = (kc == kcn - 1) and (tb == c.TB - 1)
                P.op("pe",
                     lambda e, o=ps[:, tb * c.TBW:(tb + 1) * c.TBW], l=wview[:, kc, mcol:mcol + 128], r=rhs, st=st, sp_=sp_:
                     e.matmul(o, lhsT=l, rhs=r, start=st, stop=sp_),
                     reads=[wcell] + list(rcells), writes=[pscell], inc=fin)

    def hb_rhs(self, kc, tb):
        c = self.cfg
        return self.HB[:, kc, tb * c.TBW:(tb + 1) * c.TBW], [self.HBc]

    def colsum_bcast(self, acc, acc_cell):
        P = self.P
        c = self.cfg
        ps, pc = self.psum()
        for tb in range(c.TB):
            P.op("pe", lambda e, o=ps[:, tb * c.TBW:(tb + 1) * c.TBW], r=acc[:, tb * c.TBW:(tb + 1) * c.TBW]:
                 e.matmul(o, lhsT=self.ONES_F[:], rhs=r, start=True, stop=True),
                 reads=[acc_cell, self.constc], writes=[pc], inc=(tb == c.TB - 1))
        return ps, pc

    def rstd_from_sq(self, acc, acc_cell, out, out_cell, n):
        P = self.P
        c = self.cfg
        ps, pc = self.colsum_bcast(acc, acc_cell)
        P.op("dve", lambda e: e.tensor_scalar(out=out[:, 0:c.T], in0=ps[:, 0:c.T], scalar1=1.0 / n, scalar2=EPS,
                                              op0=ALU.mult, op1=ALU.add), reads=[pc], writes=[out_cell])
        P.op("act", lambda e: e.activation(out=out[:, 0:c.T], in_=out[:, 0:c.T], func=AF.Sqrt),
             reads=[out_cell], writes=[out_cell])
        P.op("dve", lambda e: e.reciprocal(out=out[:, 0:c.T], in_=out[:, 0:c.T]), reads=[out_cell], writes=[out_cell])

    def sq_accum(self, t, tcell, acc, acc_cell, first):
        P = self.P
        c = self.cfg
        if first:
            P.op("act", lambda e: e.activation(out=acc[:, 0:c.T], in_=t, func=AF.Square), reads=[tcell], writes=[acc_cell])
        else:
            sq, sqc = self.work()
            P.op("act", lambda e: e.activation(out=sq[:, 0:c.T], in_=t, func=AF.Square), reads=[tcell], writes=[sqc])
            P.op("dve", lambda e: e.tensor_tensor(out=acc[:, 0:c.T], in0=acc[:, 0:c.T], in1=sq[:, 0:c.T], op=ALU.add),
                 reads=[sqc, acc_cell], writes=[acc_cell])

    def modulation(self, i):
        P = self.P
        c = self.cfg
        w = self.W["w_mod"][i]
        ps, pc = self.psum()
        noc = 6 * c.KC
        cw = min(256, 6 * c.D)
        for cb in range(6 * c.D // cw):
            view, wc = self.load_w(w, 0, c.D, cb * cw, cw)
            for ml in range(cw // 128):
                oc = cb * (cw // 128) + ml
                for kc in range(c.KC):
                    P.op("pe", lambda e, o=ps[:, oc:oc + 1], l=view[:, kc, ml * 128:(ml + 1) * 128], r=self.SC[:, kc:kc + 1],
                         st=(kc == 0), sp_=(kc == c.KC - 1): e.matmul(o, lhsT=l, rhs=r, start=st, stop=sp_),
                         reads=[wc, self.constc], writes=[pc], inc=(kc == c.KC - 1))
        mod = self.MOD[:, i, :]
        P.op("dve", lambda e: e.tensor_tensor(out=mod, in0=ps[:, 0:noc], in1=self.sv("b_mod%d" % i), op=ALU.add),
             reads=[pc, self.constc], writes=[self.MODc])
        for s in range(2):
            sc = self.MOD[:, i, (3 * s + 1) * c.KC:(3 * s + 2) * c.KC]
            ga = self.MOD[:, i, (3 * s + 2) * c.KC:(3 * s + 3) * c.KC]
            P.op("dve", lambda e, sc=sc, s=s: e.scalar_tensor_tensor(out=sc, in0=sc, scalar=1.0, in1=self.sv("npre%d_%d" % (i, s)),
                                                                   op0=ALU.add, op1=ALU.mult),
                 reads=[self.MODc, self.constc], writes=[self.MODc])
            P.op("dve", lambda e, ga=ga, s=s: e.tensor_tensor(out=ga, in0=ga, in1=self.sv("npost%d_%d" % (i, s)), op=ALU.mult),
                 reads=[self.MODc, self.constc], writes=[self.MODc])

    def modv(self, i, which, col):
        c = self.cfg
        idx = {"sh_m": 0, "sc_m": 1, "ga_m": 2, "sh_f": 3, "sc_f": 4, "ga_f": 5}[which]
        return self.MOD[:, i, idx * c.KC + col: idx * c.KC + col + 1]

    def prenorm(self, i, s):
        P = self.P
        c = self.cfg
        sc = "sc_m" if s == 0 else "sc_f"
        sh = "sh_m" if s == 0 else "sh_f"
        for kc in range(c.KC):
            yt, yc = self.work()
            P.dma("sp", yt[:, 0:c.T], self.Y[kc], reads=[self.Yc[kc]], writes=[yc])
            P.op("dve", lambda e, yt=yt, kc=kc: e.scalar_tensor_tensor(out=yt[:, 0:c.T], in0=yt[:, 0:c.T], scalar=self.modv(i, sc, kc),
                                                                     in1=self.RSTD[:, 0:c.T], op0=ALU.mult, op1=ALU.mult),
                 reads=[yc, self.MODc, self.RSTDc], writes=[yc])
            P.op("act", lambda e, yt=yt, kc=kc: e.activation(out=self.HB[:, kc, :], in_=yt[:, 0:c.T], func=AF.Identity,
                                                            bias=self.modv(i, sh, kc), scale=1.0),
                 reads=[yc, self.MODc], writes=[self.HBc])

    def post(self, i, s, last):
        P = self.P
        c = self.cfg
        ga = "ga_m" if s == 0 else "ga_f"
        self.rstd_from_sq(self.SQO, self.SQOc, self.RSTDO, self.RSTDOc, c.D)
        for kc in range(c.KC):
            ot, oc = self.work()
            yt, yc = self.work()
            P.dma("sp", ot[:, 0:c.T], self.O[kc], reads=[self.Oc[kc]], writes=[oc])
            P.dma("sp", yt[:, 0:c.T], self.Y[kc], reads=[self.Yc[kc]], writes=[yc])
            P.op("dve", lambda e, ot=ot, kc=kc: e.scalar_tensor_tensor(out=ot[:, 0:c.T], in0=ot[:, 0:c.T], scalar=self.modv(i, ga, kc),
                                                                     in1=self.RSTDO[:, 0:c.T], op0=ALU.mult, op1=ALU.mult),
                 reads=[oc, self.MODc, self.RSTDOc], writes=[oc])
            P.op("dve", lambda e, ot=ot, yt=yt: e.tensor_tensor(out=yt[:, 0:c.T], in0=yt[:, 0:c.T], in1=ot[:, 0:c.T], op=ALU.add),
                 reads=[oc, yc], writes=[yc])
            dst = self.YOUT[kc] if last else self.Y[kc]
            P.dma("sp", dst, yt[:, 0:c.T], reads=[yc], writes=[self.Yc[kc]])
            if not last:
                self.sq_accum(yt[:, 0:c.T], yc, self.SQY, self.SQYc, first=(kc == 0))
        if not last:
            self.rstd_from_sq(self.SQY, self.SQYc, self.RSTD, self.RSTDc, c.D)

    def out_epilogue(self, ps, pc, m, bias_ap):
        P = self.P
        c = self.cfg
        ot, oc = self.work()
        if bias_ap is None:
            P.op("act", lambda e: e.activation(out=ot[:, 0:c.T], in_=ps[:, 0:c.T], func=AF.Identity), reads=[pc], writes=[oc])
        else:
            P.op("act", lambda e: e.activation(out=ot[:, 0:c.T], in_=ps[:, 0:c.T], func=AF.Identity, bias=bias_ap, scale=1.0),
                 reads=[pc, self.constc], writes=[oc])
        P.dma("sp", self.O[m], ot[:, 0:c.T], reads=[oc], writes=[self.Oc[m]])
        self.sq_accum(ot[:, 0:c.T], oc, self.SQO, self.SQOc, first=(m == 0))

    def halo_fill(self, eng, buf, cell, h):
        P = self.P
        c = self.cfg
        S = c.SEG
        n = c.NSEG
        fS = self.sv("flags", 0)
        if n > 1:
            P.op(eng, lambda e: e.tensor_scalar(out=buf[:, 1:n, 0:h], in0=buf[:, 0:n - 1, S:S + h], scalar1=fS, scalar2=None, op0=ALU.mult),
                 reads=[cell, self.constc], writes=[cell])
            P.op(eng, lambda e: e.tensor_scalar(out=buf[:, 0:n - 1, S + h:S + 2 * h], in0=buf[:, 1:n, h:2 * h], scalar1=fS, scalar2=None, op0=ALU.mult),
                 reads=[cell, self.constc], writes=[cell])

    def ffn(self, i, last):
        P = self.P
        c = self.cfg
        S = c.SEG
        n = c.NSEG
        self.prenorm(i, 1)
        self.a2_switch(self.ABc)
        wup = self.W["ffn_w_up"][i]
        PW = 2
        views = {}
        for m in range(c.FC):
            if m % PW == 0:
                npair = min(PW, c.FC - m)
                va = self.load_w(wup, 0, c.D, m * 128, npair * 128)
                vg = self.load_w(wup, 0, c.D, c.DFF + m * 128, npair * 128)
            ml = m % PW
            res = []
            for half, (view, wc) in enumerate((va, vg)):
                ps, pc = self.psum()
                self.matmul_acc(ps, pc, view, wc, ml * 128, c.KC, self.hb_rhs)
                ch = half * c.FC + m
                pad, padc = self.work()
                pv = pad[:, 0:n * (S + 2)].rearrange("p (s w) -> p s w", s=n)
                P.op("act", lambda e, pv=pv, ps=ps: e.activation(out=pv[:, :, 1:S + 1], in_=ps[:, 0:c.T].rearrange("p (s w) -> p s w", s=n),
                                                               func=AF.Identity), reads=[pc], writes=[padc])
                P.op("dve", lambda e, pv=pv: e.memset(pv[:, 0, 0:1], 0.0), reads=[padc], writes=[padc])
                P.op("dve", lambda e, pv=pv: e.memset(pv[:, n - 1, S + 1:S + 2], 0.0), reads=[padc], writes=[padc])
                self.halo_fill("dve", pv, padc, 1)
                cv, cvc = self.work()
                cvv = cv[:, 0:c.T].rearrange("p (s w) -> p s w", s=n)
                P.op("dve", lambda e, pv=pv, cvv=cvv, ch=ch: e.tensor_scalar(out=cvv, in0=pv[:, :, 0:S], scalar1=self.sv("ffn_w%d_0" % i, ch),
                                                                           scalar2=self.sv("ffn_b%d" % i, ch), op0=ALU.mult, op1=ALU.add),
                     reads=[padc, self.constc], writes=[cvc])
                for k in (1, 2):
                    P.op("dve", lambda e, pv=pv, cvv=cvv, ch=ch, k=k: e.scalar_tensor_tensor(out=cvv, in0=pv[:, :, k:k + S],
                                                                                           scalar=self.sv("ffn_w%d_%d" % (i, k), ch),
                                                                                           in1=cvv, op0=ALU.mult, op1=ALU.add),
                         reads=[padc, cvc, self.constc], writes=[cvc])
                res.append((cv, cvc))
            (ca, cac), (cg, cgc) = res
            P.op("act", lambda e, cg=cg: e.activation(out=cg[:, 0:c.T], in_=cg[:, 0:c.T], func=AF.Silu), reads=[cgc], writes=[cgc])
            ab, abc = self.abuf()
            P.op("dve", lambda e, ca=ca, cg=cg, ab=ab: e.tensor_tensor(out=ab[:, 0:c.T], in0=ca[:, 0:c.T], in1=cg[:, 0:c.T], op=ALU.mult),
                 reads=[cac, cgc], writes=[abc])
            P.dma("sp", self.ACTD[m], ab[:, 0:c.T], reads=[abc], writes=[self.ACTDc[m]])
        wdn = self.W["ffn_w_down"][i]
        KG = 11 if c.FC % 11 == 0 else c.FC
        for m0 in range(0, c.KC, 2):
            ms = [m for m in (m0, m0 + 1) if m < c.KC]
            pss = [self.psum() for _ in ms]
            for g0 in range(0, c.FC, KG):
                wv = [self.load_w(wdn, g0 * 128, KG * 128, m * 128, 128) for m in ms]
                for kk in range(KG):
                    kc = g0 + kk
                    slot = self.hb_slot()
                    P.dma("sp", self.HB[:, slot, :], self.ACTD[kc], reads=[self.ACTDc[kc]], writes=[self.HBsc[slot]])
                    for (ps, pc), (view, wc) in zip(pss, wv):
                        for tb in range(c.TB):
                            P.op("pe", lambda e, o=ps[:, tb * c.TBW:(tb + 1) * c.TBW], l=view[:, kk, 0:128],
                                 r=self.HB[:, slot, tb * c.TBW:(tb + 1) * c.TBW], st=(kc == 0), sp_=(kc == c.FC - 1):
                                 e.matmul(o, lhsT=l, rhs=r, start=st, stop=sp_),
                                 reads=[wc, self.HBsc[slot]], writes=[pc], inc=(tb == c.TB - 1))
            for m, (ps, pc) in zip(ms, pss):
                self.out_epilogue(ps, pc, m, None)
        self.hb_release()
        self.post(i, 1, last)

    def hb_slot(self):
        if not self.hb_slot_mode:
            for k in range(self.cfg.KC):
                self.HBsc[k].w = self.HBc.w
                self.HBsc[k].r = dict(self.HBc.r)
            self.hb_slot_mode = True
            self.hb_rr = 0
        s = self.hb_rr
        self.hb_rr = (self.hb_rr + 1) % self.cfg.KC
        return s

    def hb_release(self):
        if self.hb_slot_mode:
            r = {}
            w = self.HBc.w
            for k in range(self.cfg.KC):
                cl = self.HBsc[k]
                for a, i in cl.r.items():
                    if r.get(a, 0) < i:
                        r[a] = i
                if cl.w is not None:
                    if r.get(cl.w[0], 0) < cl.w[1]:
                        r[cl.w[0]] = cl.w[1]
            self.HBc.r = r
            self.hb_slot_mode = False

    def a2_switch(self, new_cells):
        self.P.handoff(self.A2cells, new_cells)
        self.A2cells = list(new_cells)

    def a2_conv_view(self):
        self.a2_switch([self.UPc, self.DGc])
        self.P.op("dve", lambda e: e.memset(self.UPF, 0.0), writes=[self.UPc])

    def abuf(self):
        i = self.ab_rr
        self.ab_rr = (self.ab_rr + 1) % len(self.AB)
        return self.AB[i], self.ABc[i]

    def conformer(self, i, j):
        P = self.P
        c = self.cfg
        S = c.SEG
        n = c.NSEG
        H = (c.CW - 1) // 2
        PADW = S + 2 * H
        self.prenorm(i, 0)
        self.a2_conv_view()
        w1 = self.W["cv_w_pw1"][j]
        PW = 2
        for m in range(c.KC):
            if m % PW == 0:
                npair = min(PW, c.KC - m)
                va = self.load_w(w1, 0, c.D, m * 128, npair * 128)
                vg = self.load_w(w1, 0, c.D, c.D + m * 128, npair * 128)
            ml = m % PW
            psa, pca = self.psum()
            self.matmul_acc(psa, pca, va[0], va[1], ml * 128, c.KC, self.hb_rhs)
            psg, pcg = self.psum()
            self.matmul_acc(psg, pcg, vg[0], vg[1], ml * 128, c.KC, self.hb_rhs)
            at, atc = self.work()
            gt, gtc = self.work()
            P.op("act", lambda e, at=at, psa=psa, m=m: e.activation(out=at[:, 0:c.T], in_=psa[:, 0:c.T], func=AF.Identity,
                                                                  bias=self.sv("cv_b1_%d" % j, m), scale=1.0),
                 reads=[pca, self.constc], writes=[atc])
            P.op("act", lambda e, gt=gt, psg=psg, m=m: e.activation(out=gt[:, 0:c.T], in_=psg[:, 0:c.T], func=AF.Sigmoid,
                                                                  bias=self.sv("cv_b1_%d" % j, c.KC + m), scale=1.0),
                 reads=[pcg, self.constc], writes=[gtc])
            up = self.UP
            P.op("dve", lambda e, at=at, gt=gt: e.tensor_tensor(out=up[:, :, H:H + S], in0=at[:, 0:c.T].rearrange("p (s w) -> p s w", s=n),
                                                              in1=gt[:, 0:c.T].rearrange("p (s w) -> p s w", s=n), op=ALU.mult),
                 reads=[atc, gtc], writes=[self.UPc])
            self.halo_fill("dve", up, self.UPc, H)
            for k in range(c.CW):
                P.op("dve", lambda e, k=k, m=m: e.tensor_scalar(out=self.DG[:, k, :], in0=self.IDB[:], scalar1=self.sv("cv_w%d_%d" % (j, k), m),
                                                              scalar2=None, op0=ALU.mult),
                     reads=[self.constc, self.DGc], writes=[self.DGc])
            psc, pcc = self.psum()
            for s_ in range(n):
                for k in range(c.CW):
                    P.op("pe", lambda e, s_=s_, k=k, psc=psc: e.matmul(psc[:, s_ * S:(s_ + 1) * S], lhsT=self.DG[:, k, :], rhs=up[:, s_, k:k + S],
                                                                    start=(k == 0), stop=(k == c.CW - 1)),
                         reads=[self.DGc, self.UPc], writes=[pcc], inc=(s_ == n - 1 and k == c.CW - 1))
            ct, ctc = self.work()
            P.op("act", lambda e, ct=ct, psc=psc, m=m: e.activation(out=ct[:, 0:c.T], in_=psc[:, 0:c.T], func=AF.Identity,
                                                                  bias=self.sv("cv_bdw_%d" % j, m), scale=1.0),
                 reads=[pcc, self.constc], writes=[ctc])
            P.dma("sp", self.CV[m], ct[:, 0:c.T], reads=[ctc], writes=[self.CVc[m]])
            if m == 0:
                P.op("dve", lambda e, ct=ct: e.tensor_copy(out=self.SQY[:, 0:c.T], in_=ct[:, 0:c.T]), reads=[ctc], writes=[self.SQYc])
            else:
                P.op("dve", lambda e, ct=ct: e.tensor_tensor(out=self.SQY[:, 0:c.T], in0=self.SQY[:, 0:c.T], in1=ct[:, 0:c.T], op=ALU.add),
                     reads=[ctc, self.SQYc], writes=[self.SQYc])
            self.sq_accum(ct[:, 0:c.T], ctc, self.SQO, self.SQOc, first=(m == 0))
        psm, pcm = self.colsum_bcast(self.SQY, self.SQYc)
        mean, meanc = self.SQY, self.SQYc
        P.op("act", lambda e: e.activation(out=mean[:, 0:c.T], in_=psm[:, 0:c.T], func=AF.Identity, scale=1.0 / c.D),
             reads=[pcm], writes=[meanc])
        psq, pcq = self.colsum_bcast(self.SQO, self.SQOc)
        var, varc = self.SQO, self.SQOc
        msq, msqc = self.work()
        P.op("dve", lambda e: e.tensor_tensor(out=msq[:, 0:c.T], in0=mean[:, 0:c.T], in1=mean[:, 0:c.T], op=ALU.mult),
             reads=[meanc], writes=[msqc])
        P.op("dve", lambda e: e.scalar_tensor_tensor(out=var[:, 0:c.T], in0=psq[:, 0:c.T], scalar=1.0 / c.D, in1=msq[:, 0:c.T],
                                                     op0=ALU.mult, op1=ALU.subtract), reads=[pcq, msqc], writes=[varc])
        P.op("dve", lambda e: e.tensor_scalar(out=var[:, 0:c.T], in0=var[:, 0:c.T], scalar1=EPS, scalar2=None, op0=ALU.add),
             reads=[varc], writes=[varc])
        P.op("act", lambda e: e.activation(out=var[:, 0:c.T], in_=var[:, 0:c.T], func=AF.Sqrt), reads=[varc], writes=[varc])
        P.op("dve", lambda e: e.reciprocal(out=var[:, 0:c.T], in_=var[:, 0:c.T]), reads=[varc], writes=[varc])
        rln = var
        for m in range(c.KC):
            ct, ctc = self.work()
            P.dma("sp", ct[:, 0:c.T], self.CV[m], reads=[self.CVc[m]], writes=[ctc])
            P.op("dve", lambda e, ct=ct: e.tensor_tensor(out=ct[:, 0:c.T], in0=ct[:, 0:c.T], in1=mean[:, 0:c.T], op=ALU.subtract),
                 reads=[ctc, meanc], writes=[ctc])
            P.op("dve", lambda e, ct=ct: e.tensor_tensor(out=ct[:, 0:c.T], in0=ct[:, 0:c.T], in1=rln[:, 0:c.T], op=ALU.mult),
                 reads=[ctc, varc], writes=[ctc])
            P.op("act", lambda e, ct=ct, m=m: e.activation(out=self.HB[:, m, :], in_=ct[:, 0:c.T], func=AF.Silu,
                                                         bias=self.sv("cv_lnb_%d" % j, m), scale=self.sv("cv_lng_%d" % j, m)),
                 reads=[ctc, self.constc], writes=[self.HBc])
        w2 = self.W["cv_w_pw2"][j]
        CWD = min(256, c.D)
        for m in range(c.KC):
            if (m * 128) % CWD == 0:
                v2 = self.load_w(w2, 0, c.D, m * 128, CWD)
            ps, pc = self.psum()
            self.matmul_acc(ps, pc, v2[0], v2[1], (m * 128) % CWD, c.KC, self.hb_rhs)
            self.out_epilogue(ps, pc, m, self.sv("cv_b2_%d" % j, m))
        self.post(i, 0, False)

    def ssd(self, i, j):
        P = self.P
        c = self.cfg
        S = c.SEG
        n = c.NSEG
        DI, SH, SG = c.DI, c.SH, c.SG
        HPG = SH // SG
        GW = HPG * 64
        XC = DI // 128
        XPG = GW // 128
        NCH = c.T // 128
        NDT = 2 * SH
        NF = NCH * NDT
        NXBC = XC + 2 * SG
        CPS = S // 128
        HBAT = min(4, HPG)
        w_in = self.W["ssd_w_in"][j]
        off_x = DI
        off_dt = 2 * DI + 2 * SG * 128
        fS = self.sv("flags", 0)
        IDB = self.IDB
        U, LW, SL, SU = (self.CF[:, k, :] for k in (2, 3, 4, 5))

        def b_last(ap, k):
            return ap.unsqueeze(2).to_broadcast([128, ap.shape[1], k])

        def b_mid(ap, k):
            return ap.unsqueeze(1).to_broadcast([128, k, ap.shape[1]])

        self.prenorm(i, 0)
        self.a2_conv_view()
        wdt, wdtc = self.load_w(w_in, 0, c.D, off_dt, NDT)
        psd, pcd = self.psum()
        for ch in range(NCH):
            for kc in range(c.KC):
                P.op("pe", lambda e, ch=ch, kc=kc: e.matmul(psd[:, ch * NDT:(ch + 1) * NDT], lhsT=self.HB[:, kc, ch * 128:(ch + 1) * 128],
                                                          rhs=wdt[:, kc, 0:NDT], start=(kc == 0), stop=(kc == c.KC - 1)),
                     reads=[self.HBc, wdtc], writes=[pcd], inc=(kc == c.KC - 1))
        dtw, dtwc = self.RSTDO, self.RSTDOc
        P.op("dve", lambda e: e.tensor_tensor(out=dtw[:, 0:NF].rearrange("p (c h) -> p c h", c=NCH),
                                              in0=psd[:, 0:NF].rearrange("p (c h) -> p c h", c=NCH),
                                              in1=b_mid(self.sv("ssd_dtb%d" % j), NCH), op=ALU.add),
             reads=[pcd, self.constc], writes=[dtwc])
        P.op("act", lambda e: e.activation(out=dtw[:, 0:NF], in_=dtw[:, 0:NF], func=AF.Exp), reads=[dtwc], writes=[dtwc])
        P.op("act", lambda e: e.activation(out=dtw[:, 0:NF], in_=dtw[:, 0:NF], func=AF.Ln, bias=self.CF[:, 0, 0:1], scale=1.0),
             reads=[dtwc, self.constc], writes=[dtwc])
        if getattr(self, 'stop_at', None) == 'p0a':
            self.bailed = True
            return
        CPT = c.T // 256
        for cb in range(DI // 256):
            wz, wzc = self.load_w(w_in, 0, c.D, cb * 256, 256)
            for c0 in range(0, NCH, CPT):
                ps, pc = self.psum()
                for cc in range(CPT):
                    ch = c0 + cc
                    for kc in range(c.KC):
                        P.op("pe", lambda e, ch=ch, cc=cc, kc=kc, ps=ps: e.matmul(ps[:, cc * 256:(cc + 1) * 256],
                                                                                lhsT=self.HB[:, kc, ch * 128:(ch + 1) * 128],
                                                                                rhs=wz[:, kc, 0:256], start=(kc == 0), stop=(kc == c.KC - 1)),
                             reads=[self.HBc, wzc], writes=[pc], inc=(kc == c.KC - 1))
                zt, ztc = self.work()
                ztb = zt[:, 0:c.T // 2].bitcast(BF16)
                P.op("act", lambda e, ztb=ztb, ps=ps: e.activation(out=ztb, in_=ps[:, 0:c.T], func=AF.Silu), reads=[pc], writes=[ztc])
                P.dma("sp", self.ZT[c0:c0 + CPT, :, cb * 256:(cb + 1) * 256].rearrange("c p m -> p c m"),
                      ztb.rearrange("p (c m) -> p c m", c=CPT), reads=[ztc], writes=[[self.ZTc[c0 + q][cb] for q in range(CPT)]])
        if getattr(self, 'stop_at', None) == 'p0b':
            self.bailed = True
            return
        H = (c.SCW - 1) // 2
        up5 = self.UPF[:, 0:n * (S + 2 * H)].rearrange("p (s w) -> p s w", s=n)
        for m in range(NXBC):
            if m % 2 == 0:
                wx = self.load_w(w_in, 0, c.D, off_x + m * 128, min(2, NXBC - m) * 128)
            ps, pc = self.psum()
            self.matmul_acc(ps, pc, wx[0], wx[1], (m % 2) * 128, c.KC, self.hb_rhs)
            P.op("act", lambda e, ps=ps: e.activation(out=up5[:, :, H:H + S], in_=ps[:, 0:c.T].rearrange("p (s w) -> p s w", s=n),
                                                    func=AF.Identity), reads=[pc], writes=[self.UPc])
            self.halo_fill("dve", up5, self.UPc, H)
            for k in range(c.SCW):
                P.op("dve", lambda e, k=k, m=m: e.tensor_scalar(out=self.DG[:, k, :], in0=IDB[:], scalar1=self.sv("ssd_cw%d_%d" % (j, k), m),
                                                              scalar2=None, op0=ALU.mult),
                     reads=[self.constc, self.DGc], writes=[self.DGc])
            ps2, pc2 = self.psum()
            for s_ in range(n):
                for k in range(c.SCW):
                    P.op("pe", lambda e, s_=s_, k=k, ps2=ps2: e.matmul(ps2[:, s_ * S:(s_ + 1) * S], lhsT=self.DG[:, k, :], rhs=up5[:, s_, k:k + S],
                                                                    start=(k == 0), stop=(k == c.SCW - 1)),
                         reads=[self.DGc, self.UPc], writes=[pc2], inc=(s_ == n - 1 and k == c.SCW - 1))
            xt, xtc = self.work()
            xtb = xt[:, 0:c.T // 2].bitcast(BF16)
            P.op("act", lambda e, xtb=xtb, ps2=ps2, m=m: e.activation(out=xtb, in_=ps2[:, 0:c.T], func=AF.Silu,
                                                                    bias=self.sv("ssd_cb%d" % j, m), scale=1.0),
                 reads=[pc2, self.constc], writes=[xtc])
            P.dma("sp", self.XBC[m], xtb, reads=[xtc], writes=[self.XBCc[m]])
        if getattr(self, 'stop_at', None) == 'p0c':
            self.bailed = True
            return
        self.hb_release()
        AF32 = self.ARENA[:, :].bitcast(F32)
        tl = [AF32[:, k * NF:(k + 1) * NF] for k in range(5)]
        tlc = [P.cell() for _ in range(5)]
        DT, DTA, ACS, DTE, DCH = tl
        DTc, DTAc, ACSc, DTEc, DCHc = tlc
        boff = 5 * NF * 2
        big = [self.ARENA[:, boff + k * DI: boff + (k + 1) * DI] for k in range(3)]
        bigc = [P.cell() for _ in range(3)]
        P.handoff([self.HBc], tlc + bigc)
        v3 = lambda t: t.rearrange("p (c h) -> p c h", c=NCH)
        P.op("dve", lambda e: e.tensor_copy(out=DT, in_=dtw[:, 0:NF]), reads=[dtwc], writes=[DTc])
        aw, awc = self.work()
        P.op("act", lambda e: e.activation(out=aw[:, 0:NDT], in_=self.sv("ssd_alog%d" % j), func=AF.Exp), reads=[self.constc], writes=[awc])
        P.op("dve", lambda e: e.scalar_tensor_tensor(out=v3(DTA), in0=v3(DT), scalar=-1.0, in1=b_mid(aw[:, 0:NDT], NCH),
                                                     op0=ALU.mult, op1=ALU.mult), reads=[DTc, awc], writes=[DTAc])
        if getattr(self, 'stop_at', None) == 'q1':
            self.bailed = True
            return
        NB = (NF + 511) // 512

        def fmm(lhsT, dst_ps, dst_pc):
            for b_ in range(NB):
                w_ = min(512, NF - b_ * 512)
                P.op("pe", lambda e, b_=b_, w_=w_: e.matmul(dst_ps[:, b_ * 512:b_ * 512 + w_], lhsT=lhsT, rhs=DTA[:, b_ * 512:b_ * 512 + w_],
                                                          start=True, stop=True),
                     reads=[DTAc, self.constc], writes=[dst_pc], inc=(b_ == NB - 1))
        psF, pcF = self.psum()
        fmm(U, psF, pcF)
        P.op("act", lambda e: e.activation(out=v3(ACS)[:, :, 0:SH], in_=v3(psF[:, 0:NF])[:, :, 0:SH], func=AF.Identity), reads=[pcF], writes=[ACSc])
        if getattr(self, 'stop_at', None) == 'q2':
            self.bailed = True
            return
        psB, pcB = self.psum()
        fmm(LW, psB, pcB)
        P.op("act", lambda e: e.activation(out=v3(ACS)[:, :, SH:NDT], in_=v3(psB[:, 0:NF])[:, :, SH:NDT], func=AF.Identity), reads=[pcB], writes=[ACSc])
        if getattr(self, 'stop_at', None) == 'q3':
            self.bailed = True
            return
        psT, pcT = self.psum()
        fmm(self.ONES_F, psT, pcT)
        P.op("act", lambda e: e.activation(out=DCH, in_=psT[:, 0:NF], func=AF.Exp), reads=[pcT], writes=[DCHc])
        if getattr(self, 'stop_at', None) == 'r1':
            self.bailed = True
            return
        P.op("dve", lambda e: e.tensor_tensor(out=DTE, in0=psT[:, 0:NF], in1=ACS, op=ALU.subtract), reads=[pcT, ACSc], writes=[DTEc])
        if getattr(self, 'stop_at', None) == 'r2':
            self.bailed = True
            return
        P.op("act", lambda e: e.activation(out=DTE, in_=DTE, func=AF.Exp), reads=[DTEc], writes=[DTEc])
        if getattr(self, 'stop_at', None) == 'r3':
            self.bailed = True
            return
        P.op("dve", lambda e: e.tensor_tensor(out=DTE, in0=DTE, in1=DT, op=ALU.mult), reads=[DTEc, DTc], writes=[DTEc])
        if getattr(self, 'stop_at', None) == 'r4':
            self.bailed = True
            return
        P.op("act", lambda e: e.activation(out=ACS, in_=ACS, func=AF.Exp), reads=[ACSc], writes=[ACSc])
        EACS, EACSc = ACS, ACSc
        if getattr(self, 'stop_at', None) == 'q4':
            self.bailed = True
            return
        DGD = self.A2[:, 0:XC * 128].rearrange("p (x m) -> p x m", x=XC)
        DGDc = P.cell()
        XCV = self.A2[:, XC * 128:2 * XC * 128].rearrange("p (x m) -> p x m", x=XC)
        XCVc = P.cell()
        BCV = self.A2[:, 2 * XC * 128:2 * XC * 128 + 2 * SG * 128].rearrange("p (x m) -> p x m", x=2 * SG)
        BCVc = P.cell()
        GMB = self.GMB[:, :].rearrange("p (d m) -> p d m", d=2)
        GMBc = P.cell()
        self.a2_switch([DGDc, XCVc, BCVc])
        for x in range(XC):
            P.op("dve", lambda e, x=x: e.tensor_scalar(out=DGD[:, x, :], in0=IDB[:], scalar1=self.sv("ssd_dch%d" % j, x), scalar2=None, op0=ALU.mult),
                 reads=[self.constc, DGDc], writes=[DGDc])
        WBF = self.WBALL[:, :].bitcast(F32)
        wbc = [P.cell() for _ in range(4)]
        P.handoff(self.WBc, wbc)
        stg = [self.WBALL[:, k * DI:(k + 1) * DI] for k in range(4)]

        def load_chunk_fm(ch, first, cnt):
            v, tc_ = (XCV, XCVc) if first == 0 else (BCV, BCVc)
            P.dma("sp", v, self.XBC[first:first + cnt, :, ch * 128:(ch + 1) * 128].rearrange("x p m -> p x m"),
                  reads=[self.XBCc[first:first + cnt]], writes=[tc_])
            return v, tc_

        if getattr(self, 'stop_at', None) == 'p1':
            self.bailed = True
            return
        for ch in range(NCH):
            xcv, xcc = load_chunk_fm(ch, 0, XC)
            bcv, bcc = load_chunk_fm(ch, XC, 2 * SG)
            def stage1(g):
                psx, pcx = self.psum_bank()
                for jj in range(XPG):
                    P.op("pe", lambda e, jj=jj, g=g, psx=psx: e.matmul(psx[:, jj * 128:(jj + 1) * 128], lhsT=xcv[:, g * XPG + jj, :], rhs=IDB[:],
                                                                    start=(jj == 0), stop=True),
                         reads=[xcc, self.constc], writes=[pcx], inc=(jj == XPG - 1))
                xv = psx[:, 0:GW].rearrange("p (h q) -> p h q", h=HPG)
                for d in range(2):
                    h0 = d * SH + g * HPG
                    P.op("dve", lambda e, d=d, h0=h0, xv=xv, g=g: e.tensor_tensor(out=stg[d][:, g * GW:(g + 1) * GW].rearrange("p (h q) -> p h q", h=HPG),
                                                                               in0=xv, in1=b_last(v3(DT)[:, ch, h0:h0 + HPG], 64), op=ALU.mult),
                         reads=[pcx, DTc], writes=[wbc[d]])
                    P.op("dve", lambda e, d=d, h0=h0, xv=xv, g=g: e.tensor_tensor(out=stg[2 + d][:, g * GW:(g + 1) * GW].rearrange("p (h q) -> p h q", h=HPG),
                                                                               in0=xv, in1=b_last(v3(DTE)[:, ch, h0:h0 + HPG], 64), op=ALU.mult),
                         reads=[pcx, DTEc], writes=[wbc[2 + d]])
                psb, pcb = self.psum_bank()
                P.op("pe", lambda e, g=g, psb=psb: e.matmul(psb[:, 0:128], lhsT=bcv[:, g, :], rhs=IDB[:], start=True, stop=True),
                     reads=[bcc, self.constc], writes=[pcb])
                bt, btc = self.work()
                btb = bt[:, 0:64].bitcast(BF16)
                P.op("act", lambda e, btb=btb, psb=psb: e.activation(out=btb, in_=psb[:, 0:128], func=AF.Identity), reads=[pcb], writes=[btc])
                return btb, btc

            def stage2(g, btb, btc):
                for d in range(2):
                    pss, pcs = self.psum_bank()
                    P.op("pe", lambda e, d=d, g=g, pss=pss, btb=btb: e.matmul(pss[:, 0:GW], lhsT=btb, rhs=stg[2 + d][:, g * GW:(g + 1) * GW], start=True, stop=True),
                         reads=[btc, wbc[2 + d]], writes=[pcs])
                    ev, evc = self.work()
                    P.op("act", lambda e, ev=ev, pss=pss: e.activation(out=ev[:, 0:GW], in_=pss[:, 0:GW], func=AF.Identity), reads=[pcs], writes=[evc])
                    P.dma("sp", self.CS[d, ch][:, g * GW:(g + 1) * GW], ev[:, 0:GW], reads=[evc], writes=[self.CSc[d][ch][g]])

            prev = None
            for g in range(SG):
                cur = (g,) + stage1(g)
                if prev is not None:
                    stage2(*prev)
                prev = cur
            stage2(*prev)
            for d in range(2):
                P.dma("sp", self.XDT[d, ch], stg[d], reads=[wbc[d]], writes=[self.XDTc[d][ch]])
        if getattr(self, 'stop_at', None) == 'pA':
            self.bailed = True
            return
        HD = DI // 2
        HH = SH // 2
        st = [[WBF[:, (d * 2 + hf) * HD:(d * 2 + hf + 1) * HD] for hf in range(2)] for d in range(2)]
        stc = [[P.cell() for _ in range(2)] for _ in range(2)]
        P.handoff(wbc, stc)
        for d in range(2):
            for hf in range(2):
                P.dma("sp", st[d][hf], self.H0T[d][:, hf * HD:(hf + 1) * HD], writes=[stc[d][hf]])
        for k_ in range(NCH):
            for d in range(2):
                ch = k_ if d == 0 else NCH - 1 - k_
                for hf in range(2):
                    sc_, scc = st[d][hf], stc[d][hf]
                    sb_, sbc = self.work()
                    sbb = sb_[:, 0:HD // 2].bitcast(BF16)
                    P.op("act", lambda e, sbb=sbb, sc_=sc_: e.activation(out=sbb, in_=sc_, func=AF.Identity), reads=[scc], writes=[sbc])
                    P.dma("sp", self.SIN[d, ch][:, hf * HD:(hf + 1) * HD], sbb, reads=[sbc], writes=[self.SINc[d][ch][hf]])
                    cs_, csc = self.work()
                    P.dma("sp", cs_[:, 0:HD], self.CS[d, ch][:, hf * HD:(hf + 1) * HD], reads=[self.CSc[d][ch]], writes=[csc])
                    h0 = d * SH + hf * HH
                    P.op("dve", lambda e, sc_=sc_, h0=h0, ch=ch: e.tensor_tensor(out=sc_.rearrange("p (h q) -> p h q", h=HH),
                                                                               in0=sc_.rearrange("p (h q) -> p h q", h=HH),
                                                                               in1=b_last(v3(DCH)[:, ch, h0:h0 + HH], 64), op=ALU.mult),
                         reads=[scc, DCHc], writes=[scc])
                    P.op("dve", lambda e, sc_=sc_, cs_=cs_: e.tensor_tensor(out=sc_, in0=sc_, in1=cs_[:, 0:HD], op=ALU.add),
                         reads=[scc, csc], writes=[scc])
                    seg_end = (ch % CPS == CPS - 1) if d == 0 else (ch % CPS == 0)
                    if seg_end:
                        P.dma("sp", self.NEWST[ch // CPS, d][:, hf * HD:(hf + 1) * HD], sc_, reads=[scc], writes=[P.cell()])
                        P.op("dve", lambda e, sc_=sc_: e.tensor_scalar(out=sc_, in0=sc_, scalar1=fS, scalar2=None, op0=ALU.mult),
                             reads=[scc, self.constc], writes=[scc])
        if getattr(self, 'stop_at', None) == 'pA2':
            self.bailed = True
            return
        YG = WBF[:, 0:DI]
        YGc = P.cell()
        xdt = [self.WBALL[:, 2 * DI + d * DI: 2 * DI + (d + 1) * DI] for d in range(2)]
        xdtc = [P.cell() for _ in range(2)]
        P.handoff(stc, [YGc] + xdtc)
        ZTs, SIN0, SIN1 = big
        ZTsc, SIN0c, SIN1c = bigc
        sins, sinsc = (SIN0, SIN1), (SIN0c, SIN1c)
        TRI = (U, LW)
        STR = (SL, SU)
        for ch in range(NCH):
            xcv, xcc = load_chunk_fm(ch, 0, XC)
            bcv, bcc = load_chunk_fm(ch, XC, 2 * SG)
            P.dma("sp", ZTs, self.ZT[ch], reads=[self.ZTc[ch]], writes=[ZTsc])
            for d in range(2):
                P.dma("sp", xdt[d], self.XDT[d, ch], reads=[self.XDTc[d][ch]], writes=[xdtc[d]])
                P.dma("sp", sins[d], self.SIN[d, ch], reads=[self.SINc[d][ch]], writes=[sinsc[d]])
            for g in range(SG):
                psg, pcg = self.psum_bank()
                P.op("pe", lambda e, g=g, psg=psg: e.matmul(psg[:, 0:128], lhsT=bcv[:, g, :], rhs=bcv[:, SG + g, :], start=True, stop=True),
                     reads=[bcc], writes=[pcg])
                gmb, gmc = GMB, GMBc
                for d in range(2):
                    P.op("dve", lambda e, d=d, psg=psg, gmb=gmb: e.tensor_tensor(out=gmb[:, d, :], in0=psg[:, 0:128], in1=TRI[d], op=ALU.mult),
                         reads=[pcg, self.constc], writes=[gmc])
                psy, pcy = self.psum_bank()
                for jj in range(XPG):
                    x = g * XPG + jj
                    P.op("pe", lambda e, jj=jj, x=x, psy=psy: e.matmul(psy[:, jj * 128:(jj + 1) * 128], lhsT=xcv[:, x, :], rhs=DGD[:, x, :],
                                                                    start=(jj == 0), stop=False),
                         reads=[xcc, DGDc], writes=[pcy], inc=False)
                NBT = HPG // HBAT
                decw, decwc = self.work()
                etw, etwc = self.work()
                etall = etw[:, 0:2 * NBT * HBAT * 64].bitcast(BF16)
                batches = []
                for d in range(2):
                    for hb in range(NBT):
                        bi = d * NBT + hb
                        hl = hb * HBAT
                        h0 = d * SH + g * HPG + hl
                        decv = decw[:, bi * HBAT * 128:(bi + 1) * HBAT * 128].rearrange("p (h m) -> p h m", h=HBAT)
                        P.op("dve", lambda e, d=d, h0=h0, decv=decv: e.tensor_tensor(out=decv, in0=b_mid(STR[d], HBAT),
                                                                                   in1=b_last(v3(DTA)[:, ch, h0:h0 + HBAT], 128), op=ALU.mult),
                             reads=[DTAc, self.constc], writes=[decwc])
                        psd_, pcd_ = self.psum_bank()
                        for hh in range(HBAT):
                            P.op("pe", lambda e, hh=hh, d=d, decv=decv, psd_=psd_: e.matmul(psd_[:, hh * 128:(hh + 1) * 128], lhsT=decv[:, hh, :], rhs=TRI[d],
                                                                                         start=(hh == 0), stop=True),
                                 reads=[decwc, self.constc], writes=[pcd_], inc=(hh == HBAT - 1))
                        etb = etall[:, bi * HBAT * 128:(bi + 1) * HBAT * 128].rearrange("p (h m) -> p h m", h=HBAT)
                        P.op("act", lambda e, etb=etb, psd_=psd_: e.activation(out=etb, in_=psd_[:, 0:HBAT * 128].rearrange("p (h m) -> p h m", h=HBAT), func=AF.Exp),
                             reads=[pcd_], writes=[etwc])
                        P.op("dve", lambda e, etb=etb, d=d, gmb=gmb: e.tensor_tensor(out=etb, in0=etb, in1=b_mid(gmb[:, d, :], HBAT), op=ALU.mult),
                             reads=[etwc, gmc], writes=[etwc])
                        batches.append((d, hl, etb))
                yos = []
                for d in range(2):
                    pso, pco = self.psum_bank()
                    P.op("pe", lambda e, d=d, g=g, pso=pso: e.matmul(pso[:, 0:GW], lhsT=bcv[:, SG + g, :], rhs=sins[d][:, g * GW:(g + 1) * GW], start=True, stop=True),
                         reads=[bcc, sinsc[d]], writes=[pco])
                    yo, yoc = self.work()
                    yob = yo[:, 0:GW // 2].bitcast(BF16)
                    h0g = d * SH + g * HPG
                    P.op("dve", lambda e, yob=yob, pso=pso, h0g=h0g: e.tensor_tensor(out=yob.rearrange("p (h q) -> p h q", h=HPG),
                                                                                   in0=pso[:, 0:GW].rearrange("p (h q) -> p h q", h=HPG),
                                                                                   in1=b_last(v3(EACS)[:, ch, h0g:h0g + HPG], 64), op=ALU.mult),
                         reads=[pco, EACSc], writes=[yoc])
                    yos.append((yob, yoc))
                for (d, hl, etb) in batches:
                    for hh in range(HBAT):
                        col = (hl + hh) * 64
                        P.op("pe", lambda e, hh=hh, col=col, d=d, etb=etb, psy=psy, g=g: e.matmul(psy[:, col:col + 64], lhsT=etb[:, hh, :],
                                                                                               rhs=xdt[d][:, g * GW + col: g * GW + col + 64],
                                                                                               start=False, stop=False),
                             reads=[etwc, xdtc[d]], writes=[pcy], inc=False)
                for d in range(2):
                    yob, yoc = yos[d]
                    P.op("pe", lambda e, yob=yob, psy=psy, d=d: e.matmul(psy[:, 0:GW], lhsT=IDB[:], rhs=yob, start=False, stop=(d == 1)),
                         reads=[yoc, self.constc], writes=[pcy], inc=(d == 1))
                P.op("dve", lambda e, psy=psy, g=g: e.tensor_tensor(out=YG[:, g * GW:(g + 1) * GW], in0=psy[:, 0:GW], in1=ZTs[:, g * GW:(g + 1) * GW], op=ALU.mult),
                     reads=[pcy, ZTsc], writes=[YGc])
            if self.debug:
                P.dma("sp", self.DBG[ch], YG, reads=[YGc], writes=[P.cell()])
            sq, sqc = self.work()
            ss, ssc = self.work()
            P.op("act", lambda e, sq=sq, ss=ss: e.activation(out=sq[:, 0:DI // 2].bitcast(BF16), in_=YG, func=AF.Square, accum_out=ss[:, 0:1]),
                 reads=[YGc], writes=[sqc, ssc])
            P.op("dve", lambda e, ss=ss: e.tensor_scalar(out=ss[:, 0:1], in0=ss[:, 0:1], scalar1=1.0 / DI, scalar2=EPS, op0=ALU.mult, op1=ALU.add),
                 reads=[ssc], writes=[ssc])
            P.op("act", lambda e, ss=ss: e.activation(out=ss[:, 0:1], in_=ss[:, 0:1], func=AF.Sqrt), reads=[ssc], writes=[ssc])
            P.op("dve", lambda e, ss=ss: e.reciprocal(out=ss[:, 0:1], in_=ss[:, 0:1]), reads=[ssc], writes=[ssc])
            yn, ync = self.work()
            ynb = yn[:, 0:DI // 2].bitcast(BF16)
            P.op("act", lambda e, ynb=ynb, ss=ss: e.activation(out=ynb, in_=YG, func=AF.Identity, scale=ss[:, 0:1]), reads=[YGc, ssc], writes=[ync])
            yt, ytc = self.work()
            ytb = yt[:, 0:XC * 64].bitcast(BF16).rearrange("p (x m) -> p x m", x=XC)
            XB = min(4, XC)
            for x0 in range(0, XC, XB):
                pst, pct = self.psum_bank()
                for xx in range(XB):
                    P.op("pe", lambda e, xx=xx, x0=x0, pst=pst, ynb=ynb: e.matmul(pst[:, xx * 128:(xx + 1) * 128], lhsT=ynb[:, (x0 + xx) * 128:(x0 + xx + 1) * 128],
                                                                               rhs=IDB[:], start=True, stop=True),
                         reads=[ync, self.constc], writes=[pct], inc=(xx == XB - 1))
                P.op("dve", lambda e, x0=x0, pst=pst, ytb=ytb: e.tensor_tensor(out=ytb[:, x0:x0 + XB, :], in0=pst[:, 0:XB * 128].rearrange("p (x m) -> p x m", x=XB),
                                                                            in1=b_last(self.sv("ssd_ng%d" % j)[:, x0:x0 + XB], 128), op=ALU.mult),
                     reads=[pct, self.constc], writes=[ytc])
            P.dma("sp", self.YF[0:XC, :, ch * 128:(ch + 1) * 128].rearrange("x p m -> p x m"), ytb, reads=[ytc], writes=[self.YFc[ch]])
        if getattr(self, 'stop_at', None) == 'pB':
            self.bailed = True
            return
        P.handoff(tlc + bigc, [self.HBc])
        P.handoff([YGc] + xdtc, self.WBc)
        wout = self.W["ssd_w_out"][j]
        for m0 in range(0, c.KC, 2):
            ms = [m for m in (m0, m0 + 1) if m < c.KC]
            pss = [self.psum() for _ in ms]
            wv = [self.load_w(wout, 0, DI, m * 128, 128) for m in ms]
            for kc in range(XC):
                slot = self.hb_slot()
                P.dma("sp", self.HB[:, slot, :], self.YF[kc], reads=[self.YFc], writes=[self.HBsc[slot]])
                for (ps, pc), (view, wc) in zip(pss, wv):
                    for tb in range(c.TB):
                        P.op("pe", lambda e, o=ps[:, tb * c.TBW:(tb + 1) * c.TBW], l=view[:, kc, 0:128],
                             r=self.HB[:, slot, tb * c.TBW:(tb + 1) * c.TBW], st=(kc == 0), sp_=(kc == XC - 1):
                             e.matmul(o, lhsT=l, rhs=r, start=st, stop=sp_),
                             reads=[wc, self.HBsc[slot]], writes=[pc], inc=(tb == c.TB - 1))
            for m, (ps, pc) in zip(ms, pss):
                self.out_epilogue(ps, pc, m, None)
        self.hb_release()
        self.post(i, 0, False)

    def attention(self, i, j):
        P = self.P
        c = self.cfg
        NH, NKV = c.NH, c.NKV
        KVG = NH // NKV
        T, PAST = c.T, c.PAST
        NT, PT = T // 128, PAST // 128
        NTK = NT + PT
        CPS = c.SEG // 128
        SPQ = c.TBW // c.SEG
        wq = self.W["attn_w_qkv"][j]
        zero_col = self.CF[:, 7, 0:1]
        offb = self.sv("flags", 1)
        cacheb = self.sv("flags", 2)
        scale = 128.0 ** -0.5
        self.prenorm(i, 0)
        COS, COSc, SIN, SINc = self.RSTD, self.RSTDc, self.RSTDO, self.RSTDOc
        P.dma("sp", COS[:, 0:T], self.ROPE[0], writes=[COSc])
        P.dma("sp", SIN[:, 0:T], self.ROPE[1], writes=[SINc])

        def norm_rope(ps, pc, gain_col, raw_out=None, raw_cell=None):
            xt, xc = self.work()
            P.op("act", lambda e: e.activation(out=xt[:, 0:T], in_=ps[:, 0:T], func=AF.Identity), reads=[pc], writes=[xc])
            sq, sqc = self.work()
            P.op("act", lambda e: e.activation(out=sq[:, 0:T], in_=xt[:, 0:T], func=AF.Square), reads=[xc], writes=[sqc])
            ps2, pc2 = self.colsum_bcast(sq, sqc)
            P.op("dve", lambda e: e.tensor_scalar(out=sq[:, 0:T], in0=ps2[:, 0:T], scalar1=1.0 / 128, scalar2=EPS, op0=ALU.mult, op1=ALU.add),
                 reads=[pc2], writes=[sqc])
            P.op("act", lambda e: e.activation(out=sq[:, 0:T], in_=sq[:, 0:T], func=AF.Sqrt), reads=[sqc], writes=[sqc])
            P.op("dve", lambda e: e.reciprocal(out=sq[:, 0:T], in_=sq[:, 0:T]), reads=[sqc], writes=[sqc])
            P.op("dve", lambda e: e.scalar_tensor_tensor(out=xt[:, 0:T], in0=xt[:, 0:T], scalar=gain_col, in1=sq[:, 0:T], op0=ALU.mult, op1=ALU.mult),
                 reads=[xc, sqc, self.constc], writes=[xc])
            if raw_out is not None:
                P.dma("sp", raw_out, xt[:, 0:T], reads=[xc], writes=[raw_cell])
            xb, xbc = self.work()
            xbb = xb[:, 0:T // 2].bitcast(BF16)
            P.op("act", lambda e: e.activation(out=xbb, in_=xt[:, 0:T], func=AF.Identity), reads=[xc], writes=[xbc])
            ps3, pc3 = self.psum()
            for tb in range(c.TB):
                P.op("pe", lambda e, tb=tb: e.matmul(ps3[:, tb * c.TBW:(tb + 1) * c.TBW], lhsT=self.ROTB[:], rhs=xbb[:, tb * c.TBW:(tb + 1) * c.TBW],
                                                   start=True, stop=True), reads=[xbc, self.constc], writes=[pc3], inc=(tb == c.TB - 1))
            P.op("dve", lambda e: e.tensor_tensor(out=xt[:, 0:T], in0=xt[:, 0:T], in1=COS[:, 0:T], op=ALU.mult), reads=[xc, COSc], writes=[xc])
            P.op("dve", lambda e: e.tensor_tensor(out=sq[:, 0:T], in0=ps3[:, 0:T], in1=SIN[:, 0:T], op=ALU.mult), reads=[pc3, SINc], writes=[sqc])
            P.op("dve", lambda e: e.tensor_tensor(out=xbb, in0=xt[:, 0:T], in1=sq[:, 0:T], op=ALU.add), reads=[xc, sqc], writes=[xbc])
            return xbb, xbc

        for kv in range(NKV):
            wv_, wc_ = self.load_w(wq, 0, c.D, (NH + kv) * 128, 128)
            ps, pc = self.psum()
            self.matmul_acc(ps, pc, wv_, wc_, 0, c.KC, self.hb_rhs)
            kb, kbc = norm_rope(ps, pc, self.sv("att_kn%d" % j, 0), raw_out=self.NEWK[kv], raw_cell=P.cell())
            P.dma("sp", self.KD[kv][:, 0:T], kb, reads=[kbc], writes=[self.KDc[kv]])
            ck, ckc = self.work()
            P.dma("sp", ck[:, 0:PAST], self.CACHEK[kv], writes=[ckc])
            cb, cbc = self.work()
            cbb = cb[:, 0:PAST // 2].bitcast(BF16)
            P.op("act", lambda e, cbb=cbb, ck=ck: e.activation(out=cbb, in_=ck[:, 0:PAST], func=AF.Identity), reads=[ckc], writes=[cbc])
            P.dma("sp", self.KD[kv][:, T:T + PAST], cbb, reads=[cbc], writes=[self.KDc[kv]])
        VW = NKV * 128
        VH = min(256, VW)
        for v0 in range(0, VW, VH):
            wv_, wc_ = self.load_w(wq, 0, c.D, (NH + NKV) * 128 + v0, VH)
            for tt in range(NT):
                psv, pcv = self.psum_bank()
                for kc in range(c.KC):
                    P.op("pe", lambda e, tt=tt, kc=kc, psv=psv: e.matmul(psv[:, 0:VH], lhsT=self.HB[:, kc, tt * 128:(tt + 1) * 128], rhs=wv_[:, kc, 0:VH],
                                                                      start=(kc == 0), stop=(kc == c.KC - 1)),
                         reads=[self.HBc, wc_], writes=[pcv], inc=(kc == c.KC - 1))
                vt, vtc = self.work()
                P.op("act", lambda e, vt=vt, psv=psv: e.activation(out=vt[:, 0:VH], in_=psv[:, 0:VH], func=AF.Identity), reads=[pcv], writes=[vtc])
                P.dma("sp", self.NEWV[tt * 128:(tt + 1) * 128, v0:v0 + VH], vt[:, 0:VH], reads=[vtc], writes=[P.cell()])
                vb, vbc = self.work()
                vbb = vb[:, 0:VH // 2].bitcast(BF16)
                P.op("dve", lambda e, vbb=vbb, vt=vt: e.tensor_copy(out=vbb, in_=vt[:, 0:VH]), reads=[vtc], writes=[vbc])
                P.dma("sp", self.VD[tt][:, v0:v0 + VH], vbb, reads=[vbc], writes=[self.VDc[tt]])
        for pt in range(PT):
            cv_, cvc = self.work()
            P.dma("sp", cv_[:, 0:VW], self.CACHEV[pt * 128:(pt + 1) * 128, :], writes=[cvc])
            cb, cbc = self.work()
            cbb = cb[:, 0:VW // 2].bitcast(BF16)
            P.op("dve", lambda e, cbb=cbb, cv_=cv_: e.tensor_copy(out=cbb, in_=cv_[:, 0:VW]), reads=[cvc], writes=[cbc])
            P.dma("sp", self.VD[NT + pt], cbb, reads=[cbc], writes=[self.VDc[NT + pt]])
        for h in range(NH):
            if h % 2 == 0:
                wq_ = self.load_w(wq, 0, c.D, h * 128, min(2, NH - h) * 128)
            ps, pc = self.psum()
            self.matmul_acc(ps, pc, wq_[0], wq_[1], (h % 2) * 128, c.KC, self.hb_rhs)
            qb, qbc = norm_rope(ps, pc, self.sv("att_qn%d" % j, 0))
            P.dma("sp", self.QD[h], qb, reads=[qbc], writes=[self.QDc[h]])
        self.hb_release()
        A = self.ARENA
        o_ = 0
        KB = A[:, o_:o_ + NKV * (T + PAST)].rearrange("p (k t) -> p k t", k=NKV); o_ += NKV * (T + PAST)
        VB = A[:, o_:o_ + NTK * VW].rearrange("p (t v) -> p t v", t=NTK); o_ += NTK * VW
        QB = [A[:, o_ + k * T:o_ + (k + 1) * T] for k in range(2)]; o_ += 2 * T
        EB = [A[:, o_ + k * 512:o_ + (k + 1) * 512] for k in range(4)]; o_ += 4 * 512
        AO = [A[:, o_ + k * T:o_ + (k + 1) * T] for k in range(1)]; o_ += T
        KBc, VBc = P.cell(), P.cell()
        QBc = [P.cell() for _ in QB]
        EBc = [P.cell() for _ in EB]
        AOc = [P.cell() for _ in AO]
        P.handoff([self.HBc], [KBc, VBc] + QBc + EBc + AOc)
        for kv in range(NKV):
            P.dma("sp", KB[:, kv, :], self.KD[kv], reads=[self.KDc[kv]], writes=[KBc])
        for tk in range(NTK):
            P.dma("sp", VB[:, tk, :], self.VD[tk], reads=[self.VDc[tk]], writes=[VBc])
        self._eb_rr = 0
        NQC = T // c.TBW
        for h in range(NH):
            kv = h // KVG
            qb, qbc = QB[h % 2], QBc[h % 2]
            P.dma("sp", qb, self.QD[h], reads=[self.QDc[h]], writes=[qbc])
            ao, aoc = AO[0], AOc[0]
            for qc in range(NQC):
                par = (h * NQC + qc) % 2
                pso, pco = self.PS[1][:, (2 * par) * 512:(2 * par + 1) * 512], self.PSc[1][2 * par]
                psl, pcl = self.PS[1][:, (2 * par + 1) * 512:(2 * par + 2) * 512], self.PSc[1][2 * par + 1]
                def emit_s(tk):
                    sb_ = tk % 4
                    pss, pcs = self.PS[0][:, sb_ * 512:(sb_ + 1) * 512], self.PSc[0][sb_]
                    P.op("pe", lambda e, tk=tk, pss=pss, qb=qb, qc=qc, kv=kv: e.matmul(pss[:, 0:c.TBW], lhsT=KB[:, kv, tk * 128:(tk + 1) * 128],
                                                                                 rhs=qb[:, qc * c.TBW:(qc + 1) * c.TBW], start=True, stop=True),
                         reads=[KBc, qbc], writes=[pcs])
                    k_ = self._eb_rr
                    self._eb_rr = (k_ + 1) % 4
                    eb, ebc = EB[k_], EBc[k_]
                    if tk >= NT:
                        segs = [(0, c.TBW, cacheb)]
                    else:
                        sk = tk // CPS
                        segs = []
                        for jh in range(SPQ):
                            bias = zero_col if (qc * SPQ + jh == sk) else offb
                            segs.append((jh * c.SEG, (jh + 1) * c.SEG, bias))
                        merged = [segs[0]]
                        for sgm in segs[1:]:
                            if sgm[2] is merged[-1][2]:
                                merged[-1] = (merged[-1][0], sgm[1], sgm[2])
                            else:
                                merged.append(sgm)
                        segs = merged
                    for (a0, a1, bias) in segs:
                        P.op("act", lambda e, a0=a0, a1=a1, bias=bias, eb=eb, pss=pss: e.activation(out=eb[:, a0:a1], in_=pss[:, a0:a1], func=AF.Exp,
                                                                                           bias=bias, scale=scale),
                             reads=[pcs, self.constc], writes=[ebc])
                    return eb, ebc

                def emit_pv(tk, eb, ebc):
                    P.op("pe", lambda e, tk=tk, eb=eb, pso=pso, kv=kv: e.matmul(pso[:, 0:c.TBW], lhsT=VB[:, tk, kv * 128:(kv + 1) * 128], rhs=eb[:, 0:c.TBW],
                                                                          start=(tk == 0), stop=(tk == NTK - 1)),
                         reads=[VBc, ebc], writes=[pco], inc=False)
                    P.op("pe", lambda e, tk=tk, eb=eb, psl=psl: e.matmul(psl[:, 0:c.TBW], lhsT=self.ONESB[:], rhs=eb[:, 0:c.TBW],
                                                                      start=(tk == 0), stop=(tk == NTK - 1)),
                         reads=[self.constc, ebc], writes=[pcl])

                pend = []
                for tk in range(NTK):
                    pend.append((tk,) + emit_s(tk))
                    if len(pend) > 2:
                        emit_pv(*pend.pop(0))
                while pend:
                    emit_pv(*pend.pop(0))
                rc, rcc = self.work()
                P.op("dve", lambda e, rc=rc, psl=psl: e.reciprocal(out=rc[:, 0:c.TBW], in_=psl[:, 0:c.TBW]), reads=[pcl], writes=[rcc])
                P.op("dve", lambda e, rc=rc, pso=pso, ao=ao, qc=qc: e.tensor_tensor(out=ao[:, qc * c.TBW:(qc + 1) * c.TBW], in0=pso[:, 0:c.TBW], in1=rc[:, 0:c.TBW], op=ALU.mult),
                     reads=[pco, pcl, rcc], writes=[aoc])
            P.dma("sp", self.AOD[h], ao, reads=[aoc], writes=[self.AODc[h]])
        P.handoff([KBc, VBc] + QBc + EBc + AOc, [self.HBc])
        wo = self.W["attn_w_o"][j]
        for m0 in range(0, c.KC, 2):
            ms = [m for m in (m0, m0 + 1) if m < c.KC]
            pss_ = [self.psum() for _ in ms]
            wv2 = [self.load_w(wo, 0, NH * 128, m * 128, 128) for m in ms]
            for kc in range(NH):
                slot = self.hb_slot()
                P.dma("sp", self.HB[:, slot, :], self.AOD[kc], reads=[self.AODc[kc]], writes=[self.HBsc[slot]])
                for (ps, pc), (view, wc) in zip(pss_, wv2):
                    for tb in range(c.TB):
                        P.op("pe", lambda e, o=ps[:, tb * c.TBW:(tb + 1) * c.TBW], l=view[:, kc, 0:128],
                             r=self.HB[:, slot, tb * c.TBW:(tb + 1) * c.TBW], st=(kc == 0), sp_=(kc == NH - 1):
                             e.matmul(o, lhsT=l, rhs=r, start=st, stop=sp_),
                             reads=[wc, self.HBsc[slot]], writes=[pc], inc=(tb == c.TB - 1))
            for m, (ps, pc) in zip(ms, pss_):
                self.out_epilogue(ps, pc, m, None)
        self.hb_release()
        self.post(i, 0, False)

    def build(self):
        c = self.cfg
        nc = bass.Bass("TRN2", target_bir_lowering=False)
        self.nc = nc
        es = ExitStack()
        self.es = es
        P = Prog(nc, es)
        self.P = P

        def din(name, shape):
            return nc.dram_tensor(name, list(shape), F32, kind="ExternalInput").ap()

        def dscr(name, shape, dt):
            return nc.dram_tensor(name, list(shape), dt, kind="Internal").ap()

        self.XIN = din("xin", (c.KC, 128, c.T))
        self.SMALL = din("small", (128, self.NSMALL))
        self.CONSTF = din("constf", (128, 8, 128))
        self.W = {}
        self.W["w_mod"] = din("w_mod", (c.DEPTH, c.D, 6 * c.D))
        self.W["cv_w_pw1"] = din("cv_w_pw1", (c.NCONV, c.D, 2 * c.D))
        self.W["cv_w_pw2"] = din("cv_w_pw2", (c.NCONV, c.D, c.D))
        self.W["ffn_w_up"] = din("ffn_w_up", (c.DEPTH, c.D, 2 * c.DFF))
        self.W["ffn_w_down"] = din("ffn_w_down", (c.DEPTH, c.DFF, c.D))
        NCH, XC, NXBC = c.T // 128, c.DI // 128, c.DI // 128 + 2 * c.SG
        if c.NSSD:
            self.W["ssd_w_in"] = din("ssd_w_in", (c.NSSD, c.D, 2 * c.DI + 2 * c.SG * 128 + 2 * c.SH))
            self.W["ssd_w_out"] = din("ssd_w_out", (c.NSSD, c.DI, c.D))
            self.H0T = din("h0t", (2, 128, c.DI))
            self.NEWST = nc.dram_tensor("newst", [c.NSEG, 2, 128, c.DI], F32, kind="ExternalOutput").ap()
            self.ZT = dscr("ZT", (NCH, 128, c.DI), BF16)
            self.XBC = dscr("XBC", (NXBC, 128, c.T), BF16)
            self.XDT = dscr("XDT", (2, NCH, 128, c.DI), BF16)
            self.CS = dscr("CS", (2, NCH, 128, c.DI), F32)
            self.SIN = dscr("SIN", (2, NCH, 128, c.DI), BF16)
            self.YF = dscr("YF", (XC, 128, c.T), BF16)
            self.ZTc = [[P.cell() for _ in range(c.DI // 256)] for _ in range(NCH)]
            self.XBCc = [P.cell() for _ in range(NXBC)]
            self.XDTc = [[P.cell() for _ in range(NCH)] for _ in range(2)]
            self.CSc = [[[P.cell() for _ in range(c.SG)] for _ in range(NCH)] for _ in range(2)]
            self.SINc = [[[P.cell() for _ in range(2)] for _ in range(NCH)] for _ in range(2)]
            self.YFc = [P.cell() for _ in range(NCH)]
        if c.NATT:
            NT_, PT_ = c.T // 128, c.PAST // 128
            self.W["attn_w_qkv"] = din("attn_w_qkv", (c.NATT, c.D, (c.NH + 2 * c.NKV) * 128))
            self.W["attn_w_o"] = din("attn_w_o", (c.NATT, c.NH * 128, c.D))
            self.ROPE = din("rope", (2, 128, c.T))
            self.CACHEK = din("cachek", (c.NKV, 128, c.PAST))
            self.CACHEV = din("cachev", (c.PAST, c.NKV * 128))
            self.NEWK = nc.dram_tensor("newk", [c.NKV, 128, c.T], F32, kind="ExternalOutput").ap()
            self.NEWV = nc.dram_tensor("newv", [c.T, c.NKV * 128], F32, kind="ExternalOutput").ap()
            self.KD = dscr("KD", (c.NKV, 128, c.T + c.PAST), BF16)
            self.VD = dscr("VD", (NT_ + PT_, 128, c.NKV * 128), BF16)
            self.QD = dscr("QD", (c.NH, 128, c.T), BF16)
            self.AOD = dscr("AOD", (c.NH, 128, c.T), BF16)
            self.KDc = [P.cell() for _ in range(c.NKV)]
            self.VDc = [P.cell() for _ in range(NT_ + PT_)]
            self.QDc = [P.cell() for _ in range(c.NH)]
            self.AODc = [P.cell() for _ in range(c.NH)]
        if self.debug:
            self.DBG = nc.dram_tensor("dbg", [NCH, 128, c.DI], F32, kind="ExternalOutput").ap()
        self.YOUT = nc.dram_tensor("yout", [c.KC, 128, c.T], F32, kind="ExternalOutput").ap()
        self.Y = dscr("Y", (c.KC, 128, c.T), F32)
        self.O = dscr("O", (c.KC, 128, c.T), F32)
        self.CV = dscr("CV", (c.KC, 128, c.T), F32)
        self.ACTD = dscr("ACTD", (c.FC, 128, c.T), BF16)
        self.Yc = [P.cell() for _ in range(c.KC)]
        self.Oc = [P.cell() for _ in range(c.KC)]
        self.CVc = [P.cell() for _ in range(c.KC)]
        self.ACTDc = [P.cell() for _ in range(c.FC)]

        def sb(name, shape, dt):
            return es.enter_context(nc.sbuf_tensor(name, list(shape), dt))

        H = (c.CW - 1) // 2
        NCH_, NDT_ = c.T // 128, 2 * c.SH
        att_el = c.NKV * (c.T + c.PAST) + (c.T // 128 + c.PAST // 128) * c.NKV * 128 + 2 * c.T + 4 * 512 + c.T
        arena_el = max(c.KC * c.T, 10 * NCH_ * NDT_ + 3 * c.DI, att_el if c.NATT else 0)
        self.ARENA = sb("ARENA", (128, arena_el), BF16)
        self.HB = self.ARENA[:, 0:c.KC * c.T].rearrange("p (k t) -> p k t", k=c.KC)
        self.HBc = P.cell()
        self.HBsc = [P.cell() for _ in range(c.KC)]
        self.hb_slot_mode = False
        self.WSLOT = 4096
        self.WBALL = sb("WBALL", (128, 4 * self.WSLOT), BF16)
        self.WB = [self.WBALL[:, k * self.WSLOT:(k + 1) * self.WSLOT] for k in range(4)]
        self.WBc = [P.cell() for _ in self.WB]
        self.wb_rr = 0
        self.WK = [sb("WK%d" % k, (128, max(c.T + 64, 2112)), F32) for k in range(5)]
        self.WKc = [P.cell() for _ in self.WK]
        self.wk_rr = 0
        XC_ = c.DI // 128
        upn = c.NSEG * (c.SEG + 2 * H)
        a2_el = max(upn + c.CW * 128, 2 * c.T, 2 * XC_ * 128 + 2 * c.SG * 128)
        self.A2 = sb("A2", (128, a2_el), BF16)
        self.AB = [self.A2[:, k * c.T:(k + 1) * c.T] for k in range(2)]
        self.ABc = [P.cell() for _ in self.AB]
        self.ab_rr = 0
        self.UPF = self.A2[:, 0:upn]
        self.UP = self.UPF.rearrange("p (s w) -> p s w", s=c.NSEG)
        self.UPc = P.cell()
        self.DG = self.A2[:, upn:upn + c.CW * 128].rearrange("p (k m) -> p k m", k=c.CW)
        self.DGc = P.cell()
        self.A2cells = []
        self.GMB = sb("GMB", (128, 256), BF16)
        self.RSTD = sb("RSTD", (128, c.T), F32)
        self.RSTDc = P.cell()
        self.RSTDO = sb("RSTDO", (128, c.T), F32)
        self.RSTDOc = P.cell()
        self.SQY, self.SQYc = self.RSTD, self.RSTDc
        self.SQO, self.SQOc = self.RSTDO, self.RSTDOc
        self.SV = sb("SV", (128, self.NSMALL), F32)
        self.CF = sb("CF", (128, 8, 128), F32)
        self.ONES_F = self.CF[:, 0, :]
        self.IDB = sb("IDB", (128, 128), BF16)
        self.ROTB = sb("ROTB", (128, 128), BF16)
        self.ONESB = sb("ONESB", (128, 128), BF16)
        self.SC = sb("SC", (128, c.KC), BF16)
        self.MOD = sb("MOD", (128, c.DEPTH, 6 * c.KC), F32)
        self.MODc = P.cell()
        self.constc = P.cell()
        self.PS = [es.enter_context(nc.psum_tensor("PS%d" % k, [128, 2048], F32)) for k in range(2)]
        self.PSc = [[Cell(excl=True) for _ in range(4)] for _ in self.PS]
        self.ps_rr = 0
        self.psb_rr = 0

        P.dma("sp", self.SV[:], self.SMALL, writes=[self.constc])
        P.dma("sp", self.CF[:], self.CONSTF, writes=[self.constc])
        P.op("dve", lambda e: e.tensor_copy(out=self.IDB[:], in_=self.CF[:, 1, :]), reads=[self.constc], writes=[self.constc])
        P.op("dve", lambda e: e.tensor_copy(out=self.ROTB[:], in_=self.CF[:, 6, :]), reads=[self.constc], writes=[self.constc])
        P.op("dve", lambda e: e.tensor_copy(out=self.ONESB[:], in_=self.CF[:, 0, :]), reads=[self.constc], writes=[self.constc])
        P.op("act", lambda e: e.activation(out=self.SC[:], in_=self.sv("cond"), func=AF.Silu), reads=[self.constc], writes=[self.constc])
        for kc in range(c.KC):
            xt, xc = self.work()
            P.dma("sp", xt[:, 0:c.T], self.XIN[kc], writes=[xc])
            P.dma("sp", self.Y[kc], xt[:, 0:c.T], reads=[xc], writes=[self.Yc[kc]])
            self.sq_accum(xt[:, 0:c.T], xc, self.SQY, self.SQYc, first=(kc == 0))
        self.rstd_from_sq(self.SQY, self.SQYc, self.RSTD, self.RSTDc, c.D)
        for i in range(self.n_layers):
            self.modulation(i)
        for i in range(self.n_layers):
            kind, j = i % 3, i // 3
            last = (i == self.n_layers - 1)
            if kind == 0:
                self.conformer(i, j)
            elif kind == 1:
                self.ssd(i, j)
            else:
                self.attention(i, j)
            if getattr(self, 'bailed', False):
                break
            self.ffn(i, last)
        P.finish()
        es.close()
        return nc

    @property
    def NSMALL(self):
        return self.sp_layout.n


def const_f():
    cf = np.zeros((128, 8, 128), np.float32)
    cf[:, 0, :] = 1.0
    cf[:, 1, :] = np.eye(128, dtype=np.float32)
    cf[:, 2, :] = np.triu(np.ones((128, 128), np.float32))
    cf[:, 3, :] = np.tril(np.ones((128, 128), np.float32))
    cf[:, 4, :] = np.tril(np.ones((128, 128), np.float32), -1)
    cf[:, 5, :] = np.triu(np.ones((128, 128), np.float32), 1)
    for i_ in range(64):
        cf[2 * i_ + 1, 6, 2 * i_] = -1.0
        cf[2 * i_, 6, 2 * i_ + 1] = 1.0
    return cf


def core_plan(cfg, n_prompt, n_sample):
    plan = [("s", b) for b in range(n_sample)]
    per = cfg.NSEG
    for s0 in range(0, n_prompt, per):
        plan.append(("p", s0))
    return plan


def make_in_maps(cfg, inputs, plan):
    c = cfg
    names = ["w_mod", "cv_w_pw1", "cv_w_pw2", "ffn_w_up", "ffn_w_down"]
    if c.NSSD:
        names += ["ssd_w_in", "ssd_w_out"]
    if c.NATT:
        names += ["attn_w_qkv", "attn_w_o"]
    shared = {k: np.ascontiguousarray(np.asarray(inputs[k], np.float32)) for k in names}
    cf = const_f()
    maps = []
    for kind, idx in plan:
        if kind == "s":
            x = np.asarray(inputs["x_sample"][idx], np.float32)
            cond = np.asarray(inputs["c"][idx], np.float32)
            flags = [1.0, 0.0, 0.0]
        else:
            x = np.asarray(inputs["x_prompt"][idx:idx + c.NSEG], np.float32).reshape(c.T, c.D)
            cond = np.asarray(inputs["c_ctx"], np.float32)
            flags = [0.0, NEG, NEG]
        sp = small_layout(c, inputs, cond=cond, flags=flags)
        m = dict(shared)
        m["xin"] = fm(x)
        if c.NSSD:
            if kind == "s":
                st = np.asarray(inputs["state_ssd"][idx, 0], np.float32)
                m["h0t"] = np.ascontiguousarray(st.reshape(2, c.DI, 128).transpose(0, 2, 1))
            else:
                m["h0t"] = np.zeros((2, 128, c.DI), np.float32)
        if c.NATT:
            rope = np.zeros((2, 128, c.T), np.float32)
            if kind == "s":
                rows = c.T // c.GRID_W
                pos_row = np.repeat(np.arange(rows, dtype=np.float32), c.GRID_W)
                pos_col = np.tile(np.arange(c.GRID_W, dtype=np.float32), rows)
                inv = (np.float32(10000.0) ** (-np.arange(32, dtype=np.float32) / np.float32(32))).astype(np.float32)
                ang = np.concatenate([pos_row[:, None] * inv, pos_col[:, None] * inv], axis=-1).astype(np.float32)
                rope[0] = np.repeat(np.cos(ang).T, 2, axis=0)
                rope[1] = np.repeat(np.sin(ang).T, 2, axis=0)
                ck = np.asarray(inputs["cache_k"][idx, 0], np.float32)
                m["cachek"] = np.ascontiguousarray(ck.transpose(1, 2, 0))
                m["cachev"] = np.ascontiguousarray(np.asarray(inputs["cache_v"][idx, 0], np.float32).reshape(c.PAST, c.NKV * 128))
            else:
                rope[0] = 1.0
                m["cachek"] = np.zeros((c.NKV, 128, c.PAST), np.float32)
                m["cachev"] = np.zeros((c.PAST, c.NKV * 128), np.float32)
            m["rope"] = rope
        m["small"] = sp.build()
        m["constf"] = cf
        maps.append(m)
    return maps


def run(cfg, inputs, n_cores=None, n_layers=None, trace=False, debug=False):
    c = cfg
    n_prompt = inputs["x_prompt"].shape[0]
    n_sample = inputs["x_sample"].shape[0]
    plan = core_plan(c, n_prompt, n_sample)
    n_cores = len(plan) if n_cores is None else n_cores
    maps = make_in_maps(c, inputs, plan)
    if n_cores == 8 and len(plan) == 4:
        slot_of = [0, 1, 4, 5]
        zero = {k: np.zeros_like(v) for k, v in maps[0].items()}
        in_maps = [zero] * 8
        in_maps = list(in_maps)
        for it, sl in enumerate(slot_of):
            in_maps[sl] = maps[it]
    else:
        slot_of = list(range(len(plan)))
        in_maps = [maps[k % len(maps)] for k in range(n_cores)]
    b = Builder(c, n_layers=n_layers, debug=debug)
    import os
    if os.environ.get("STOP_AT"):
        b.stop_at = os.environ["STOP_AT"]
    nc = b.build()
    res = run_bass_kernel_spmd(nc, in_maps, core_ids=list(range(n_cores)), trace=trace)
    if debug:
        global DBG_RES
        DBG_RES = res
    if trace:
        print("exec_time_ns", res.exec_time_ns)
    yp = np.zeros((n_prompt, c.SEG, c.D), np.float32)
    ys = np.zeros((n_sample, c.T, c.D), np.float32)
    nst = np.zeros((n_prompt, 1, 2, c.SH, 64, 128), np.float32) if c.NSSD else None
    nk = np.zeros((n_prompt, 1, c.SEG, c.NKV, 128), np.float32) if c.NATT else None
    nv = np.zeros((n_prompt, 1, c.SEG, c.NKV, 128), np.float32) if c.NATT else None
    for k, (kind, idx) in enumerate(plan):
        r = res.results[slot_of[k]]
        y = unfm(np.asarray(r["yout"], np.float32))
        if kind == "s":
            ys[idx] = y
        else:
            yp[idx:idx + c.NSEG] = y.reshape(c.NSEG, c.SEG, c.D)
            if c.NSSD and (n_layers is None or n_layers >= 2):
                ns = np.asarray(r["newst"], np.float32)
                nst[idx:idx + c.NSEG, 0] = ns.transpose(0, 1, 3, 2).reshape(c.NSEG, 2, c.SH, 64, 128)
            if c.NATT and (n_layers is None or n_layers >= 3):
                k_ = np.asarray(r["newk"], np.float32)
                nk[idx:idx + c.NSEG, 0] = k_.transpose(2, 0, 1).reshape(c.NSEG, c.SEG, c.NKV, 128)
                v_ = np.asarray(r["newv"], np.float32)
                nv[idx:idx + c.NSEG, 0] = v_.reshape(c.NSEG, c.SEG, c.NKV, 128)
    return yp, ys, nst, nk, nv


N_CORES = 8


def kernel(**inputs):
    cfg = Cfg()
    inputs = {k: np.asarray(v) for k, v in inputs.items()}
    yp, ys, nst, nk, nv = run(cfg, inputs, n_cores=N_CORES)
    return (yp, ys, nst, nk, nv)
```
